# Optimizing a Trainium2 kernel written in Bass

```python
import math
import jax, jax.numpy as jnp
from jax import lax
import numpy as np

D_MODEL = 1024
BATCH = 8
SEQ = 8192
DEPTH = 1
DEC_BATCH = 128
DEC_SEQ = 4
PAST_LEN = 8192
PAGE_SIZE = 128

F32 = jnp.float32
NSA_HEADS = 8
NSA_KV = 2
NSA_REP = NSA_HEADS // NSA_KV
HEAD_DIM = 64
ROPE_DIM = HEAD_DIM // 4
ROPE_THETA = 500000.0
CMP_BLOCK = 32
CMP_STRIDE = 16
SEL_BLOCK = 64
N_SEL = 16
WINDOW = 512
Q_BLOCK = 128
N_BRANCH = 3
FORCE_BONUS = 1.0e4
NEG = -1.0e30
RET_HEADS = 4
RET_DK = 64
RET_DV = 128
RET_THETA = 10000.0
RET_CHUNK = 128
D_FF = 2816
CONV_W = 3
NORM_EPS = 1e-6
NSA_Q = NSA_HEADS * HEAD_DIM
KV_W = 2 * NSA_KV * HEAD_DIM
RET_QK = RET_HEADS * RET_DK
RET_V = RET_HEADS * RET_DV
GATE_W = NSA_HEADS * N_BRANCH
D_IN = NSA_Q + 3 * KV_W + 2 * RET_QK + 2 * RET_V + GATE_W
MIX_W = NSA_Q + RET_V

kernel_name = "hymba_nsa_retention_convffn_step"


def rms_norm(x, g):
    xf = x.astype(F32)
    y = xf * lax.rsqrt(jnp.mean(xf * xf, axis=-1, keepdims=True) + NORM_EPS)
    return (y * g.astype(F32)).astype(x.dtype)


def rope_rotate(x, pos, rot_dim, theta):
    half = rot_dim // 2
    inv = theta ** (-jnp.arange(half, dtype=F32) / half)
    ang = pos.astype(F32)[:, None] * inv[None, :]
    cos = jnp.cos(ang)[:, None, :]
    sin = jnp.sin(ang)[:, None, :]
    xf = x.astype(F32)
    x1 = xf[..., :half]
    x2 = xf[..., half:rot_dim]
    out = jnp.concatenate([x1 * cos - x2 * sin, x2 * cos + x1 * sin, xf[..., rot_dim:]], axis=-1)
    return out.astype(x.dtype)


def split_in(z):
    sizes = (NSA_Q, KV_W, KV_W, KV_W, RET_QK, RET_QK, RET_V, RET_V, GATE_W)
    idx = np.cumsum(sizes)[:-1].tolist()
    return jnp.split(z, idx, axis=-1)


def mix_project(x, pos, norm_mix, w_in, norm_q, norm_k):
    B, S = x.shape[:2]
    xn = rms_norm(x, norm_mix)
    q, kvc, kvs, kvw, rq, rk, rv, rg, gt = split_in(xn @ w_in)
    q = rope_rotate(rms_norm(q.reshape(B, S, NSA_HEADS, HEAD_DIM), norm_q), pos, ROPE_DIM, ROPE_THETA)

    def kv_rows(z, g_k):
        z = z.reshape(B, S, 2, NSA_KV, HEAD_DIM)
        k = rope_rotate(rms_norm(z[:, :, 0], g_k), pos, ROPE_DIM, ROPE_THETA)
        return jnp.stack([k, z[:, :, 1]], axis=2)

    kvc = kv_rows(kvc, norm_k[0])
    kvs = kv_rows(kvs, norm_k[1])
    kvw = kv_rows(kvw, norm_k[2])
    rq = rope_rotate(rq.reshape(B, S, RET_HEADS, RET_DK), pos, RET_DK, RET_THETA).astype(F32)
    rk = rope_rotate(rk.reshape(B, S, RET_HEADS, RET_DK), pos, RET_DK, RET_THETA).astype(F32) * (RET_DK ** -0.5)
    rv = rv.reshape(B, S, RET_HEADS, RET_DV).astype(F32)
    gates = jax.nn.sigmoid(gt.astype(F32)).reshape(B, S, NSA_HEADS, N_BRANCH)
    return q, kvc, kvs, kvw, rq, rk, rv, rg, gates


def compress(rows, w_cmp1, w_cmp2, pe_cmp):
    B, T = rows.shape[:2]
    ratio = CMP_BLOCK // CMP_STRIDE
    n_chunk = T // CMP_STRIDE
    n_c = n_chunk - ratio + 1
    r = rows[:, :n_chunk * CMP_STRIDE].reshape(B, n_chunk, CMP_STRIDE, 2, NSA_KV, HEAD_DIM).astype(F32)
    w1 = w_cmp1.astype(F32).reshape(2, ratio, CMP_STRIDE, HEAD_DIM, HEAD_DIM)
    pe_term = jnp.einsum('kld,kldo->ko', pe_cmp.astype(F32), w_cmp1.astype(F32).reshape(2, CMP_BLOCK, HEAD_DIM, HEAD_DIM))
    h = pe_term[:, None, :]
    for a in range(ratio):
        part = jnp.einsum('bcskgd,ksdo->bckgo', r, w1[:, a])
        h = h + part[:, a:a + n_c]
    h = jax.nn.gelu(h)
    return jnp.einsum('bckgi,kio->bckgo', h, w_cmp2.astype(F32))


def sel_blocks(rows):
    B, T = rows.shape[:2]
    n_s = -(-T // SEL_BLOCK)
    rows = jnp.pad(rows, ((0, 0), (0, n_s * SEL_BLOCK - T), (0, 0), (0, 0), (0, 0)))
    return rows.reshape(B, n_s, SEL_BLOCK, 2, NSA_KV, HEAD_DIM)


def nsa_memory(rows_c, rows_s, w_cmp1, w_cmp2, pe_cmp):
    cmp = compress(rows_c, w_cmp1, w_cmp2, pe_cmp)
    kc, vc = cmp[:, :, 0], cmp[:, :, 1]
    blk = sel_blocks(rows_s)
    kb = blk[:, :, :, 0].transpose(0, 3, 1, 2, 4)
    vb = blk[:, :, :, 1].transpose(0, 3, 1, 2, 4)
    return kc, vc, kb, vb


def nsa_core(q, qpos, gates, kc, vc, kb, vb, kw, vw, wpos):
    B, Qn = q.shape[:2]
    scale = HEAD_DIM ** -0.5
    qg = q.reshape(B, Qn, NSA_KV, NSA_REP, HEAD_DIM).astype(F32)
    n_c = kc.shape[1]
    s_c = jnp.einsum('bqgrd,bcgd->bgrqc', qg, kc) * scale
    c_start = jnp.arange(n_c) * CMP_STRIDE
    m_c = (c_start + CMP_BLOCK - 1)[None, :] <= qpos[:, None]
    p_c = jnp.where(m_c, jax.nn.softmax(jnp.where(m_c, s_c, NEG), axis=-1), 0.0)
    o_cmp = jnp.einsum('bgrqc,bcgd->bqgrd', p_c, vc)
    n_s = kb.shape[2]
    j_start = jnp.arange(n_s) * SEL_BLOCK
    overlap = jnp.clip(jnp.minimum(c_start[:, None] + CMP_BLOCK, j_start[None, :] + SEL_BLOCK)
                       - jnp.maximum(c_start[:, None], j_start[None, :]), 0, None).astype(F32) / CMP_BLOCK
    imp = jnp.einsum('bgrqc,cj->bgqj', p_c, overlap)
    blk = jnp.arange(n_s)[None, :]
    cur = (qpos // SEL_BLOCK)[:, None]
    forced = (blk == 0) | (blk == cur) | (blk == cur - 1)
    causal_b = j_start[None, :] <= qpos[:, None]
    score = jnp.where(causal_b, imp + jnp.where(forced, FORCE_BONUS, 0.0), NEG)
    k_top = min(N_SEL, n_s)
    _, idx = lax.top_k(score, k_top)
    bi = jnp.arange(B)[:, None, None, None]
    gi = jnp.arange(NSA_KV)[None, :, None, None]
    ks = kb[bi, gi, idx].astype(F32)
    vs = vb[bi, gi, idx].astype(F32)
    s_s = jnp.einsum('bqgrd,bgqkld->bgrqkl', qg, ks) * scale
    kpos = idx[..., None] * SEL_BLOCK + jnp.arange(SEL_BLOCK)
    m_s = (kpos <= qpos[None, None, :, None, None])[:, :, None]
    s_s = jnp.where(m_s, s_s, NEG).reshape(B, NSA_KV, NSA_REP, Qn, k_top * SEL_BLOCK)
    p_s = jax.nn.softmax(s_s, axis=-1).reshape(B, NSA_KV, NSA_REP, Qn, k_top, SEL_BLOCK)
    o_sel = jnp.einsum('bgrqkl,bgqkld->bqgrd', p_s, vs)
    s_w = jnp.einsum('bqgrd,bwgd->bgrqw', qg, kw.astype(F32)) * scale
    dpos = qpos[:, None] - wpos[None, :]
    m_w = (dpos >= 0) & (dpos < WINDOW) & (wpos[None, :] >= 0)
    p_w = jax.nn.softmax(jnp.where(m_w, s_w, NEG), axis=-1)
    o_win = jnp.einsum('bgrqw,bwgd->bqgrd', p_w, vw.astype(F32))
    g = gates.reshape(B, Qn, NSA_KV, NSA_REP, N_BRANCH)
    out = g[..., 0:1] * o_cmp + g[..., 1:2] * o_sel + g[..., 2:3] * o_win
    return out.reshape(B, Qn, NSA_Q).astype(q.dtype)


def nsa_prompt(q, gates, kc, vc, kb, vb, kvw):
    B, S = q.shape[:2]
    win_pad = jnp.pad(kvw, ((0, 0), (WINDOW, 0), (0, 0), (0, 0), (0, 0)))

    def block(i):
        q0 = i * Q_BLOCK
        qb = lax.dynamic_slice_in_dim(q, q0, Q_BLOCK, axis=1)
        gb = lax.dynamic_slice_in_dim(gates, q0, Q_BLOCK, axis=1)
        wb = lax.dynamic_slice_in_dim(win_pad, q0, WINDOW + Q_BLOCK, axis=1)
        qpos = q0 + jnp.arange(Q_BLOCK)
        wpos = q0 - WINDOW + jnp.arange(WINDOW + Q_BLOCK)
        return nsa_core(qb, qpos, gb, kc, vc, kb, vb, wb[:, :, 0], wb[:, :, 1], wpos)

    out = lax.map(block, jnp.arange(S // Q_BLOCK))
    return out.transpose(1, 0, 2, 3).reshape(B, S, NSA_Q)


def gather_pages(cache, page_table):
    g = cache[page_table]
    db, n_pages = page_table.shape
    return g.reshape(db, n_pages * PAGE_SIZE, 2, NSA_KV, HEAD_DIM)


def retention_log_decay():
    return jnp.log1p(-(2.0 ** (-5.0 - jnp.arange(RET_HEADS, dtype=F32))))


def retention_chunk(s_prev, q, k, v, log_g):
    C = q.shape[1]
    i = jnp.arange(C, dtype=F32)
    diff = i[:, None] - i[None, :]
    dec = jnp.where(diff[None] >= 0, jnp.exp(jnp.maximum(diff, 0.0)[None] * log_g[:, None, None]), 0.0)
    s = jnp.einsum('bihd,bjhd->bhij', q, k) * dec
    o = jnp.einsum('bhij,bjhe->bihe', s, v)
    cross = jnp.exp((i[:, None] + 1.0) * log_g[None, :])
    o = o + jnp.einsum('bihd,bhde->bihe', q, s_prev) * cross[None, :, :, None]
    w = jnp.exp((C - 1.0 - i)[:, None] * log_g[None, :])
    s_new = jnp.exp(C * log_g)[None, :, None, None] * s_prev + jnp.einsum('bjhd,bjhe,jh->bhde', k, v, w)
    return s_new, o


def retention_prompt(q, k, v, log_g):
    B, S = q.shape[:2]
    n = S // RET_CHUNK

    def chunks(a):
        return a.reshape(B, n, RET_CHUNK, *a.shape[2:]).swapaxes(0, 1)

    s0 = jnp.zeros((B, RET_HEADS, RET_DK, RET_DV), F32)
    s_fin, o = lax.scan(lambda s, xs: retention_chunk(s, xs[0], xs[1], xs[2], log_g), s0,
                        (chunks(q), chunks(k), chunks(v)))
    return s_fin, o.swapaxes(0, 1).reshape(B, S, RET_HEADS, RET_DV)


def mix_output(o_nsa, o_ret, g_ret, w_out):
    B, S = o_ret.shape[:2]
    r = o_ret * lax.rsqrt(jnp.mean(o_ret * o_ret, axis=-1, keepdims=True) + NORM_EPS)
    r = jax.nn.silu(g_ret.astype(F32)) * r.reshape(B, S, RET_V)
    mixed = jnp.concatenate([o_nsa.astype(F32), r], axis=-1).astype(w_out.dtype)
    return mixed @ w_out


def conv_ffn(xn, conv_prev, w_ffn_in, w_conv, b_conv, w_ffn_out):
    S = xn.shape[1]
    u, v = jnp.split(xn @ w_ffn_in, 2, axis=-1)
    up = jnp.concatenate([conv_prev.astype(u.dtype), u], axis=1)
    uc = b_conv
    for j in range(CONV_W):
        uc = uc + w_conv[j] * up[:, j:j + S]
    y = (jax.nn.silu(uc) * v) @ w_ffn_out
    return y, up[:, up.shape[1] - (CONV_W - 1):]


def setup_inputs(seed: int = 0) -> dict:
    key = jax.random.key(seed)
    ks = jax.random.split(key, 24)
    n_pages = PAST_LEN // PAGE_SIZE
    n_pool = (DEC_BATCH * n_pages * 5) // 4
    w_buf = min(WINDOW, PAST_LEN)

    def nrm(k, shape, scale=1.0):
        return jax.random.normal(k, shape, F32) * scale

    return {
        'x_prompt': nrm(ks[0], (BATCH, SEQ, D_MODEL)),
        'x_sample': nrm(ks[1], (DEC_BATCH, DEC_SEQ, D_MODEL)),
        'cache_kv_cmp': nrm(ks[2], (DEPTH, n_pool, PAGE_SIZE, 2, NSA_KV, HEAD_DIM)),
        'cache_kv_sel': nrm(ks[3], (DEPTH, n_pool, PAGE_SIZE, 2, NSA_KV, HEAD_DIM)),
        'state_kv_win': nrm(ks[4], (DEPTH, DEC_BATCH, w_buf, 2, NSA_KV, HEAD_DIM)),
        'state_ret': nrm(ks[5], (DEPTH, DEC_BATCH, RET_HEADS, RET_DK, RET_DV)),
        'state_conv': nrm(ks[6], (DEPTH, DEC_BATCH, CONV_W - 1, D_FF)),
        'page_table': jax.random.permutation(ks[7], n_pool)[:DEC_BATCH * n_pages].reshape(DEC_BATCH, n_pages).astype(jnp.int32),
        'norm_mix': 1.0 + nrm(ks[8], (DEPTH, D_MODEL), 0.1),
        'w_in': nrm(ks[9], (DEPTH, D_MODEL, D_IN), D_MODEL ** -0.5),
        'norm_q': 1.0 + nrm(ks[10], (DEPTH, HEAD_DIM), 0.1),
        'norm_k': 1.0 + nrm(ks[11], (DEPTH, N_BRANCH, HEAD_DIM), 0.1),
        'w_cmp1': nrm(ks[12], (DEPTH, 2, CMP_BLOCK * HEAD_DIM, HEAD_DIM), (CMP_BLOCK * HEAD_DIM) ** -0.5),
        'w_cmp2': nrm(ks[13], (DEPTH, 2, HEAD_DIM, HEAD_DIM), HEAD_DIM ** -0.5),
        'pe_cmp': nrm(ks[14], (DEPTH, 2, CMP_BLOCK, HEAD_DIM), 0.5),
        'w_out': nrm(ks[15], (DEPTH, MIX_W, D_MODEL), MIX_W ** -0.5),
        'norm_ffn': 1.0 + nrm(ks[16], (DEPTH, D_MODEL), 0.1),
        'w_ffn_in': nrm(ks[17], (DEPTH, D_MODEL, 2 * D_FF), D_MODEL ** -0.5),
        'w_conv': nrm(ks[18], (DEPTH, CONV_W, D_FF), CONV_W ** -0.5),
        'b_conv': nrm(ks[19], (DEPTH, D_FF), 0.02),
        'w_ffn_out': nrm(ks[20], (DEPTH, D_FF, D_MODEL), D_FF ** -0.5),
    }


def reference(x_prompt, x_sample, cache_kv_cmp, cache_kv_sel, state_kv_win, state_ret, state_conv, page_table,
              norm_mix, w_in, norm_q, norm_k, w_cmp1, w_cmp2, pe_cmp, w_out, norm_ffn, w_ffn_in, w_conv, b_conv,
              w_ffn_out):
    B, S = x_prompt.shape[:2]
    DB, SN = x_sample.shape[:2]
    P = page_table.shape[1] * PAGE_SIZE
    w_buf = state_kv_win.shape[2]
    pos_p = jnp.arange(S, dtype=jnp.int32)
    pos_s = P + jnp.arange(SN, dtype=jnp.int32)
    wpos_s = P - w_buf + jnp.arange(w_buf + SN, dtype=jnp.int32)
    log_g = retention_log_decay()
    h_p, h_s = x_prompt, x_sample
    kvc_p, kvc_s, kvs_p, kvs_s, kvw_p, kvw_s = [], [], [], [], [], []
    ret_p, ret_s, conv_p, conv_s = [], [], [], []
    for l in range(DEPTH):
        q, kvc, kvs, kvw, rq, rk, rv, rg, gates = mix_project(h_p, pos_p, norm_mix[l], w_in[l], norm_q[l], norm_k[l])
        kc, vc, kb, vb = nsa_memory(kvc, kvs, w_cmp1[l], w_cmp2[l], pe_cmp[l])
        o_nsa = nsa_prompt(q, gates, kc, vc, kb, vb, kvw)
        s_fin, o_ret = retention_prompt(rq, rk, rv, log_g)
        h_p = h_p + mix_output(o_nsa, o_ret, rg, w_out[l]).astype(h_p.dtype)
        f, conv_new = conv_ffn(rms_norm(h_p, norm_ffn[l]), jnp.zeros((B, CONV_W - 1, D_FF), h_p.dtype),
                               w_ffn_in[l], w_conv[l], b_conv[l], w_ffn_out[l])
        h_p = h_p + f.astype(h_p.dtype)
        kvc_p.append(kvc)
        kvs_p.append(kvs)
        kvw_p.append(kvw[:, S - min(WINDOW, S):])
        ret_p.append(s_fin)
        conv_p.append(conv_new)
        q, kvc, kvs, kvw, rq, rk, rv, rg, gates = mix_project(h_s, pos_s, norm_mix[l], w_in[l], norm_q[l], norm_k[l])
        rows_c = jnp.concatenate([gather_pages(cache_kv_cmp[l], page_table), kvc], axis=1)
        rows_s = jnp.concatenate([gather_pages(cache_kv_sel[l], page_table), kvs], axis=1)
        kc, vc, kb, vb = nsa_memory(rows_c, rows_s, w_cmp1[l], w_cmp2[l], pe_cmp[l])
        win = jnp.concatenate([state_kv_win[l], kvw], axis=1)
        o_nsa = nsa_core(q, pos_s, gates, kc, vc, kb, vb, win[:, :, 0], win[:, :, 1], wpos_s)
        s_new, o_ret = retention_chunk(state_ret[l].astype(F32), rq, rk, rv, log_g)
        h_s = h_s + mix_output(o_nsa, o_ret, rg, w_out[l]).astype(h_s.dtype)
        f, conv_new = conv_ffn(rms_norm(h_s, norm_ffn[l]), state_conv[l], w_ffn_in[l], w_conv[l], b_conv[l],
                               w_ffn_out[l])
        h_s = h_s + f.astype(h_s.dtype)
        kvc_s.append(kvc)
        kvs_s.append(kvs)
        kvw_s.append(win[:, win.shape[1] - w_buf:])
        ret_s.append(s_new)
        conv_s.append(conv_new)
    kv_cmp_prompt = jnp.stack(kvc_p)
    kv_cmp_sample = jnp.stack(kvc_s)
    kv_sel_prompt = jnp.stack(kvs_p)
    kv_sel_sample = jnp.stack(kvs_s)
    kv_win_prompt = jnp.stack(kvw_p)
    kv_win_sample = jnp.stack(kvw_s)
    ret_prompt = jnp.stack(ret_p)
    ret_sample = jnp.stack(ret_s)
    conv_prompt = jnp.stack(conv_p)
    conv_sample = jnp.stack(conv_s)
    return (h_p, h_s, kv_cmp_prompt, kv_cmp_sample, kv_sel_prompt, kv_sel_sample, kv_win_prompt, kv_win_sample,
            ret_prompt, ret_sample, conv_prompt, conv_sample)
```

```python
import math
from contextlib import ExitStack
import numpy as np
import ml_dtypes
import concourse.bass as bass
import concourse.mybir as mybir
from concourse.bass_utils import run_bass_kernel_spmd

F32 = mybir.dt.float32
BF16 = mybir.dt.bfloat16
I32 = mybir.dt.int32
AF = mybir.ActivationFunctionType
ALU = mybir.AluOpType
AX = mybir.AxisListType
NPBF = ml_dtypes.bfloat16

import os
POOL_ENG = os.environ.get('K_POOL', 'gpsimd')
SKIP = os.environ.get('K_SKIP', '')
PSTOP = int(os.environ.get('K_PSTOP', '99'))
PSTOP2 = int(os.environ.get('K_PSTOP2', '99'))
ENGS = ("tensor", "vector", "scalar", "gpsimd", "sync")
NEGM = -32768.0
EPS = 1e-6
D_IN = 2840
D_FF = 2816
NFC = 22


class Op:
    __slots__ = ("eng", "fn", "deps", "signal", "semkey", "idx", "count", "isdma")

    def __init__(self, eng, fn, isdma, semkey):
        self.eng = eng
        self.fn = fn
        self.deps = []
        self.signal = False
        self.semkey = semkey
        self.isdma = isdma
        self.count = None


class Prog:
    def __init__(self):
        self.ops = []
        self.last_w = {}
        self.readers = {}

    EXCL = ("S0", "S1", "A0", "A1", "A2", "T0", "M0", "M1", "U0", "U1", "V0", "V1", "T0b", "Y0", "Y1")

    def add(self, eng, fn, reads=(), writes=(), dma=None):
        xr = [r for r in reads if r in self.EXCL]
        if xr:
            writes = list(writes) + [r for r in xr if r not in writes]
        op = Op(eng, fn, dma is not None, ("dma", dma) if dma is not None else ("eng", eng))
        deps = set()
        for r in reads:
            w = self.last_w.get(r)
            if w is not None:
                deps.add(w)
        for r in writes:
            w = self.last_w.get(r)
            if w is not None:
                deps.add(w)
            for rd in self.readers.get(r, ()):
                deps.add(rd)
        if eng == "tensor":
            deps = {d for d in deps if not (d.eng == "tensor" and not d.isdma)}
        op.deps = list(deps)
        for d in op.deps:
            d.signal = True
        for r in reads:
            self.readers.setdefault(r, []).append(op)
        for r in writes:
            self.last_w[r] = op
            self.readers[r] = []
        op.idx = len(self.ops)
        self.ops.append(op)
        return op

    def emit(self, nc, tag):
        for op in self.ops:
            if op.isdma:
                op.signal = True
        lastop = {}
        for op in self.ops:
            if not op.isdma:
                lastop[op.eng] = op
        for op in lastop.values():
            op.signal = True
        counts = {}
        for op in self.ops:
            if op.signal:
                inc = 16 if op.isdma else 1
                counts[op.semkey] = counts.get(op.semkey, 0) + inc
                op.count = counts[op.semkey]
        keys = list(counts.keys())
        with ExitStack() as es:
            sems = {}
            for i, k in enumerate(keys):
                sems[k] = es.enter_context(nc.semaphore("%s%d" % (tag, i)))
            block = es.enter_context(nc.Block())
            per_eng = {e: [op for op in self.ops if op.eng == e] for e in ENGS}

            def make(ename):
                ops = per_eng[ename]

                def body(eng):
                    waited = {}
                    for op in ops:
                        need = {}
                        for d in op.deps:
                            if need.get(d.semkey, 0) < d.count:
                                need[d.semkey] = d.count
                        for k, v in need.items():
                            if waited.get(k, 0) < v:
                                eng.wait_ge(sems[k], v)
                                waited[k] = v
                        ins = op.fn(eng)
                        if op.signal:
                            ins.then_inc(sems[op.semkey], 16 if op.isdma else 1)
                    for k, v in counts.items():
                        if waited.get(k, 0) < v:
                            eng.wait_ge(sems[k], v)
                return body

            for ename in ENGS:
                getattr(block, ename)(make(ename))
        return len(self.ops), len(keys)


class Bld:
    def __init__(self, nc, P, es):
        self.nc, self.P, self.es = nc, P, es
        self.rr = 0

    def sb(self, name, shape, dt):
        return self.es.enter_context(self.nc.sbuf_tensor(name, shape, dt))

    def ps(self, name, shape, dt):
        return self.es.enter_context(self.nc.psum_tensor(name, shape, dt))

    def dma(self, out, in_, r, w, key, eng="sync"):
        self.P.add(eng, lambda e: e.dma_start(out=out, in_=in_), r, w, dma=key)

    def mm(self, out, lhsT, rhs, start, stop, r, w, nocheck=False):
        if nocheck:
            self.P.add("tensor", lambda e: e.matmul(out, lhsT=lhsT, rhs=rhs, start=start, stop=stop, skip_group_check=True), r, w)
        else:
            self.P.add("tensor", lambda e: e.matmul(out, lhsT=lhsT, rhs=rhs, start=start, stop=stop), r, w)

    def tr(self, out, in_, ident, r, w):
        self.P.add("tensor", lambda e: e.transpose(out=out, in_=in_, identity=ident), r, w)

    def act(self, out, in_, func, r, w, bias=None, scale=None, accum=None):
        kw = {}
        if bias is not None:
            kw["bias"] = bias
        if scale is not None:
            kw["scale"] = scale
        if accum is not None:
            kw["accum_out"] = accum
        self.P.add("scalar", lambda e: e.activation(out=out, in_=in_, func=func, **kw), r, w)

    def _e(self, eng):
        return POOL_ENG if eng == "gpsimd" else eng

    def tt(self, eng, out, in0, in1, op, r, w):
        eng = self._e(eng)
        self.P.add(eng, lambda e: e.tensor_tensor(out=out, in0=in0, in1=in1, op=op), r, w)

    def ts(self, eng, out, in0, s1, s2, op0, op1, r, w):
        eng = self._e(eng)
        self.P.add(eng, lambda e: e.tensor_scalar(out=out, in0=in0, scalar1=s1, scalar2=s2, op0=op0, op1=op1), r, w)

    def ts1(self, eng, out, in_, s, op, r, w):
        eng = self._e(eng)
        self.P.add(eng, lambda e: e.tensor_single_scalar(out=out, in_=in_, scalar=s, op=op), r, w)

    def stt(self, eng, out, in0, scalar, in1, op0, op1, r, w):
        eng = self._e(eng)
        self.P.add(eng, lambda e: e.scalar_tensor_tensor(out=out, in0=in0, scalar=scalar, in1=in1, op0=op0, op1=op1), r, w)

    def cp(self, eng, out, in_, r, w):
        eng = self._e(eng)
        if eng == "scalar":
            self.P.add(eng, lambda e: e.copy(out=out, in_=in_), r, w)
        else:
            self.P.add(eng, lambda e: e.tensor_copy(out=out, in_=in_), r, w)

    def red(self, out, in_, r, w):
        self.P.add("vector", lambda e: e.reduce_sum(out=out, in_=in_, axis=AX.X), r, w)

    def recip(self, out, in_, r, w):
        self.P.add("vector", lambda e: e.reciprocal(out=out, in_=in_), r, w)

    def memset(self, eng, ap, val, w):
        eng = self._e(eng)
        self.P.add(eng, lambda e: e.memset(ap, val), (), w)

    def rsqrt(self, out, in_, scale, r, w):
        self.act(out, in_, AF.Ln, r, w, bias=self.eps_ap[0:out.shape[0], :], scale=scale)
        self.act(out, out, AF.Exp, w, w, scale=-0.5)


def _gammas():
    return 1.0 - 2.0 ** (-5.0 - np.arange(4, dtype=np.float64))


def host_consts(S, NB, PAST):
    RS = NB * 4
    c = {}
    c["ident"] = np.eye(128, dtype=np.float32).astype(NPBF)
    c["identf"] = np.eye(128, dtype=np.float32)

    def ropeN(pos):
        inv = 500000.0 ** (-np.arange(8, dtype=np.float32) / 8)
        ang = pos.astype(np.float32)[:, None] * inv[None, :]
        cs, sn = np.cos(ang), np.sin(ang)
        return np.concatenate([cs, cs, -sn, sn], axis=1).astype(np.float32)

    def ropeR(pos):
        inv = 10000.0 ** (-np.arange(32, dtype=np.float32) / 32)
        ang = pos.astype(np.float32)[:, None] * inv[None, :]
        cs, sn = np.cos(ang), np.sin(ang)
        return np.concatenate([cs, cs, -sn, sn], axis=1).astype(np.float32)

    pos_p = np.arange(S)
    pos_s = PAST + (np.arange(RS) % 4)
    c["ropeN_p"], c["ropeR_p"] = ropeN(pos_p), ropeR(pos_p)
    c["ropeN_s"], c["ropeR_s"] = ropeN(pos_s), ropeR(pos_s)
    k = np.arange(128)[:, None]
    q = np.arange(128)[None, :]
    causal = np.where(k <= q, 0.0, NEGM).astype(np.float32)
    winfirst = np.where(k > q, 0.0, NEGM).astype(np.float32)
    c["causal4"] = np.tile(causal, (1, 4)).astype(NPBF)
    c["winfirst4"] = np.tile(winfirst, (1, 4)).astype(NPBF)
    kk = np.arange(8)[:, None]
    base = np.where(q >= 16 * kk + 15, 0.0, NEGM).astype(np.float32)
    c["cbase4"] = np.tile(base, (1, 4)).astype(NPBF)
    sh = np.zeros((8, 272), np.float32)
    for i in range(8):
        sh[i, i + 128] = 1.0
    c["shid"] = sh.astype(NPBF)
    E = np.zeros((64, 4096), np.float32)
    for j in range(64):
        E[j, 64 * j:64 * j + 64] = 1.0
    c["E"] = E.astype(NPBF)
    ov = np.zeros((128, 4, 128), np.float32)
    for cc in range(512):
        for j in range(128):
            lo, hi = max(16 * cc, 64 * j), min(16 * cc + 32, 64 * j + 64)
            if hi > lo:
                ov[cc % 128, cc // 128, j] = (hi - lo) / 32.0
    c["ov"] = ov.reshape(128, 512).astype(NPBF)
    g = _gammas()
    i_ = np.arange(128)
    dec = np.zeros((128, 4, 128), np.float64)
    for h in range(4):
        d = i_[None, :] - i_[:, None]
        dec[:, h, :] = np.where(d >= 0, g[h] ** np.maximum(d, 0), 0.0) / 8.0
    c["decT"] = dec.reshape(128, 512).astype(np.float32)
    cross = np.zeros((64, 4, 128), np.float64)
    for h in range(4):
        cross[:, h, :] = (g[h] ** (i_ + 1.0))[None, :]
    c["crossT"] = cross.reshape(64, 512).astype(np.float32)
    c["wtab"] = np.stack([g[h] ** (127.0 - i_) / 8.0 for h in range(4)], axis=1).astype(np.float32)
    pat = np.zeros((128, 3), np.float32)
    pat[:64, 0] = 1e4
    pat[:, 1] = 1e4
    pat[64:, 2] = 1e4
    c["pat"] = pat
    c["iota"] = np.arange(128, dtype=np.float32).reshape(128, 1)
    s4 = np.arange(4)
    scaus = np.where(s4[:, None] <= s4[None, :], 0.0, NEGM).astype(np.float32)
    c["scausal4"] = np.tile(scaus, (1, 4)).astype(NPBF)
    swf = np.zeros((128, 4), np.float32)
    for w in range(4):
        for s in range(4):
            swf[w, s] = 0.0 if w > s else NEGM
    c["swinfirst4"] = np.tile(swf, (1, 4)).astype(NPBF)
    dec4 = np.zeros((4, 4, 4), np.float64)
    for h in range(4):
        d = s4[None, :] - s4[:, None]
        dec4[:, h, :] = np.where(d >= 0, g[h] ** np.maximum(d, 0), 0.0) / 8.0
    c["dec4T"] = dec4.reshape(4, 16).astype(np.float32)
    cross4 = np.zeros((64, 4, 4), np.float64)
    for h in range(4):
        cross4[:, h, :] = (g[h] ** (s4 + 1.0))[None, :]
    c["cross4T"] = cross4.reshape(64, 16).astype(np.float32)
    c["w4tab"] = np.stack([g[h] ** (3.0 - s4) / 8.0 for h in range(4)], axis=1).astype(np.float32)
    spat = np.zeros((4, 2), np.float32)
    spat[:, 0] = 1e4
    spat[:, 1] = 1e4
    c["spat"] = spat
    return c


CONST_SHAPES = None


import os
DBG = int(os.environ.get('K_DBG', '9'))


def build(S, NB, PAST, NPOOL, do_sample=True):
    NT = S // 128
    NPG = PAST // 128
    RS = NB * 4
    NCS = PAST // 16 - 1
    nc = bass.Bass("TRN2", target_bir_lowering=False)
    consts = host_consts(S, NB, PAST)

    def din(name, shape, dt=F32):
        return nc.dram_tensor(name, list(shape), dt, kind="ExternalInput").ap()

    def dout(name, shape):
        return nc.dram_tensor(name, list(shape), F32, kind="ExternalOutput").ap()

    cd = {}
    for k, v in consts.items():
        cd[k] = din("c_" + k, v.shape, BF16 if v.dtype == NPBF else F32)
    x_p = din("x_p", [S, 1024])
    x_s = din("x_s", [RS, 1024])
    c_pg = din("c_pg", [NPOOL * 128, 512])
    st_winK = din("st_winK", [NB, 128, 512])
    st_winV = din("st_winV", [NB, 512, 128])
    st_win = din("st_win", [NB, 512, 256])
    st_ret = din("st_ret", [64, NB * 512])
    st_conv = din("st_conv", [128, NFC * NB * 2])
    ptab = din("ptab", [1, NB * NPG], I32)
    nm_mix = din("nm_mix", [128, 8])
    w_in = din("w_in", [1024, D_IN])
    gq = din("gq", [128, 64])
    gk = din("gk", [128, 192])
    w1h = din("w1h", [128, 4096])
    w2h = din("w2h", [128, 128 + 128])
    peT = din("peT", [128, 64])
    w_out = din("w_out", [1024, 1024])
    nm_ffn = din("nm_ffn", [128, 8])
    w_fi = din("w_fi", [1024, 2 * D_FF])
    wconv = din("wconv", [128, NFC * 4])
    w_fo = din("w_fo", [D_FF, 1024])

    y_p = dout("y_p", [S, 1024])
    y_s = dout("y_s", [RS, 1024])
    o_kvc_p = dout("o_kvc_p", [S, 256])
    o_kvs_p = dout("o_kvs_p", [S, 256])
    o_kvw_p = dout("o_kvw_p", [512, 256])
    o_kvc_s = dout("o_kvc_s", [RS, 256])
    o_kvs_s = dout("o_kvs_s", [RS, 256])
    o_kvw_s = dout("o_kvw_s", [NB, 512, 256])
    o_ret_p = dout("o_ret_p", [64, 512])
    o_ret_s = dout("o_ret_s", [64, NB * 512])
    o_conv_p = dout("o_conv_p", [128, NFC * 2])
    o_conv_s = dout("o_conv_s", [128, NFC * NB * 2])

    P = Prog()
    with ExitStack() as es:
        B = Bld(nc, P, es)
        sb, ps = B.sb, B.ps
        Sb = [ps("S0", [128, 512], F32), ps("S1", [128, 512], F32)]
        Ab = [ps("A0", [128, 512], F32), ps("A1", [128, 512], F32), ps("A2", [128, 512], F32)]
        T0 = ps("T0", [128, 1024], BF16)
        Mb = [ps("M0", [128, 512], F32), ps("M1", [128, 512], F32)]
        ct = {}
        for k, v in consts.items():
            if k in ("ropeN_p", "ropeR_p", "ropeN_s", "ropeR_s", "identf"):
                continue
            ct[k] = sb("k_" + k, list(v.shape), BF16 if v.dtype == NPBF else F32)
            B.dma(ct[k][:], cd[k], [], ["k_" + k], "k_" + k)
        ident = ct["ident"]
        epsc = sb("epsc", [128, 1], F32)
        B.memset("vector", epsc[:], EPS, ["epsc"])
        B.eps_ap = epsc
        w_in_bf = sb("w_in_bf", [128, 8, D_IN], BF16)
        w_out_bf = sb("w_out_bf", [128, 8, 1024], BF16)
        W1 = sb("W1", [128, 2, 32, 64], BF16)
        w2 = sb("w2", [128, 256], BF16)
        peT_bf = sb("peT_bf", [128, 64], BF16)
        peterm = sb("peterm", [128, 2], F32)
        nm_sb = sb("nm_sb", [128, 8], F32)
        gq_sb = sb("gq_sb", [128, 64], F32)
        gk_sb = sb("gk_sb", [128, 192], F32)
        stage = [sb("stage0", [128, 1024], F32), sb("stage1", [128, 1024], F32)]
        B.dma(nm_sb[:], nm_mix, [], ["nm_sb"], "nm_sb")
        B.dma(gq_sb[:], gq, [], ["gq_sb"], "gq_sb")
        B.dma(gk_sb[:], gk, [], ["gk_sb"], "gk_sb")
        B.ts1("vector", gq_sb[:], gq_sb[:], 0.125, ALU.mult, ["gq_sb"], ["gq_sb"])
        si = 0
        for k in range(8):
            for c0 in range(0, D_IN, 1024):
                s_ = si % 2
                c1 = min(D_IN, c0 + 1024)
                B.dma(stage[s_][:, 0:c1 - c0], w_in[k * 128:(k + 1) * 128, c0:c1], [], ["stage%d" % s_], "stage%d" % s_)
                eng = "vector" if si % 2 == 0 else "gpsimd"
                B.ts1(eng, w_in_bf[:, k, c0:c1], stage[s_][:, 0:c1 - c0], nm_sb[:, k:k + 1], ALU.mult,
                      ["stage%d" % s_, "nm_sb"], ["w_in_bf"])
                si += 1
        for k in range(8):
            s_ = si % 2
            B.dma(stage[s_][:, 0:1024], w_out[k * 128:(k + 1) * 128, :], [], ["stage%d" % s_], "stage%d" % s_)
            B.cp("vector" if k % 2 == 0 else "gpsimd", w_out_bf[:, k, :], stage[s_][:, 0:1024], ["stage%d" % s_], ["w_out_bf"])
            si += 1
        for k in range(4):
            s_ = si % 2
            B.dma(stage[s_][:, 0:1024], w1h[:, k * 1024:(k + 1) * 1024], [], ["stage%d" % s_], "stage%d" % s_)
            B.cp("vector", W1[:, k // 2, (k % 2) * 16:(k % 2) * 16 + 16, :].rearrange("p l o -> p (l o)"), stage[s_][:, 0:1024],
                 ["stage%d" % s_], ["W1"])
            si += 1
        s_ = si % 2
        B.dma(stage[s_][:, 0:256], w2h, [], ["stage%d" % s_], "stage%d" % s_)
        B.cp("vector", w2[:, :], stage[s_][:, 0:256], ["stage%d" % s_], ["w2"])
        si += 1
        s_ = si % 2
        B.dma(stage[s_][:, 0:64], peT, [], ["stage%d" % s_], "stage%d" % s_)
        B.cp("vector", peT_bf[:], stage[s_][:, 0:64], ["stage%d" % s_], ["peT_bf"])
        si += 1
        for hp in range(0 if 'pe' in SKIP else 2):
            for k in range(2):
                for l in range(32):
                    B.mm(Mb[0][64 * hp:64 * hp + 64, k:k + 1], W1[64 * hp:64 * hp + 64, k, l, :],
                         peT_bf[64 * hp:64 * hp + 64, k * 32 + l:k * 32 + l + 1], l == 0, l == 31,
                         ["W1", "peT_bf"], ["M0"])
        if "pe" not in SKIP:
            B.cp("vector", peterm[:], Mb[0][:, 0:2], ["M0"], ["peterm"])

        LK = max(S, PAST)
        ksT = sb("ksT", [128, LK + 4], BF16)
        vs = sb("vs", [128, LK // 128 + 1, 2, 65], BF16)
        kwT = sb("kwT", [128, 8, 128], BF16)
        vw = sb("vw", [128, 8, 2, 65], BF16)
        kcT = sb("kcT", [128, 512], BF16)
        hgT = sb("hgT", [128, 2, 512], BF16)
        vc = sb("vc", [128, 4, 2, 65], BF16)
        cmpT = sb("cmpT", [128, 2, 16 + 1024], BF16)
        if "ms" not in SKIP:
            B.memset("vector", vs[:, :, :, 64:65], 1.0, ["vs"])
            B.memset("vector", vw[:, :, :, 64:65], 1.0, ["vw"])
            B.memset("vector", vc[:, :, :, 64:65], 1.0, ["vc"])
        Sf = sb("Sf", [64, 4, 128], F32)
        Sbf = sb("Sbf", [64, 4, 128], BF16)
        xs_ = [sb("xs0", [128, 1024], F32)]
        ropeN = sb("ropeN", [128, 32], F32)
        ropeR = sb("ropeR", [128, 128], F32)
        st1 = sb("st1", [128, 16], F32)
        junk = sb("junk", [128, 1024], BF16)
        xn = sb("xn", [128, 1024], BF16)
        xnT = sb("xnT", [128, 8, 128], BF16)
        qn = sb("qn", [128, 512], F32)
        q_bf = sb("q_bf", [128, 8, 128], BF16)
        B.memset("vector", q_bf[:], 0.0, ["q_bf"])
        qT = sb("qT", [128, 8, 128], BF16)
        kvo = sb("kvo", [128, 768], F32)
        k_bf = sb("k_bf", [128, 3, 128], BF16)
        cv_bf = sb("cv_bf", [128, 128], BF16)
        rtmp = sb("rtmp", [128, 256], F32)
        rtmp2 = sb("rtmp2", [128, 256], F32)
        rqk_bf = sb("rqk_bf", [128, 512], BF16)
        rqT = sb("rqT", [64, 4, 128], BF16)
        rkT = sb("rkT", [64, 4, 128], BF16)
        rv_bf = sb("rv_bf", [128, 512], BF16)
        rg_f = sb("rg_f", [128, 512], F32)
        oret = sb("oret", [128, 512], F32)
        rg_e = oret

        gates = sb("gates", [128, 24], F32)
        ebuf = [sb("e0", [128, 512], BF16), sb("e1", [128, 512], BF16)]
        hx = sb("hx", [128, 32], F32)
        hg_bf = sb("hg_bf", [128, 2, 128], BF16)
        hwork = sb("hwork", [128, 3, 256], F32)
        ocmp = sb("ocmp", [128, 4, 64], F32)
        osel = sb("osel", [128, 4, 64], F32)
        owin = sb("owin", [128, 4, 64], F32)
        rs_c = sb("rs_c", [128, 12], F32)
        imp = sb("imp", [128, 128], F32)
        imp2 = sb("imp2", [128, 128], F32)
        mx = sb("mx", [128, 16], F32)
        negq = sb("negq", [128, 128], BF16)
        negT4 = sb("negT4", [64, 2, 512], BF16)
        mixed = sb("mixed", [128, 1024], BF16)
        mixT = xnT
        sTm = sb("sTm", [128, 512], BF16)
        qcr = sb("qcr", [64, 4, 128], BF16)
        kwk = sb("kwk", [128, 256], BF16)


        B.memset("vector", Sf[:], 0.0, ["Sf"])
        B.memset("vector", Sbf[:], 0.0, ["Sbf"])
        B.memset("vector", cmpT[:, :, 0:16], 0.0, ["cmpT"])

        ecnt = [0]
        scnt = [0]

        def project(R, x_dram, xslot, ropeN_d, ropeR_d, kv_douts, tg):
            xr = "xs%d" % xslot
            x_sb = xs_[xslot]
            B.dma(x_sb[0:R, :], x_dram, [], [xr], xr)
            B.dma(ropeN[0:R, :], ropeN_d, [], ["ropeN"], "ropeN")
            B.dma(ropeR[0:R, :], ropeR_d, [], ["ropeR"], "ropeR")
            B.act(junk[0:R, :], x_sb[0:R, :], AF.Square, [xr], ["junk", "st1"], accum=st1[0:R, 0:1])
            B.rsqrt(st1[0:R, 1:2], st1[0:R, 0:1], 1.0 / 1024, ["st1"], ["st1"])
            if PSTOP <= 0:
                return
            B.act(xn[0:R, :], x_sb[0:R, :], AF.Identity, [xr, "st1"], ["xn"], scale=st1[0:R, 1:2])
            if PSTOP <= 1:
                return
            for k in range(8):
                B.tr(T0[:, k * 128:k * 128 + R], xn[0:R, k * 128:(k + 1) * 128], ident[0:R, 0:R], ["xn", "k_ident"], ["T0"])
            B.cp("vector", xnT[:, :, 0:R], T0[:, :].rearrange("p (k t) -> p k t", k=8)[:, :, 0:R], ["T0"], ["xnT"])
            if PSTOP <= 2:
                return
            def zchunk(n, bank):
                c0 = n * 512
                w = min(512, D_IN - c0)
                for k in range(8):
                    B.mm(Mb[bank][0:R, 0:w], xnT[:, k, 0:R], w_in_bf[:, k, c0:c0 + w], k == 0, k == 7,
                         ["xnT", "w_in_bf"], ["M%d" % bank])
                return w
            zchunk(0, 0)
            zchunk(1, 1)
            zq = Mb[0]
            if PSTOP <= 3:
                return
            B.act(junk[0:R, 0:512], zq[0:R, :], AF.Square, ["M0"], ["junk"])
            B.red(st1[0:R, 2:10], junk[0:R, 0:512].rearrange("p (h d) -> p h d", h=8), ["junk"], ["st1"])
            B.rsqrt(st1[0:R, 2:10], st1[0:R, 2:10], 1.0 / 64, ["st1"], ["st1"])
            qn3 = qn[0:R, :].rearrange("p (h d) -> p h d", h=8)
            B.tt("vector", qn3, zq[0:R, :].rearrange("p (h d) -> p h d", h=8),
                 st1[0:R, 2:10].unsqueeze(2).broadcast_to([R, 8, 64]), ALU.mult, ["M0", "st1"], ["qn"])
            B.tt("gpsimd", qn3, qn3, gq_sb[0:R, :].unsqueeze(1).broadcast_to([R, 8, 64]), ALU.mult, ["qn", "gq_sb"], ["qn"])
            if PSTOP <= 4:
                return
            rope16(R, qn3, 8, "qn")
            if PSTOP <= 5:
                return
            for g_ in range(2):
                B.cp("scalar", q_bf[0:R, 4 * g_:4 * g_ + 4, 64 * g_:64 * g_ + 64], qn3[:, 4 * g_:4 * g_ + 4, :], ["qn"], ["q_bf"])
            B.cp("scalar", kvo[0:R, 0:512], Mb[1][0:R, :], ["M1"], ["kvo"])
            if PSTOP <= 6:
                return
            zchunk(2, 0)
            B.cp("scalar", kvo[0:R, 512:768], Mb[0][0:R, 0:256], ["M0"], ["kvo"])
            ropeR64(R, Mb[0][0:R, 256:512], rqk_bf[0:R, 0:256], "M0")
            if PSTOP <= 7:
                return
            zchunk(3, 1)
            if "c3r" not in SKIP:
                ropeR64(R, Mb[1][0:R, 0:256], rqk_bf[0:R, 256:512], "M1")
            if "c3a" not in SKIP:
                B.cp("scalar", rv_bf[0:R, 0:256], Mb[1][0:R, 256:512], ["M1"], ["rv_bf"])
            if PSTOP2 <= 1:
                return
            zchunk(4, 0)
            B.cp("scalar", rv_bf[0:R, 256:512], Mb[0][0:R, 0:256], ["M0"], ["rv_bf"])
            B.cp("vector", rg_f[0:R, 0:256], Mb[0][0:R, 256:512], ["M0"], ["rg_f"])
            B.act(rg_e[0:R, 0:256], Mb[0][0:R, 256:512], AF.Exp, ["M0"], ["oret"], scale=-1.0)
            if PSTOP2 <= 2:
                return
            zchunk(5, 1)
            B.cp("vector", rg_f[0:R, 256:512], Mb[1][0:R, 0:256], ["M1"], ["rg_f"])
            B.act(rg_e[0:R, 256:512], Mb[1][0:R, 0:256], AF.Exp, ["M1"], ["oret"], scale=-1.0)
            B.act(gates[0:R, :], Mb[1][0:R, 256:280], AF.Exp, ["M1"], ["gates"], scale=-1.0)
            B.ts1("vector", gates[0:R, :], gates[0:R, :], 1.0, ALU.add, ["gates"], ["gates"])
            B.recip(gates[0:R, :], gates[0:R, :], ["gates"], ["gates"])
            if PSTOP2 <= 3:
                return
            B.ts1("gpsimd", rg_e[0:R, :], rg_e[0:R, :], 1.0, ALU.add, ["oret"], ["oret"])
            B.recip(rg_e[0:R, :], rg_e[0:R, :], ["oret"], ["oret"])
            B.tt("gpsimd", rg_f[0:R, :], rg_f[0:R, :], rg_e[0:R, :], ALU.mult, ["rg_f", "oret"], ["rg_f"])
            if PSTOP <= 8:
                return
            k4 = kvo[0:R, :].rearrange("p (b v g d) -> p b v g d", b=3, v=2, g=2)[:, :, 0, :, :]
            B.act(junk[0:R, 0:384].rearrange("p (b g d) -> p b g d", b=3, g=2), k4, AF.Square, ["kvo"], ["junk"])
            B.red(st1[0:R, 10:16], junk[0:R, 0:384].rearrange("p (h d) -> p h d", h=6), ["junk"], ["st1"])
            B.rsqrt(st1[0:R, 10:16], st1[0:R, 10:16], 1.0 / 64, ["st1"], ["st1"])
            B.tt("vector", k4, k4, st1[0:R, 10:16].rearrange("p (b g) -> p b g", b=3).unsqueeze(3).broadcast_to([R, 3, 2, 64]),
                 ALU.mult, ["kvo", "st1"], ["kvo"])
            B.tt("gpsimd", k4, k4, gk_sb[0:R, :].rearrange("p (b d) -> p b d", b=3).unsqueeze(2).broadcast_to([R, 3, 2, 64]),
                 ALU.mult, ["kvo", "gk_sb"], ["kvo"])
            for b_ in range(3):
                rope16(R, k4[:, b_, :, :], 2, "kvo")
            if PSTOP <= 9:
                return
            for i_, d_ in enumerate(kv_douts):
                if d_ is not None:
                    B.dma(d_, kvo[0:R, i_ * 256:(i_ + 1) * 256], ["kvo"], [], "kvo_out")
            B.cp("vector", k_bf[0:R, :, :].rearrange("p b (g d) -> p b g d", g=2), k4, ["kvo"], ["k_bf"])
            B.cp("gpsimd", cv_bf[0:R, :], kvo[0:R, 128:256], ["kvo"], ["cv_bf"])

        def rope16(R, x3, H, res):
            C = ropeN[0:R, 0:16].unsqueeze(1).broadcast_to([R, H, 16])
            nsn = ropeN[0:R, 16:24].unsqueeze(1).broadcast_to([R, H, 8])
            psn = ropeN[0:R, 24:32].unsqueeze(1).broadcast_to([R, H, 8])
            t1 = rtmp[0:R, 0:H * 16].rearrange("p (h d) -> p h d", h=H)
            t2 = rtmp2[0:R, 0:H * 16].rearrange("p (h d) -> p h d", h=H)
            B.tt("vector", t1, x3[:, :, 0:16], C, ALU.mult, [res, "ropeN"], ["rtmp"])
            B.tt("gpsimd", t2[:, :, 0:8], x3[:, :, 8:16], nsn, ALU.mult, [res, "ropeN"], ["rtmp2"])
            B.tt("gpsimd", t2[:, :, 8:16], x3[:, :, 0:8], psn, ALU.mult, [res, "ropeN"], ["rtmp2"])
            B.tt("vector", x3[:, :, 0:16], t1, t2, ALU.add, ["rtmp", "rtmp2"], [res])

        def ropeR64(R, zsrc, dst_bf, res):
            H = 4
            z3 = zsrc.rearrange("p (h d) -> p h d", h=H)
            C = ropeR[0:R, 0:64].unsqueeze(1).broadcast_to([R, H, 64])
            nsn = ropeR[0:R, 64:96].unsqueeze(1).broadcast_to([R, H, 32])
            psn = ropeR[0:R, 96:128].unsqueeze(1).broadcast_to([R, H, 32])
            t1 = rtmp[0:R, 0:256].rearrange("p (h d) -> p h d", h=H)
            t2 = rtmp2[0:R, 0:256].rearrange("p (h d) -> p h d", h=H)
            B.tt("vector", t1, z3, C, ALU.mult, [res, "ropeR"], ["rtmp"])
            B.tt("vector", t2[:, :, 0:32], z3[:, :, 32:64], nsn, ALU.mult, [res, "ropeR"], ["rtmp2"])
            B.tt("vector", t2[:, :, 32:64], z3[:, :, 0:32], psn, ALU.mult, [res, "ropeR"], ["rtmp2"])
            B.tt("gpsimd", dst_bf.rearrange("p (h d) -> p h d", h=H), t1, t2, ALU.add, ["rtmp", "rtmp2"], ["rqk_bf"])

        def compress(ncols, first_c, skip_first):
            n0 = 1 if skip_first else 0
            nb = ncols - n0
            c0 = first_c + n0
            for kv in range(2):
                for g in range(2):
                    i = 0
                    for a in range(2):
                        for s in range(16):
                            l = 16 * a + s
                            off = 16 * (n0 + a) + s
                            rhs = cmpT[64 * g:64 * g + 64, kv, off:off + 16 * (nb - 1) + 1:16]
                            B.mm(Ab[2][64 * g:64 * g + 64, kv * 128:kv * 128 + nb], W1[64 * g:64 * g + 64, kv, l, :], rhs,
                                 i == 0, i == 31, ["W1", "cmpT"], ["A2"])
                            i += 1
            if 'cg' in SKIP:
                return
            hw = hwork
            for kv in range(2):
                B.act(hw[:, 0, kv * 128:kv * 128 + nb], Ab[2][:, kv * 128:kv * 128 + nb], AF.Identity, ["A2", "peterm"], ["hwork"],
                      bias=peterm[:, kv:kv + 1])
            xx = hw[:, 0, :].rearrange("p (k c) -> p k c", k=2)[:, :, 0:nb]
            t_ = hw[:, 1, :].rearrange("p (k c) -> p k c", k=2)[:, :, 0:nb]
            u_ = hw[:, 2, :].rearrange("p (k c) -> p k c", k=2)[:, :, 0:nb]
            B.tt("vector", t_, xx, xx, ALU.mult, ["hwork"], ["hwork"])
            B.ts("vector", t_, t_, 0.044715, 1.0, ALU.mult, ALU.add, ["hwork"], ["hwork"])
            B.tt("vector", t_, t_, xx, ALU.mult, ["hwork"], ["hwork"])
            B.act(u_, t_, AF.Exp, ["hwork"], ["hwork"], scale=-2.0 * math.sqrt(2.0 / math.pi))
            B.ts1("vector", u_, u_, 1.0, ALU.add, ["hwork"], ["hwork"])
            B.recip(u_, u_, ["hwork"], ["hwork"])
            B.tt("vector", hgT[:, :, c0:c0 + nb], xx, u_, ALU.mult, ["hwork"], ["hgT"])
            if 'ck' in SKIP:
                return
            B.mm(Ab[2][:, 256:256 + nb], w2[:, 0:128], hgT[:, 0, c0:c0 + nb], True, True, ["w2", "hgT"], ["A2"])
            B.cp("vector", kcT[:, c0:c0 + nb], Ab[2][:, 256:256 + nb], ["A2"], ["kcT"])
            if 'cv' in SKIP:
                return
            c_last = c0 + nb - 1
            for ctile in range(c0 // 128, c_last // 128 + 1):
                M = min(128, c_last + 1 - 128 * ctile)
                B.mm(Ab[2][0:M, 0:128], hgT[:, 1, 128 * ctile:128 * ctile + M], w2[:, 128:256], True, True, ["w2", "hgT"], ["A2"])
                B.cp("vector", vc[0:M, ctile, :, 0:64], Ab[2][0:M, 0:128].rearrange("p (g d) -> p g d", g=2), ["A2"], ["vc"])

        def emask(kt, QN):
            return (ct["E"][:, (kt % 32) * 128:(kt % 32 + 1) * 128], negT4[:, kt // 32, 0:4 * QN], ["k_E", "negT4"])

        def branch(QN, qrhs, qres, tiles, acc, accres, W, tagU=None):
            n = len(tiles)
            NQ = 4 * QN
            CH = max(1, 512 // NQ)
            first = True
            for c0 in range(0, n, CH):
                chunk = tiles[c0:c0 + CH]
                sbk = scnt[0] % 2
                scnt[0] += 1
                Sx = Sb[sbk]
                sres = "S%d" % sbk
                Mmax = max(tl["M"] for tl in chunk)
                for j, tl in enumerate(chunk):
                    M = tl["M"]
                    masks = tl.get("masks", [])
                    B.mm(Sx[0:M, j * NQ:(j + 1) * NQ], tl["kT"], qrhs, True, len(masks) == 0, [tl["kres"], qres], [sres])
                    for mi, (ml, mr, mres) in enumerate(masks):
                        B.mm(Sx[0:M, j * NQ:(j + 1) * NQ], ml, mr, False, mi == len(masks) - 1, mres, [sres])
                eb = ecnt[0] % 2
                ecnt[0] += 1
                eres = "e%d" % eb
                B.act(ebuf[eb][0:Mmax, 0:len(chunk) * NQ], Sx[0:Mmax, 0:len(chunk) * NQ], AF.Exp, [sres], [eres])
                for j, tl in enumerate(chunk):
                    M = tl["M"]
                    last = (c0 + j == n - 1)
                    for r in range(4):
                        B.mm(acc[:, r, :], ebuf[eb][0:M, j * NQ + r * QN:j * NQ + (r + 1) * QN], tl["v"], first and r == 0, last,
                             [eres, tl["vres"]], [accres], nocheck=True)
                    if tagU is not None:
                        for r in range(4):
                            B.mm(tagU[0][:, r, :], ebuf[eb][0:M, j * NQ + r * QN:j * NQ + (r + 1) * QN], tl["u"], first and r == 0, last,
                                 [eres, "k_ov"], [tagU[1]], nocheck=True)
                    first = False

        def finish_branch(QN, acc, accres, odst, ores, col):
            B.ts1("vector", rs_c[0:QN, col:col + 4], acc[:, :, 64], 1e-30, ALU.max, [accres], ["rs_c"])
            B.recip(rs_c[0:QN, col:col + 4], rs_c[0:QN, col:col + 4], ["rs_c"], ["rs_c"])
            B.tt("vector", odst[0:QN, :, :], acc[:, :, 0:64], rs_c[0:QN, col:col + 4].unsqueeze(2).broadcast_to([QN, 4, 64]),
                 ALU.mult, [accres, "rs_c"], [ores])

        def combine(QN, g, gates_ap):
            g3 = gates_ap.rearrange("p (h b) -> p h b", b=3)[:, 4 * g:4 * g + 4, :]
            for bi, (o_, ores) in enumerate(((ocmp, "ocmp"), (osel, "osel"), (owin, "owin"))):
                B.tt("vector" if bi != 1 else "gpsimd", o_[0:QN, :, :], o_[0:QN, :, :],
                     g3[:, :, bi:bi + 1].broadcast_to([QN, 4, 64]), ALU.mult, [ores, "gates"], [ores])
            B.tt("vector", ocmp[0:QN, :, :], ocmp[0:QN, :, :], osel[0:QN, :, :], ALU.add, ["ocmp", "osel"], ["ocmp"])
            B.tt("vector", mixed[0:QN, g * 256:(g + 1) * 256].rearrange("p (r d) -> p r d", r=4), ocmp[0:QN, :, :], owin[0:QN, :, :],
                 ALU.add, ["ocmp", "owin"], ["mixed"])

        def select_blocks(QN, accU, ures, rs_col, bonus_fn, kth):
            B.ts1("vector", imp[0:QN, :], accU[:, 0, :], rs_c[0:QN, rs_col:rs_col + 1], ALU.mult, [ures, "rs_c"], ["imp"])
            for r in range(1, 4):
                B.stt("vector", imp[0:QN, :], accU[:, r, :], rs_c[0:QN, rs_col + r:rs_col + r + 1], imp[0:QN, :],
                      ALU.mult, ALU.add, [ures, "rs_c", "imp"], ["imp"])
            bonus_fn()
            B.P.add("vector", lambda e: e.max(out=mx[0:QN, 0:8], in_=imp[0:QN, :]), ["imp"], ["mx"])
            B.P.add("vector", lambda e: e.match_replace(out=imp2[0:QN, :], in_to_replace=mx[0:QN, 0:8], in_values=imp[0:QN, :],
                                                        imm_value=-1e30), ["imp", "mx"], ["imp2"])
            B.P.add("vector", lambda e: e.max(out=mx[0:QN, 8:16], in_=imp2[0:QN, :]), ["imp2"], ["mx"])
            B.ts1("vector", imp2[0:QN, :], imp[0:QN, :], mx[0:QN, 8 + kth - 9:8 + kth - 8], ALU.is_ge, ["imp", "mx"], ["imp2"])
            B.ts("vector", negq[0:QN, :], imp2[0:QN, :], -1.0, -NEGM, ALU.add, ALU.mult, ["imp2"], ["negq"])
            for hf in range(2):
                B.tr(T0[0:64, hf * 128:hf * 128 + QN], negq[0:QN, hf * 64:(hf + 1) * 64], ident[0:QN, 0:QN], ["negq", "k_ident"], ["T0"])
            for hf in range(2):
                B.cp("vector", negT4[:, hf, 0:4 * QN].rearrange("p (r q) -> p r q", r=4),
                     T0[0:64, hf * 128:hf * 128 + QN].unsqueeze(1).broadcast_to([64, 4, QN]), ["T0"], ["negT4"])

        def retention(R, t):
            gam = [1.0 - 2.0 ** (-5.0 - h) for h in range(4)]
            for h in range(4):
                B.mm(Mb[0][0:R, h * 128:h * 128 + R], rkT[:, h, 0:R], rqT[:, h, 0:R], True, True, ["rkT", "rqT"], ["M0"])
            B.tt("vector", sTm[0:R, :].rearrange("p (h i) -> p h i", h=4)[:, :, 0:R],
                 Mb[0][0:R, :].rearrange("p (h i) -> p h i", h=4)[:, :, 0:R],
                 ct["decT"][0:R, :].rearrange("p (h i) -> p h i", h=4)[:, :, 0:R], ALU.mult, ["M0", "k_decT"], ["sTm"])
            B.tt("gpsimd", qcr[:, :, 0:R], rqT[:, :, 0:R],
                 ct["crossT"][:, :].rearrange("p (h i) -> p h i", h=4)[:, :, 0:R], ALU.mult, ["rqT", "k_crossT"], ["qcr"])
            for h in range(4):
                B.mm(Mb[1][0:R, h * 128:(h + 1) * 128], sTm[0:R, h * 128:h * 128 + R], rv_bf[0:R, h * 128:(h + 1) * 128],
                     True, False, ["sTm", "rv_bf"], ["M1"])
                B.mm(Mb[1][0:R, h * 128:(h + 1) * 128], qcr[:, h, 0:R], Sbf[:, h, :], False, True, ["qcr", "Sbf"], ["M1"])
            B.tt("vector", kwk[0:R, :].rearrange("p (h d) -> p h d", h=4), rqk_bf[0:R, 256:512].rearrange("p (h d) -> p h d", h=4),
                 ct["wtab"][0:R, :].unsqueeze(2).broadcast_to([R, 4, 64]), ALU.mult, ["rqk_bf", "k_wtab"], ["kwk"])
            for h in range(4):
                B.mm(Mb[0][0:64, h * 128:(h + 1) * 128], kwk[0:R, h * 64:(h + 1) * 64],
                     rv_bf[0:R, h * 128:(h + 1) * 128], True, True, ["kwk", "rv_bf", "sTm"], ["M0"])
            for h in range(4):
                B.stt("vector", Sf[:, h, :], Sf[:, h, :], float(gam[h] ** 128), Mb[0][0:64, h * 128:(h + 1) * 128],
                      ALU.mult, ALU.add, ["Sf", "M0"], ["Sf"])
            B.cp("gpsimd", Sbf[:, :, :], Sf[:, :, :], ["Sf"], ["Sbf"])
            B.act(junk[0:R, 0:512], Mb[1][0:R, :], AF.Square, ["M1"], ["junk"])
            B.red(st1[0:R, 2:6], junk[0:R, 0:512].rearrange("p (h e) -> p h e", h=4), ["junk"], ["st1"])
            B.rsqrt(st1[0:R, 2:6], st1[0:R, 2:6], 1.0 / 128, ["st1"], ["st1"])
            B.tt("vector", oret[0:R, :].rearrange("p (h e) -> p h e", h=4), Mb[1][0:R, :].rearrange("p (h e) -> p h e", h=4),
                 st1[0:R, 2:6].unsqueeze(2).broadcast_to([R, 4, 128]), ALU.mult, ["M1", "st1"], ["oret"])
            B.tt("gpsimd", mixed[0:R, 512:1024], oret[0:R, :], rg_f[0:R, :], ALU.mult, ["oret", "rg_f"], ["mixed"])

        def out_proj(R, x_sb, xr, y_dst):
            for k in range(8):
                B.tr(T0[:, k * 128:k * 128 + R], mixed[0:R, k * 128:(k + 1) * 128], ident[0:R, 0:R], ["mixed", "k_ident"], ["T0"])
            B.cp("vector", mixT[:, :, 0:R], T0[:, :].rearrange("p (k t) -> p k t", k=8)[:, :, 0:R], ["T0"], ["xnT"])
            for nh in range(2):
                for k in range(8):
                    B.mm(Mb[nh][0:R, :], mixT[:, k, 0:R], w_out_bf[:, k, nh * 512:(nh + 1) * 512], k == 0, k == 7,
                         ["xnT", "w_out_bf"], ["M%d" % nh])
                B.tt("vector", x_sb[0:R, nh * 512:(nh + 1) * 512], Mb[nh][0:R, :], x_sb[0:R, nh * 512:(nh + 1) * 512], ALU.add,
                     ["M%d" % nh, xr], [xr])
            B.dma(y_dst, x_sb[0:R, :], [xr], [], xr)

        if NT > 0 and DBG >= 2:
            for t in range(NT):
                xslot = 0
                last4 = t >= NT - 4
                kvd = [o_kvc_p[t * 128:(t + 1) * 128, :], o_kvs_p[t * 128:(t + 1) * 128, :],
                       o_kvw_p[(t - (NT - 4)) * 128:(t - (NT - 4) + 1) * 128, :] if (last4 and NT >= 4) else None]
                if NT < 4:
                    kvd[2] = o_kvw_p[t * 128:(t + 1) * 128, :]
                project(128, x_p[t * 128:(t + 1) * 128, :], xslot, cd["ropeN_p"][t * 128:(t + 1) * 128, :],
                        cd["ropeR_p"][t * 128:(t + 1) * 128, :], kvd, "p%d" % t)
                if DBG < 3:
                    continue
                for h in range(8):
                    B.tr(T0[:, h * 128:(h + 1) * 128], q_bf[:, h, :], ident[:, :], ["q_bf", "k_ident"], ["T0"])
                B.cp("vector", qT[:, :, :], T0[:, :].rearrange("p (h t) -> p h t", h=8), ["T0"], ["qT"])
                B.tr(T0[:, 0:128], k_bf[:, 1, :], ident[:, :], ["k_bf", "k_ident"], ["T0"])
                B.tr(T0[:, 128:256], k_bf[:, 2, :], ident[:, :], ["k_bf", "k_ident"], ["T0"])
                B.tr(T0[:, 4 * 128:5 * 128], k_bf[:, 0, :], ident[:, :], ["k_bf", "k_ident"], ["T0"])
                B.tr(T0[:, 5 * 128:6 * 128], cv_bf[:, :], ident[:, :], ["cv_bf", "k_ident"], ["T0"])
                wsl = t % 8
                B.cp("vector", ksT[:, t * 128:(t + 1) * 128], T0[:, 0:128], ["T0"], ["ksT"])
                B.cp("vector", kwT[:, wsl, :], T0[:, 128:256], ["T0"], ["kwT"])
                if t > 0:
                    B.cp("vector", cmpT[:, :, 0:16], cmpT[:, :, 128:144], ["cmpT"], ["cmpT"])
                B.cp("vector", cmpT[:, :, 16:144], T0[:, 512:768].rearrange("p (k t) -> p k t", k=2), ["T0"], ["cmpT"])
                B.cp("gpsimd", vs[:, t, :, 0:64], kvo[:, 256:512].rearrange("p (v g d) -> p v g d", v=2, g=2)[:, 1, :, :], ["kvo"], ["vs"])
                B.cp("gpsimd", vw[:, wsl, :, 0:64], kvo[:, 512:768].rearrange("p (v g d) -> p v g d", v=2, g=2)[:, 1, :, :], ["kvo"], ["vw"])
                for j in range(8):
                    B.tr(T0[0:64, j * 128:(j + 1) * 128], rqk_bf[:, j * 64:(j + 1) * 64], ident[:, :], ["rqk_bf", "k_ident"], ["T0"])
                B.cp("vector", rqT[:, :, :], T0[0:64, 0:512].rearrange("p (h t) -> p h t", h=4), ["T0"], ["rqT"])
                B.cp("vector", rkT[:, :, :], T0[0:64, 512:1024].rearrange("p (h t) -> p h t", h=4), ["T0"], ["rkT"])
                if 'cmp' not in SKIP:
                    compress(8, 8 * t - 1, t == 0)
                ncb = 8 * t + 7
                for g in range(2 if DBG >= 4 else 0):
                    qrhs = qT[:, 4 * g:4 * g + 4, :].rearrange("p r t -> p (r t)")
                    tiles = []
                    nct = (ncb + 127) // 128
                    for ctile in range(nct):
                        M = min(128, ncb - 128 * ctile)
                        delta = t - 16 * ctile
                        masks = []
                        if delta <= 16:
                            a = 129 - 8 * delta
                            masks.append((ct["shid"][0:8, a:a + M], ct["cbase4"][0:8, :], ["k_shid", "k_cbase4"]))
                        tiles.append(dict(M=M, kT=kcT[:, 128 * ctile:128 * ctile + M], kres="kcT", masks=masks,
                                          v=vc[0:M, ctile, g, :], vres="vc", u=ct["ov"][0:M, ctile * 128:(ctile + 1) * 128]))
                    accC = Ab[0][:, 0:260].rearrange("p (r w) -> p r w", r=4)
                    accU = Ab[1][:, :].rearrange("p (r w) -> p r w", r=4)
                    branch(128, qrhs, "qT", tiles, accC, "A0", 65, tagU=(accU, "A1"))
                    finish_branch(128, accC, "A0", ocmp, "ocmp", 0)

                    def bonus(t=t):
                        lo = max(0, 2 * t - 1)
                        po = lo - (2 * t - 1)
                        B.tt("vector", imp[:, lo:2 * t + 2], imp[:, lo:2 * t + 2], ct["pat"][:, po:3], ALU.add, ["imp", "k_pat"], ["imp"])
                        B.ts1("vector", imp[:, 0:1], imp[:, 0:1], 1e4, ALU.add, ["imp"], ["imp"])
                    select_blocks(128, accU, "A1", 0, bonus, 16)
                    tiles = []
                    for wt in range(max(0, t - 4), t + 1):
                        masks = []
                        if wt == t - 4:
                            masks.append((ident[:, :], ct["winfirst4"][:, :], ["k_ident", "k_winfirst4"]))
                        if wt == t:
                            masks.append((ident[:, :], ct["causal4"][:, :], ["k_ident", "k_causal4"]))
                        tiles.append(dict(M=128, kT=kwT[:, wt % 8, :], kres="kwT", masks=masks,
                                          v=vw[:, wt % 8, g, :], vres="vw"))
                    accW = Ab[2][:, 0:260].rearrange("p (r w) -> p r w", r=4)
                    branch(128, qrhs, "qT", tiles, accW, "A2", 65)
                    finish_branch(128, accW, "A2", owin, "owin", 8)
                    tiles = []
                    for kt in range(t + 1):
                        masks = [emask(kt, 128)]
                        if kt == t:
                            masks.append((ident[:, :], ct["causal4"][:, :], ["k_ident", "k_causal4"]))
                        tiles.append(dict(M=128, kT=ksT[:, kt * 128:(kt + 1) * 128], kres="ksT", masks=masks,
                                          v=vs[:, kt, g, :], vres="vs"))
                    accS = Ab[0][:, 0:260].rearrange("p (r w) -> p r w", r=4)
                    branch(128, qrhs, "qT", tiles, accS, "A0", 65)
                    finish_branch(128, accS, "A0", osel, "osel", 4)
                    combine(128, g, gates[:, :])
                if DBG < 5:
                    continue
                retention(128, t)
                out_proj(128, xs_[xslot], "xs%d" % xslot, y_p[t * 128:(t + 1) * 128, :])
            if DBG >= 5:
                B.dma(o_ret_p, Sf[:, :, :].rearrange("p h e -> p (h e)"), ["Sf"], [], "o_ret_p")


        if NB > 0 and DBG >= 7:
            NS = PAST // 64
            GP = min(8, NPG)
            xslot = 0
            xr = "xs%d" % xslot
            qTs = sb("qTs", [128, NB, 8, 4], BF16)
            ksTn = sb("ksTn", [128, RS], BF16)
            kwTn = sb("kwTn", [128, RS], BF16)
            vsrc = xn
            vrow = sb("vrow", [4, 1024], BF16)
            pt_i = sb("pt_i", [128, NPG], I32)
            pt_f = sb("pt_f", [128, NPG], F32)
            idx = sb("idx", [128, NPG], I32)
            idx64 = sb("idx64", [128, NPG], I32)
            kwTs = sb("kwTs", [128, 516], BF16)
            vws = sb("vws", [128, 5, 2, 65], BF16)
            wstV = qn
            oall = sb("oall", [128, 3, 2, 256], BF16)
            ob16 = sb("ob16", [4, 3, 256], BF16)
            Sfb = Sf
            Sbb = Sbf
            sTm4 = sb("sTm4", [4, 16], BF16)
            kwk4 = sb("kwk4", [4, 256], BF16)
            oret4 = sb("oret4", [4, 512], F32)
            oretS = oret
            B.memset("vector", vws[:, :, :, 64:65], 1.0, ["vws"])
            pgb = [stage[0][:, 0:512], stage[0][:, 512:1024], stage[1][:, 512:1024]]
            wstK = stage[1][:, 0:512]

            project(RS, x_s, xslot, cd["ropeN_s"], cd["ropeR_s"], [o_kvc_s, o_kvs_s, None], "s")
            B.dma(o_kvw_s[:, 0:508, :], st_win[:, 4:512, :], [], [], "kvw_d2d")
            for b in range(NB):
                B.dma(o_kvw_s[b, 508:512, :], kvo[4 * b:4 * b + 4, 512:768], ["kvo"], [], "kvo_out")
            idr = ident[0:RS, 0:RS]
            for h in range(8):
                B.tr(T0[:, h * 128:h * 128 + RS], q_bf[0:RS, h, :], idr, ["q_bf", "k_ident"], ["T0"])
            B.cp("vector", qTs[:, :, :, :].rearrange("p b h s -> p h b s"),
                 T0[:, :].rearrange("p (h t) -> p h t", h=8)[:, :, 0:RS].rearrange("p h (b s) -> p h b s", s=4), ["T0"], ["qTs"])
            B.tr(T0[:, 0:RS], k_bf[0:RS, 1, :], idr, ["k_bf", "k_ident"], ["T0"])
            B.tr(T0[:, 128:128 + RS], k_bf[0:RS, 2, :], idr, ["k_bf", "k_ident"], ["T0"])
            B.cp("vector", ksTn[:, :], T0[:, 0:RS], ["T0"], ["ksTn"])
            B.cp("vector", kwTn[:, :], T0[:, 128:128 + RS], ["T0"], ["kwTn"])
            for j in range(8):
                B.tr(T0[0:64, j * 128:j * 128 + RS], rqk_bf[0:RS, j * 64:(j + 1) * 64], idr, ["rqk_bf", "k_ident"], ["T0"])
            B.cp("vector", rqT[:, :, 0:RS], T0[0:64, 0:512].rearrange("p (h t) -> p h t", h=4)[:, :, 0:RS], ["T0"], ["rqT"])
            B.cp("vector", rkT[:, :, 0:RS], T0[0:64, 512:1024].rearrange("p (h t) -> p h t", h=4)[:, :, 0:RS], ["T0"], ["rkT"])
            B.tt("vector", qcr[:, :, 0:RS].rearrange("p h (b s) -> p h b s", s=4), rqT[:, :, 0:RS].rearrange("p h (b s) -> p h b s", s=4),
                 ct["cross4T"][:, :].rearrange("p (h s) -> p h s", h=4).unsqueeze(2).broadcast_to([64, 4, NB, 4]), ALU.mult,
                 ["rqT", "k_cross4T"], ["qcr"])
            B.cp("vector", vsrc[0:RS, 0:128], kvo[0:RS, 384:512], ["kvo"], ["xn"])
            B.cp("vector", vsrc[0:RS, 128:256], kvo[0:RS, 640:768], ["kvo"], ["xn"])
            B.cp("vector", vsrc[0:RS, 256:768], rv_bf[0:RS, :], ["rv_bf"], ["xn"])
            B.cp("vector", vsrc[0:RS, 768:1024], rqk_bf[0:RS, 256:512], ["rqk_bf"], ["xn"])
            gam = [1.0 - 2.0 ** (-5.0 - h) for h in range(4)]

            for b in range(NB):
                B.dma(vrow[0:4, :], vsrc[4 * b:4 * b + 4, :], ["xn"], ["vrow"], "vrow")
                B.dma(pt_i[:, :], ptab[0:1, b * NPG:(b + 1) * NPG].partition_broadcast(128), [], ["pt_i"], "pt_i")
                B.cp("vector", pt_f[:, :], pt_i[:, :], ["pt_i"], ["pt_f"])
                B.ts("vector", pt_f[:, :], pt_f[:, :], 128.0, ct["iota"][:, 0:1], ALU.mult, ALU.add, ["pt_f", "k_iota"], ["pt_f"])
                B.cp("vector", idx[:, :], pt_f[:, :], ["pt_f"], ["idx"])
                B.cp("vector", pt_f[:, :], pt_i[:, :], ["pt_i", "idx"], ["pt_f"])
                B.ts("vector", pt_f[:, :], pt_f[:, :], 64.0, ct["iota"][:, 0:1], ALU.mult, ALU.add, ["pt_f", "k_iota"], ["pt_f"])
                B.cp("vector", idx64[:, :], pt_f[:, :], ["pt_f"], ["idx64"])
                B.dma(wstK, st_winK[b], [], ["wstK"], "wstK")
                B.cp("vector", kwTs[:, 0:512], wstK, ["wstK"], ["kwTs"])
                B.cp("vector", kwTs[:, 512:516], kwTn[:, 4 * b:4 * b + 4], ["kwTn"], ["kwTs"])
                B.dma(wstV[:, :].rearrange("p (t c) -> p t c", t=4), st_winV[b].rearrange("(t p) c -> p t c", p=128), [], ["qn"], "qn")
                B.cp("vector", vws[:, 0:4, :, 0:64], wstV[:, :].rearrange("p (t g d) -> p t g d", t=4, g=2), ["qn"], ["vws"])
                B.cp("vector", vws[0:4, 4, :, 0:64], vrow[0:4, 128:256].rearrange("p (g d) -> p g d", g=2), ["vrow"], ["vws"])
                B.cp("vector", ksT[:, PAST:PAST + 4], ksTn[:, 4 * b:4 * b + 4], ["ksTn"], ["ksT"])
                B.cp("vector", vs[0:4, NPG, :, 0:64], vrow[0:4, 0:128].rearrange("p (g d) -> p g d", g=2), ["vrow"], ["vs"])
                for pg in range(NPG):
                    sl = pg % 3
                    B.P.add("gpsimd", (lambda o_, i_: (lambda e: e.indirect_dma_start(out=o_, out_offset=None, in_=c_pg,
                            in_offset=bass.IndirectOffsetOnAxis(ap=i_, axis=0))))(pgb[sl], idx[:, pg:pg + 1]),
                            ["idx"], ["pgb%d" % sl], dma="pgb%d" % sl)
                    lc = 16 + (pg % GP) * 128
                    B.cp("vector", cmpT[:, :, lc:lc + 128], pgb[sl][:, 0:256].rearrange("p (k r) -> p k r", k=2), ["pgb%d" % sl], ["cmpT"])
                    B.cp("scalar", ksT[:, pg * 128:(pg + 1) * 128], pgb[sl][:, 256:384], ["pgb%d" % sl], ["ksT"])
                    B.cp("gpsimd", vs[:, pg, :, 0:64], pgb[sl][:, 384:512].rearrange("p (g d) -> p g d", g=2), ["pgb%d" % sl], ["vs"])
                    if (pg + 1) % GP == 0:
                        Gi = pg // GP
                        compress(8 * GP, 8 * GP * Gi - 1, Gi == 0)
                        B.cp("vector", cmpT[:, :, 0:16], cmpT[:, :, 128 * GP:128 * GP + 16], ["cmpT"], ["cmpT"])
                for g in range(2):
                    qrhs = qTs[:, b, 4 * g:4 * g + 4, :].rearrange("p r s -> p (r s)")
                    accC = Ab[0][0:4, 0:260].rearrange("p (r w) -> p r w", r=4)
                    accU = Ab[1][0:4, :].rearrange("p (r w) -> p r w", r=4)
                    tiles = []
                    for ctile in range((NCS + 127) // 128):
                        M = min(128, NCS - 128 * ctile)
                        tiles.append(dict(M=M, kT=kcT[:, 128 * ctile:128 * ctile + M], kres="kcT", masks=[],
                                          v=vc[0:M, ctile, g, :], vres="vc", u=ct["ov"][0:M, ctile * 128:(ctile + 1) * 128]))
                    branch(4, qrhs, "qTs", tiles, accC, "A0", 65, tagU=(accU, "A1"))
                    finish_branch(4, accC, "A0", ocmp, "ocmp", 0)

                    def bonus_s():
                        B.ts1("vector", imp[0:4, 0:1], imp[0:4, 0:1], 1e4, ALU.add, ["imp"], ["imp"])
                        B.ts1("vector", imp[0:4, NS - 1:NS], imp[0:4, NS - 1:NS], 1e4, ALU.add, ["imp"], ["imp"])
                    select_blocks(4, accU, "A1", 0, bonus_s, 15)
                    tiles = []
                    for wt in range(4):
                        masks = [(ident[:, :], ct["swinfirst4"][:, :], ["k_ident", "k_swinfirst4"])] if wt == 0 else []
                        tiles.append(dict(M=128, kT=kwTs[:, wt * 128:(wt + 1) * 128], kres="kwTs", masks=masks,
                                          v=vws[:, wt, g, :], vres="vws"))
                    tiles.append(dict(M=4, kT=kwTs[:, 512:516], kres="kwTs",
                                      masks=[(ident[0:4, 0:4], ct["scausal4"][0:4, :], ["k_ident", "k_scausal4"])],
                                      v=vws[0:4, 4, g, :], vres="vws"))
                    accW = Ab[2][0:4, 0:260].rearrange("p (r w) -> p r w", r=4)
                    branch(4, qrhs, "qTs", tiles, accW, "A2", 65)
                    finish_branch(4, accW, "A2", owin, "owin", 8)
                    tiles = []
                    for kt in range(NPG):
                        tiles.append(dict(M=128, kT=ksT[:, kt * 128:(kt + 1) * 128], kres="ksT", masks=[emask(kt, 4)],
                                          v=vs[:, kt, g, :], vres="vs"))
                    tiles.append(dict(M=4, kT=ksT[:, PAST:PAST + 4], kres="ksT",
                                      masks=[(ident[0:4, 0:4], ct["scausal4"][0:4, :], ["k_ident", "k_scausal4"])],
                                      v=vs[0:4, NPG, g, :], vres="vs"))
                    accS = Ab[0][0:4, 0:260].rearrange("p (r w) -> p r w", r=4)
                    branch(4, qrhs, "qTs", tiles, accS, "A0", 65)
                    finish_branch(4, accS, "A0", osel, "osel", 4)
                    for bi, (o_, ores) in enumerate(((ocmp, "ocmp"), (osel, "osel"), (owin, "owin"))):
                        B.cp("vector", ob16[0:4, bi, :], o_[0:4, :, :].rearrange("p r d -> p (r d)"), [ores], ["ob16_%d" % bi])
                        B.dma(oall[4 * b:4 * b + 4, bi, g, :], ob16[0:4, bi, :], ["ob16_%d" % bi], ["oall"], "ob16_%d" % bi)
                B.dma(Sfb[:, :, :].rearrange("p h e -> p (h e)"), st_ret[:, b * 512:(b + 1) * 512], [], ["Sf"], "Sf")
                B.cp("vector", Sbb[:, :, :], Sfb[:, :, :], ["Sf"], ["Sbf"])
                for h in range(4):
                    B.mm(Mb[0][0:4, h * 4:h * 4 + 4], rkT[:, h, 4 * b:4 * b + 4], rqT[:, h, 4 * b:4 * b + 4], True, True, ["rkT", "rqT"], ["M0"])
                B.tt("vector", sTm4[0:4, :], Mb[0][0:4, 0:16], ct["dec4T"][0:4, :], ALU.mult, ["M0", "k_dec4T"], ["sTm4"])
                for h in range(4):
                    B.mm(Mb[1][0:4, h * 128:(h + 1) * 128], sTm4[0:4, h * 4:h * 4 + 4], vrow[0:4, 256 + h * 128:256 + (h + 1) * 128],
                         True, False, ["sTm4", "vrow"], ["M1"])
                    B.mm(Mb[1][0:4, h * 128:(h + 1) * 128], qcr[:, h, 4 * b:4 * b + 4], Sbb[:, h, :], False, True, ["qcr", "Sbf"], ["M1"])
                B.tt("vector", kwk4[0:4, :].rearrange("p (h d) -> p h d", h=4), vrow[0:4, 768:1024].rearrange("p (h d) -> p h d", h=4),
                     ct["w4tab"][0:4, :].unsqueeze(2).broadcast_to([4, 4, 64]), ALU.mult, ["vrow", "k_w4tab"], ["kwk4"])
                for h in range(4):
                    B.mm(Mb[0][0:64, h * 128:(h + 1) * 128], kwk4[0:4, h * 64:(h + 1) * 64], vrow[0:4, 256 + h * 128:256 + (h + 1) * 128],
                         True, True, ["kwk4", "vrow"], ["M0"])
                for h in range(4):
                    B.stt("vector", Sfb[:, h, :], Sfb[:, h, :], float(gam[h] ** 4), Mb[0][0:64, h * 128:(h + 1) * 128],
                          ALU.mult, ALU.add, ["Sf", "M0"], ["Sf"])
                B.dma(o_ret_s[:, b * 512:(b + 1) * 512], Sfb[:, :, :].rearrange("p h e -> p (h e)"), ["Sf"], [], "Sf")
                B.cp("vector", oret4[0:4, :], Mb[1][0:4, :], ["M1"], ["oret4"])
                B.dma(oretS[4 * b:4 * b + 4, :], oret4[0:4, :], ["oret4"], ["oret"], "oret4")
            B.act(junk[0:RS, 0:512], oretS[0:RS, :], AF.Square, ["oret"], ["junk"])
            B.red(st1[0:RS, 2:6], junk[0:RS, 0:512].rearrange("p (h e) -> p h e", h=4), ["junk"], ["st1"])
            B.rsqrt(st1[0:RS, 2:6], st1[0:RS, 2:6], 1.0 / 128, ["st1"], ["st1"])
            B.tt("vector", oret[0:RS, :].rearrange("p (h e) -> p h e", h=4), oretS[0:RS, :].rearrange("p (h e) -> p h e", h=4),
                 st1[0:RS, 2:6].unsqueeze(2).broadcast_to([RS, 4, 128]), ALU.mult, ["oret", "st1"], ["oret"])
            B.tt("vector", mixed[0:RS, 512:1024], oret[0:RS, :], rg_f[0:RS, :], ALU.mult, ["oret", "rg_f"], ["mixed"])
            for g in range(2):
                for bi, (o_, ores) in enumerate(((ocmp, "ocmp"), (osel, "osel"), (owin, "owin"))):
                    B.cp("vector", o_[0:RS, :, :], oall[0:RS, bi, g, :].rearrange("p (r d) -> p r d", r=4), ["oall"], [ores])
                combine(RS, g, gates[0:RS, :])
            out_proj(RS, xs_[xslot], xr, y_s)
        print("sbuf bytes remaining pass1", nc.sbuf_bytes_remaining)
        print("pass1 ops", P.emit(nc, "a"))

    P2 = Prog()
    with ExitStack() as es:
      if not os.environ.get('K_NOP2'):
          B = Bld(nc, P2, es)
          sb, ps = B.sb, B.ps
          G = 256
          Ub = [ps("U0", [128, 512], F32), ps("U1", [128, 512], F32)]
          Vb = [ps("V0", [128, 512], F32), ps("V1", [128, 512], F32)]
          T0 = ps("T0b", [128, 1024], BF16)
          Yb = [ps("Y0", [128, 512], F32), ps("Y1", [128, 512], F32)]
          ident = sb("ident2", [128, 128], BF16)
          B.dma(ident[:], cd["ident"], [], ["ident2"], "ident2")
          epsc = sb("epsc2", [128, 1], F32)
          B.memset("vector", epsc[:], EPS, ["epsc2"])
          B.eps_ap = epsc
          wfi = sb("wfi", [128, 8, 2 * D_FF], BF16)
          wfo = sb("wfo", [128, NFC, 1024], BF16)
          nmf = sb("nmf", [128, 8], F32)
          wcv = sb("wcv", [128, NFC, 4], F32)
          stage = [sb("stg0", [128, D_FF], F32), sb("stg1", [128, D_FF], F32)]
          B.dma(nmf[:], nm_ffn, [], ["nmf"], "nmf")
          B.dma(wcv[:, :, :].rearrange("p c j -> p (c j)"), wconv, [], ["wcv"], "wcv")
          si = 0
          for k in range(8):
              for hf in range(2):
                  s_ = si % 2
                  B.dma(stage[s_][:, :], w_fi[k * 128:(k + 1) * 128, hf * D_FF:(hf + 1) * D_FF], [], ["stg%d" % s_], "stg%d" % s_)
                  B.ts1("vector" if si % 2 == 0 else "gpsimd", wfi[:, k, hf * D_FF:(hf + 1) * D_FF], stage[s_][:, :], nmf[:, k:k + 1], ALU.mult,
                        ["stg%d" % s_, "nmf"], ["wfi"])
                  si += 1
          for fc in range(0, NFC, 2):
              s_ = si % 2
              B.dma(stage[s_][:, 0:2048].rearrange("p (c n) -> p c n", c=2),
                    w_fo[fc * 128:(fc + 2) * 128, :].rearrange("(c p) n -> p c n", p=128), [], ["stg%d" % s_], "stg%d" % s_)
              B.cp("vector" if si % 2 == 0 else "gpsimd", wfo[:, fc:fc + 2, :], stage[s_][:, 0:2048].rearrange("p (c n) -> p c n", c=2),
                   ["stg%d" % s_], ["wfo"])
              si += 1
          hs = [sb("hs%d" % i, [128, 1024], F32) for i in range(2)]
          hn = sb("hn", [128, 1024], BF16)
          hnT = sb("hnT", [128, 8, G], BF16)
          junk = sb("junk2", [128, 1024], F32)
          st1 = sb("st2", [128, 4], F32)
          ubufs = [sb("ubuf0", [128, G + 2], F32), sb("ubuf1", [128, G + 2], F32)]
          carry = sb("carry", [128, NFC, 2], F32)
          ucs = [sb("uc0", [128, G], F32), sb("uc1", [128, G], F32)]
          ues = [sb("ue0", [128, G], F32), sb("ue1", [128, G], F32)]
          actb = sb("actb", [128, NFC, G], BF16)
          y_sb = sb("y_sb", [128, 1024], F32)
          B.memset("vector", carry[:], 0.0, ["carry"])

          def ffn_group(tiles_rows, loads, stores, conv_w, carry_mode):
              ncol = sum(tiles_rows)
              col = 0
              cols = []
              for i, R in enumerate(tiles_rows):
                  hr = "hs%d" % i
                  B.dma(hs[i][0:R, :], loads[i], [], [hr], hr)
                  B.act(junk[0:R, :], hs[i][0:R, :], AF.Square, [hr], ["junk2", "st2"], accum=st1[0:R, 0:1])
                  B.rsqrt(st1[0:R, 1:2], st1[0:R, 0:1], 1.0 / 1024, ["st2"], ["st2"])
                  B.act(hn[0:R, :], hs[i][0:R, :], AF.Identity, [hr, "st2"], ["hn"], scale=st1[0:R, 1:2])
                  for k in range(8):
                      B.tr(T0[:, k * 128:k * 128 + R], hn[0:R, k * 128:(k + 1) * 128], ident[0:R, 0:R], ["hn", "ident2"], ["T0b"])
                  B.cp("vector", hnT[:, :, col:col + R], T0[:, :].rearrange("p (k t) -> p k t", k=8)[:, :, 0:R], ["T0b"], ["hnT"])
                  cols.append(col)
                  col += R
              for fc in range(NFC):
                  bk = fc % 2
                  for k in range(8):
                      B.mm(Ub[bk][:, 0:ncol], wfi[:, k, fc * 128:(fc + 1) * 128], hnT[:, k, 0:ncol], k == 0, k == 7, ["wfi", "hnT"], ["U%d" % bk])
                  for k in range(8):
                      B.mm(Vb[bk][:, 0:ncol], wfi[:, k, D_FF + fc * 128:D_FF + (fc + 1) * 128], hnT[:, k, 0:ncol], k == 0, k == 7,
                           ["wfi", "hnT"], ["V%d" % bk])
                  uc, ue, ubuf = ucs[bk], ues[bk], ubufs[bk]
                  ucr, uer, ubr = "uc%d" % bk, "ue%d" % bk, "ubuf%d" % bk
                  conv_w(fc, Ub[bk], "U%d" % bk, ncol, uc, ucr, ubuf, ubr)
                  B.act(ue[:, 0:ncol], uc[:, 0:ncol], AF.Exp, [ucr], [uer], scale=-1.0)
                  B.ts1("gpsimd", ue[:, 0:ncol], ue[:, 0:ncol], 1.0, ALU.add, [uer], [uer])
                  B.recip(ue[:, 0:ncol], ue[:, 0:ncol], [uer], [uer])
                  B.tt("gpsimd", ue[:, 0:ncol], ue[:, 0:ncol], uc[:, 0:ncol], ALU.mult, [uer, ucr], [uer])
                  B.tt("vector", actb[:, fc, 0:ncol], ue[:, 0:ncol], Vb[bk][:, 0:ncol], ALU.mult, [uer, "V%d" % bk], ["actb%d" % fc])
              for i, R in enumerate(tiles_rows):
                  hr = "hs%d" % i
                  for nh in range(2):
                      for fc in range(NFC):
                          B.mm(Yb[nh][0:R, :], actb[:, fc, cols[i]:cols[i] + R], wfo[:, fc, nh * 512:(nh + 1) * 512], fc == 0, fc == NFC - 1,
                               ["actb%d" % fc, "wfo"], ["Y%d" % nh])
                      B.tt("vector", y_sb[0:R, nh * 512:(nh + 1) * 512], Yb[nh][0:R, :], hs[i][0:R, nh * 512:(nh + 1) * 512], ALU.add,
                           ["Y%d" % nh, hr], ["y_sb"])
                  B.dma(stores[i], y_sb[0:R, :], ["y_sb"], [], "y_out")

          def conv_seq(fc, Ux, ures, ncol, uc, ucr, ubuf, ubr):
              B.cp("gpsimd", ubuf[:, 0:2], carry[:, fc, :], ["carry"], [ubr])
              B.cp("scalar", ubuf[:, 2:2 + ncol], Ux[:, 0:ncol], [ures], [ubr])
              B.ts("vector", uc[:, 0:ncol], ubuf[:, 2:2 + ncol], wcv[:, fc, 2:3], wcv[:, fc, 3:4], ALU.mult, ALU.add, [ubr, "wcv"], [ucr])
              B.stt("vector", uc[:, 0:ncol], ubuf[:, 1:1 + ncol], wcv[:, fc, 1:2], uc[:, 0:ncol], ALU.mult, ALU.add, [ubr, "wcv", ucr], [ucr])
              B.stt("vector", uc[:, 0:ncol], ubuf[:, 0:ncol], wcv[:, fc, 0:1], uc[:, 0:ncol], ALU.mult, ALU.add, [ubr, "wcv", ucr], [ucr])
              B.cp("gpsimd", carry[:, fc, :], ubuf[:, ncol:ncol + 2], [ubr], ["carry"])

          if NT > 0 and DBG >= 6:
              tpg = G // 128
              for gi in range(NT // tpg):
                  rows = [128] * tpg
                  loads = [y_p[(gi * tpg + i) * 128:(gi * tpg + i + 1) * 128, :] for i in range(tpg)]
                  ffn_group(rows, loads, loads, conv_seq, "seq")
              B.dma(o_conv_p, carry[:, :, :].rearrange("p c j -> p (c j)"), ["carry"], [], "o_conv_p")

          if NB > 0 and DBG >= 7:
              stc = sb("stc", [128, NFC, NB, 2], F32)
              convS = sb("convS", [128, NFC, NB, 2], F32)
              B.dma(stc[:, :, :, :].rearrange("p c b j -> p (c b j)"), st_conv, [], ["stc"], "stc")

              def conv_smp(fc, Ux, ures, ncol, uc, ucr, ubuf, ubr):
                  ub = ubuf[:, 0:NB * 6].rearrange("p (b j) -> p b j", j=6)
                  B.cp("gpsimd", ub[:, :, 0:2], stc[:, fc, :, :], ["stc"], [ubr])
                  B.cp("scalar", ub[:, :, 2:6], Ux[:, 0:RS].rearrange("p (b s) -> p b s", s=4), [ures], [ubr])
                  uc3 = uc[:, 0:RS].rearrange("p (b s) -> p b s", s=4)
                  B.ts("vector", uc3, ub[:, :, 2:6], wcv[:, fc, 2:3], wcv[:, fc, 3:4], ALU.mult, ALU.add, [ubr, "wcv"], [ucr])
                  B.stt("vector", uc3, ub[:, :, 1:5], wcv[:, fc, 1:2], uc3, ALU.mult, ALU.add, [ubr, "wcv", ucr], [ucr])
                  B.stt("vector", uc3, ub[:, :, 0:4], wcv[:, fc, 0:1], uc3, ALU.mult, ALU.add, [ubr, "wcv", ucr], [ucr])
                  B.cp("gpsimd", convS[:, fc, :, :], ub[:, :, 4:6], [ubr], ["convS"])
              ffn_group([RS], [y_s], [y_s], conv_smp, "smp")
              B.dma(o_conv_s, convS[:, :, :, :].rearrange("p c b j -> p (c b j)"), ["convS"], [], "o_conv_s")

          print("pass2 ops", P2.emit(nc, "b"))
    return nc, consts


_CACHE = {}


def _w2h(w2):
    o = np.zeros((128, 256), np.float32)
    for g in range(2):
        o[64 * g:64 * g + 64, 64 * g:64 * g + 64] = w2[0]
        o[64 * g:64 * g + 64, 128 + 64 * g:128 + 64 * g + 64] = w2[1]
    return o


def _run(inputs, S, NB, PAST, NPOOL, n_cores=8):
    key = (S, NB, PAST, NPOOL)
    if key not in _CACHE:
        _CACHE[key] = build(S, NB, PAST, NPOOL)
    nc, consts = _CACHE[key]
    f32 = np.float32
    NPG = PAST // 128
    RS = NB * 4
    A = lambda v: np.ascontiguousarray(np.asarray(v))
    x_prompt, x_sample = A(inputs["x_prompt"]), A(inputs["x_sample"])
    cache_cmp, cache_sel = A(inputs["cache_kv_cmp"])[0], A(inputs["cache_kv_sel"])[0]
    c_pg = np.empty((NPOOL, 128, 512), np.float32)
    c_pg[:, :, 0:256] = cache_cmp.reshape(NPOOL, 128, 2, 128).transpose(0, 3, 2, 1).reshape(NPOOL, 128, 256)
    c_pg[:, :, 256:384] = cache_sel[:, :, 0].reshape(NPOOL, 128, 128).transpose(0, 2, 1)
    c_pg[:, :, 384:512] = cache_sel[:, :, 1].reshape(NPOOL, 128, 128)
    c_pg = c_pg.reshape(NPOOL * 128, 512)
    st_win_all = A(inputs["state_kv_win"])[0]
    st_ret_all = A(inputs["state_ret"])[0]
    st_conv_all = A(inputs["state_conv"])[0]
    pt_all = A(inputs["page_table"]).astype(np.int32)
    shared = {("c_" + k): v for k, v in consts.items()}
    shared.update({
        "c_pg": c_pg,
        "nm_mix": A(inputs["norm_mix"])[0].reshape(8, 128).T.copy(),
        "w_in": A(inputs["w_in"])[0],
        "gq": np.tile(A(inputs["norm_q"])[0][None, :], (128, 1)).astype(f32),
        "gk": np.tile(A(inputs["norm_k"])[0].reshape(1, 192), (128, 1)).astype(f32),
        "w1h": np.tile(A(inputs["w_cmp1"])[0].reshape(2, 32, 64, 64).transpose(2, 0, 1, 3).reshape(64, 4096), (2, 1)).astype(f32),
        "w2h": _w2h(A(inputs["w_cmp2"])[0]),
        "peT": np.tile(A(inputs["pe_cmp"])[0].transpose(2, 0, 1).reshape(64, 64), (2, 1)).astype(f32),
        "w_out": A(inputs["w_out"])[0],
        "nm_ffn": A(inputs["norm_ffn"])[0].reshape(8, 128).T.copy(),
        "w_fi": A(inputs["w_ffn_in"])[0],
        "wconv": np.concatenate([A(inputs["w_conv"])[0], A(inputs["b_conv"])], axis=0).reshape(4, NFC, 128).transpose(2, 1, 0).reshape(128, NFC * 4).copy(),
        "w_fo": A(inputs["w_ffn_out"])[0],
    })
    in_maps = []
    for c in range(n_cores):
        bs = slice(c * NB, (c + 1) * NB)
        sw = st_win_all[bs]
        m = dict(shared)
        m["x_p"] = x_prompt[c]
        m["x_s"] = x_sample[bs].reshape(RS, 1024)
        m["st_winK"] = np.ascontiguousarray(sw[:, :, 0].reshape(NB, 512, 128).transpose(0, 2, 1))
        m["st_winV"] = np.ascontiguousarray(sw[:, :, 1].reshape(NB, 512, 128))
        m["st_win"] = np.ascontiguousarray(sw.reshape(NB, 512, 256))
        m["st_ret"] = np.ascontiguousarray(st_ret_all[bs].transpose(2, 0, 1, 3)).reshape(64, NB * 512)
        m["st_conv"] = np.ascontiguousarray(st_conv_all[bs].reshape(NB, 2, NFC, 128).transpose(3, 2, 0, 1)).reshape(128, NFC * NB * 2)
        m["ptab"] = np.ascontiguousarray(pt_all[bs].reshape(1, NB * NPG))
        in_maps.append(m)
    res = run_bass_kernel_spmd(nc, in_maps, core_ids=list(range(n_cores)))
    R = res.results
    cat = lambda k: np.stack([r[k] for r in R], 0)
    nB = n_cores
    DB = n_cores * NB
    y_prompt = cat("y_p")
    y_sample = cat("y_s").reshape(DB, 4, 1024)
    kvc_p = cat("o_kvc_p").reshape(1, nB, S, 2, 2, 64)
    kvs_p = cat("o_kvs_p").reshape(1, nB, S, 2, 2, 64)
    kvw_p = cat("o_kvw_p").reshape(1, nB, 512, 2, 2, 64)
    kvc_s = cat("o_kvc_s").reshape(1, DB, 4, 2, 2, 64)
    kvs_s = cat("o_kvs_s").reshape(1, DB, 4, 2, 2, 64)
    kvw_s = cat("o_kvw_s").reshape(1, DB, 512, 2, 2, 64)
    ret_p = cat("o_ret_p").reshape(nB, 64, 4, 128).transpose(0, 2, 1, 3).reshape(1, nB, 4, 64, 128)
    ret_s = cat("o_ret_s").reshape(nB, 64, NB, 4, 128).transpose(0, 2, 3, 1, 4).reshape(1, DB, 4, 64, 128)
    conv_p = cat("o_conv_p").reshape(nB, 128, NFC, 2).transpose(0, 3, 2, 1).reshape(1, nB, 2, D_FF)
    conv_s = cat("o_conv_s").reshape(nB, 128, NFC, NB, 2).transpose(0, 3, 4, 2, 1).reshape(1, DB, 2, D_FF)
    outs = (y_prompt, y_sample, kvc_p, kvc_s, kvs_p, kvs_s, kvw_p, kvw_s, ret_p, ret_s, conv_p, conv_s)
    return tuple(np.ascontiguousarray(o, dtype=np.float32) for o in outs)


def kernel(**inputs):
    S = int(np.asarray(inputs["x_prompt"]).shape[1])
    DB = int(np.asarray(inputs["x_sample"]).shape[0])
    NPG = int(np.asarray(inputs["page_table"]).shape[1])
    NPOOL = int(np.asarray(inputs["cache_kv_cmp"]).shape[1])
    return _run(inputs, S, DB // 8, NPG * 128, NPOOL)
```

```python
import math
from contextlib import ExitStack
import numpy as np
import ml_dtypes
import concourse.bass as bass
import concourse.mybir as mybir
from concourse.bass_utils import run_bass_kernel_spmd

F32 = mybir.dt.float32
BF16 = mybir.dt.bfloat16
I32 = mybir.dt.int32
AF = mybir.ActivationFunctionType
ALU = mybir.AluOpType
AX = mybir.AxisListType
NPBF = ml_dtypes.bfloat16

import os
POOL_ENG = os.environ.get('K_POOL', 'vector')
SKIP = os.environ.get('K_SKIP', '')
PSTOP = int(os.environ.get('K_PSTOP', '99'))
PSTOP2 = int(os.environ.get('K_PSTOP2', '99'))
ENGS = ("tensor", "vector", "scalar", "gpsimd", "sync")
NEGM = -32768.0
EPS = 1e-6
D_IN = 2840
D_FF = 2816
NFC = 22


class Op:
    __slots__ = ("eng", "fn", "deps", "signal", "semkey", "idx", "count", "isdma")

    def __init__(self, eng, fn, isdma, semkey):
        self.eng = eng
        self.fn = fn
        self.deps = []
        self.signal = False
        self.semkey = semkey
        self.isdma = isdma
        self.count = None


class Prog:
    def __init__(self):
        self.ops = []
        self.last_w = {}
        self.readers = {}

    EXCL = ("S0", "S1", "A0", "A1", "A2", "T0", "M0", "M1", "U0", "U1", "V0", "V1", "T0b", "Y0", "Y1")

    def add(self, eng, fn, reads=(), writes=(), dma=None):
        xr = [r for r in reads if r in self.EXCL]
        if xr:
            writes = list(writes) + [r for r in xr if r not in writes]
        op = Op(eng, fn, dma is not None, ("dma", dma) if dma is not None else ("eng", eng))
        deps = set()
        for r in reads:
            w = self.last_w.get(r)
            if w is not None:
                deps.add(w)
        for r in writes:
            w = self.last_w.get(r)
            if w is not None:
                deps.add(w)
            for rd in self.readers.get(r, ()):
                deps.add(rd)
        if eng == "tensor":
            deps = {d for d in deps if not (d.eng == "tensor" and not d.isdma)}
        op.deps = list(deps)
        for d in op.deps:
            d.signal = True
        for r in reads:
            self.readers.setdefault(r, []).append(op)
        for r in writes:
            self.last_w[r] = op
            self.readers[r] = []
        op.idx = len(self.ops)
        self.ops.append(op)
        return op

    def emit(self, nc, tag):
        for op in self.ops:
            if op.isdma:
                op.signal = True
        lastop = {}
        for op in self.ops:
            if not op.isdma:
                lastop[op.eng] = op
        for op in lastop.values():
            op.signal = True
        counts = {}
        for op in self.ops:
            if op.signal:
                inc = 16 if op.isdma else 1
                counts[op.semkey] = counts.get(op.semkey, 0) + inc
                op.count = counts[op.semkey]
        keys = list(counts.keys())
        with ExitStack() as es:
            sems = {}
            for i, k in enumerate(keys):
                sems[k] = es.enter_context(nc.semaphore("%s%d" % (tag, i)))
            block = es.enter_context(nc.Block())
            per_eng = {e: [op for op in self.ops if op.eng == e] for e in ENGS}

            def make(ename):
                ops = per_eng[ename]

                def body(eng):
                    waited = {}
                    for op in ops:
                        need = {}
                        for d in op.deps:
                            if need.get(d.semkey, 0) < d.count:
                                need[d.semkey] = d.count
                        for k, v in need.items():
                            if waited.get(k, 0) < v:
                                eng.wait_ge(sems[k], v)
                                waited[k] = v
                        ins = op.fn(eng)
                        if op.signal:
                            ins.then_inc(sems[op.semkey], 16 if op.isdma else 1)
                    for k, v in counts.items():
                        if waited.get(k, 0) < v:
                            eng.wait_ge(sems[k], v)
                return body

            for ename in ENGS:
                getattr(block, ename)(make(ename))
        return len(self.ops), len(keys)


class Bld:
    def __init__(self, nc, P, es):
        self.nc, self.P, self.es = nc, P, es
        self.rr = 0

    def sb(self, name, shape, dt):
        return self.es.enter_context(self.nc.sbuf_tensor(name, shape, dt))

    def ps(self, name, shape, dt):
        return self.es.enter_context(self.nc.psum_tensor(name, shape, dt))

    def dma(self, out, in_, r, w, key, eng="sync"):
        self.P.add(eng, lambda e: e.dma_start(out=out, in_=in_), r, w, dma=key)

    def mm(self, out, lhsT, rhs, start, stop, r, w, nocheck=False):
        if nocheck:
            self.P.add("tensor", lambda e: e.matmul(out, lhsT=lhsT, rhs=rhs, start=start, stop=stop, skip_group_check=True), r, w)
        else:
            self.P.add("tensor", lambda e: e.matmul(out, lhsT=lhsT, rhs=rhs, start=start, stop=stop), r, w)

    def tr(self, out, in_, ident, r, w):
        self.P.add("tensor", lambda e: e.transpose(out=out, in_=in_, identity=ident), r, w)

    def act(self, out, in_, func, r, w, bias=None, scale=None, accum=None):
        kw = {}
        if bias is not None:
            kw["bias"] = bias
        if scale is not None:
            kw["scale"] = scale
        if accum is not None:
            kw["accum_out"] = accum
        self.P.add("scalar", lambda e: e.activation(out=out, in_=in_, func=func, **kw), r, w)

    def _e(self, eng):
        return POOL_ENG if eng == "gpsimd" else eng

    def tt(self, eng, out, in0, in1, op, r, w):
        eng = self._e(eng)
        self.P.add(eng, lambda e: e.tensor_tensor(out=out, in0=in0, in1=in1, op=op), r, w)

    def ts(self, eng, out, in0, s1, s2, op0, op1, r, w):
        eng = self._e(eng)
        self.P.add(eng, lambda e: e.tensor_scalar(out=out, in0=in0, scalar1=s1, scalar2=s2, op0=op0, op1=op1), r, w)

    def ts1(self, eng, out, in_, s, op, r, w):
        eng = self._e(eng)
        self.P.add(eng, lambda e: e.tensor_single_scalar(out=out, in_=in_, scalar=s, op=op), r, w)

    def stt(self, eng, out, in0, scalar, in1, op0, op1, r, w):
        eng = self._e(eng)
        self.P.add(eng, lambda e: e.scalar_tensor_tensor(out=out, in0=in0, scalar=scalar, in1=in1, op0=op0, op1=op1), r, w)

    def cp(self, eng, out, in_, r, w):
        eng = self._e(eng)
        if eng == "scalar":
            self.P.add(eng, lambda e: e.copy(out=out, in_=in_), r, w)
        else:
            self.P.add(eng, lambda e: e.tensor_copy(out=out, in_=in_), r, w)

    def red(self, out, in_, r, w):
        self.P.add("vector", lambda e: e.reduce_sum(out=out, in_=in_, axis=AX.X), r, w)

    def recip(self, out, in_, r, w):
        self.P.add("vector", lambda e: e.reciprocal(out=out, in_=in_), r, w)

    def memset(self, eng, ap, val, w):
        eng = self._e(eng)
        self.P.add(eng, lambda e: e.memset(ap, val), (), w)

    def rsqrt(self, out, in_, scale, r, w):
        self.act(out, in_, AF.Ln, r, w, bias=self.eps_ap[0:out.shape[0], :], scale=scale)
        self.act(out, out, AF.Exp, w, w, scale=-0.5)


def _gammas():
    return 1.0 - 2.0 ** (-5.0 - np.arange(4, dtype=np.float64))


def host_consts(S, NB, PAST):
    RS = NB * 4
    c = {}
    c["ident"] = np.eye(128, dtype=np.float32).astype(NPBF)
    c["identf"] = np.eye(128, dtype=np.float32)

    def ropeN(pos):
        inv = 500000.0 ** (-np.arange(8, dtype=np.float32) / 8)
        ang = pos.astype(np.float32)[:, None] * inv[None, :]
        cs, sn = np.cos(ang), np.sin(ang)
        return np.concatenate([cs, cs, -sn, sn], axis=1).astype(np.float32)

    def ropeR(pos):
        inv = 10000.0 ** (-np.arange(32, dtype=np.float32) / 32)
        ang = pos.astype(np.float32)[:, None] * inv[None, :]
        cs, sn = np.cos(ang), np.sin(ang)
        return np.concatenate([cs, cs, -sn, sn], axis=1).astype(np.float32)

    pos_p = np.arange(S)
    pos_s = PAST + (np.arange(RS) % 4)
    c["ropeN_p"], c["ropeR_p"] = ropeN(pos_p), ropeR(pos_p)
    c["ropeN_s"], c["ropeR_s"] = ropeN(pos_s), ropeR(pos_s)
    k = np.arange(128)[:, None]
    q = np.arange(128)[None, :]
    causal = np.where(k <= q, 0.0, NEGM).astype(np.float32)
    winfirst = np.where(k > q, 0.0, NEGM).astype(np.float32)
    c["causal4"] = np.tile(causal, (1, 4)).astype(NPBF)
    c["winfirst4"] = np.tile(winfirst, (1, 4)).astype(NPBF)
    kk = np.arange(8)[:, None]
    base = np.where(q >= 16 * kk + 15, 0.0, NEGM).astype(np.float32)
    c["cbase4"] = np.tile(base, (1, 4)).astype(NPBF)
    sh = np.zeros((8, 272), np.float32)
    for i in range(8):
        sh[i, i + 128] = 1.0
    c["shid"] = sh.astype(NPBF)
    E = np.zeros((64, 4096), np.float32)
    for j in range(64):
        E[j, 64 * j:64 * j + 64] = 1.0
    c["E"] = E.astype(NPBF)
    ov = np.zeros((128, 4, 128), np.float32)
    for cc in range(512):
        for j in range(128):
            lo, hi = max(16 * cc, 64 * j), min(16 * cc + 32, 64 * j + 64)
            if hi > lo:
                ov[cc % 128, cc // 128, j] = (hi - lo) / 32.0
    c["ov"] = ov.reshape(128, 512).astype(NPBF)
    g = _gammas()
    i_ = np.arange(128)
    dec = np.zeros((128, 4, 128), np.float64)
    for h in range(4):
        d = i_[None, :] - i_[:, None]
        dec[:, h, :] = np.where(d >= 0, g[h] ** np.maximum(d, 0), 0.0) / 8.0
    c["decT"] = dec.reshape(128, 512).astype(np.float32)
    cross = np.zeros((64, 4, 128), np.float64)
    for h in range(4):
        cross[:, h, :] = (g[h] ** (i_ + 1.0))[None, :]
    c["crossT"] = cross.reshape(64, 512).astype(np.float32)
    c["wtab"] = np.stack([g[h] ** (127.0 - i_) / 8.0 for h in range(4)], axis=1).astype(np.float32)
    pat = np.zeros((128, 3), np.float32)
    pat[:64, 0] = 1e4
    pat[:, 1] = 1e4
    pat[64:, 2] = 1e4
    c["pat"] = pat
    c["iota"] = np.arange(128, dtype=np.float32).reshape(128, 1)
    s4 = np.arange(4)
    scaus = np.where(s4[:, None] <= s4[None, :], 0.0, NEGM).astype(np.float32)
    c["scausal4"] = np.tile(scaus, (1, 4)).astype(NPBF)
    swf = np.zeros((128, 4), np.float32)
    for w in range(4):
        for s in range(4):
            swf[w, s] = 0.0 if w > s else NEGM
    c["swinfirst4"] = np.tile(swf, (1, 4)).astype(NPBF)
    dec4 = np.zeros((4, 4, 4), np.float64)
    for h in range(4):
        d = s4[None, :] - s4[:, None]
        dec4[:, h, :] = np.where(d >= 0, g[h] ** np.maximum(d, 0), 0.0) / 8.0
    c["dec4T"] = dec4.reshape(4, 16).astype(np.float32)
    cross4 = np.zeros((64, 4, 4), np.float64)
    for h in range(4):
        cross4[:, h, :] = (g[h] ** (s4 + 1.0))[None, :]
    c["cross4T"] = cross4.reshape(64, 16).astype(np.float32)
    c["w4tab"] = np.stack([g[h] ** (3.0 - s4) / 8.0 for h in range(4)], axis=1).astype(np.float32)
    spat = np.zeros((4, 2), np.float32)
    spat[:, 0] = 1e4
    spat[:, 1] = 1e4
    c["spat"] = spat
    return c


CONST_SHAPES = None


import os
DBG = int(os.environ.get('K_DBG', '9'))


def build(S, NB, PAST, NPOOL, do_sample=True):
    NT = S // 128
    NPG = PAST // 128
    RS = NB * 4
    NCS = PAST // 16 - 1
    nc = bass.Bass("TRN2", target_bir_lowering=False)
    consts = host_consts(S, NB, PAST)

    def din(name, shape, dt=F32):
        return nc.dram_tensor(name, list(shape), dt, kind="ExternalInput").ap()

    def dout(name, shape):
        return nc.dram_tensor(name, list(shape), F32, kind="ExternalOutput").ap()

    cd = {}
    for k, v in consts.items():
        cd[k] = din("c_" + k, v.shape, BF16 if v.dtype == NPBF else F32)
    x_p = din("x_p", [S, 1024])
    x_s = din("x_s", [RS, 1024])
    c_pg = din("c_pg", [NPOOL * 128, 512])
    st_winK = din("st_winK", [NB, 128, 512])
    st_winV = din("st_winV", [NB, 512, 128])
    st_win = din("st_win", [NB, 512, 256])
    st_ret = din("st_ret", [64, NB * 512])
    st_conv = din("st_conv", [128, NFC * NB * 2])
    ptab = din("ptab", [1, NB * NPG], I32)
    nm_mix = din("nm_mix", [128, 8])
    w_in = din("w_in", [1024, D_IN])
    gq = din("gq", [128, 64])
    gk = din("gk", [128, 192])
    w1h = din("w1h", [128, 4096])
    w2h = din("w2h", [128, 128 + 128])
    peT = din("peT", [128, 64])
    w_out = din("w_out", [1024, 1024])
    nm_ffn = din("nm_ffn", [128, 8])
    w_fi = din("w_fi", [1024, 2 * D_FF])
    wconv = din("wconv", [128, NFC * 4])
    w_fo = din("w_fo", [D_FF, 1024])

    y_p = dout("y_p", [S, 1024])
    y_s = dout("y_s", [RS, 1024])
    o_kvc_p = dout("o_kvc_p", [S, 256])
    o_kvs_p = dout("o_kvs_p", [S, 256])
    o_kvw_p = dout("o_kvw_p", [512, 256])
    o_kvc_s = dout("o_kvc_s", [RS, 256])
    o_kvs_s = dout("o_kvs_s", [RS, 256])
    o_kvw_s = dout("o_kvw_s", [NB, 512, 256])
    o_ret_p = dout("o_ret_p", [64, 512])
    o_ret_s = dout("o_ret_s", [64, NB * 512])
    o_conv_p = dout("o_conv_p", [128, NFC * 2])
    o_conv_s = dout("o_conv_s", [128, NFC * NB * 2])

    P = Prog()
    with ExitStack() as es:
        B = Bld(nc, P, es)
        sb, ps = B.sb, B.ps
        Sb = [ps("S0", [128, 512], F32), ps("S1", [128, 512], F32)]
        Ab = [ps("A0", [128, 512], F32), ps("A1", [128, 512], F32), ps("A2", [128, 512], F32)]
        T0 = ps("T0", [128, 1024], BF16)
        Mb = [ps("M0", [128, 512], F32), ps("M1", [128, 512], F32)]
        ct = {}
        for k, v in consts.items():
            if k in ("ropeN_p", "ropeR_p", "ropeN_s", "ropeR_s", "identf"):
                continue
            ct[k] = sb("k_" + k, list(v.shape), BF16 if v.dtype == NPBF else F32)
            B.dma(ct[k][:], cd[k], [], ["k_" + k], "k_" + k)
        ident = ct["ident"]
        epsc = sb("epsc", [128, 1], F32)
        B.memset("vector", epsc[:], EPS, ["epsc"])
        B.eps_ap = epsc
        w_in_bf = sb("w_in_bf", [128, 8, D_IN], BF16)
        w_out_bf = sb("w_out_bf", [128, 8, 1024], BF16)
        W1 = sb("W1", [128, 2, 32, 64], BF16)
        w2 = sb("w2", [128, 256], BF16)
        peT_bf = sb("peT_bf", [128, 64], BF16)
        peterm = sb("peterm", [128, 2], F32)
        nm_sb = sb("nm_sb", [128, 8], F32)
        gq_sb = sb("gq_sb", [128, 64], F32)
        gk_sb = sb("gk_sb", [128, 192], F32)
        stage = [sb("stage0", [128, 1024], F32), sb("stage1", [128, 1024], F32)]
        B.dma(nm_sb[:], nm_mix, [], ["nm_sb"], "nm_sb")
        B.dma(gq_sb[:], gq, [], ["gq_sb"], "gq_sb")
        B.dma(gk_sb[:], gk, [], ["gk_sb"], "gk_sb")
        B.ts1("vector", gq_sb[:], gq_sb[:], 0.125, ALU.mult, ["gq_sb"], ["gq_sb"])
        si = 0
        for k in range(8):
            for c0 in range(0, D_IN, 1024):
                s_ = si % 2
                c1 = min(D_IN, c0 + 1024)
                B.dma(stage[s_][:, 0:c1 - c0], w_in[k * 128:(k + 1) * 128, c0:c1], [], ["stage%d" % s_], "stage%d" % s_)
                eng = "vector" if si % 2 == 0 else "gpsimd"
                B.ts1(eng, w_in_bf[:, k, c0:c1], stage[s_][:, 0:c1 - c0], nm_sb[:, k:k + 1], ALU.mult,
                      ["stage%d" % s_, "nm_sb"], ["w_in_bf"])
                si += 1
        for k in range(8):
            s_ = si % 2
            B.dma(stage[s_][:, 0:1024], w_out[k * 128:(k + 1) * 128, :], [], ["stage%d" % s_], "stage%d" % s_)
            B.cp("vector" if k % 2 == 0 else "gpsimd", w_out_bf[:, k, :], stage[s_][:, 0:1024], ["stage%d" % s_], ["w_out_bf"])
            si += 1
        for k in range(4):
            s_ = si % 2
            B.dma(stage[s_][:, 0:1024], w1h[:, k * 1024:(k + 1) * 1024], [], ["stage%d" % s_], "stage%d" % s_)
            B.cp("vector", W1[:, k // 2, (k % 2) * 16:(k % 2) * 16 + 16, :].rearrange("p l o -> p (l o)"), stage[s_][:, 0:1024],
                 ["stage%d" % s_], ["W1"])
            si += 1
        s_ = si % 2
        B.dma(stage[s_][:, 0:256], w2h, [], ["stage%d" % s_], "stage%d" % s_)
        B.cp("vector", w2[:, :], stage[s_][:, 0:256], ["stage%d" % s_], ["w2"])
        si += 1
        s_ = si % 2
        B.dma(stage[s_][:, 0:64], peT, [], ["stage%d" % s_], "stage%d" % s_)
        B.cp("vector", peT_bf[:], stage[s_][:, 0:64], ["stage%d" % s_], ["peT_bf"])
        si += 1
        for hp in range(0 if 'pe' in SKIP else 2):
            for k in range(2):
                for l in range(32):
                    B.mm(Mb[0][64 * hp:64 * hp + 64, k:k + 1], W1[64 * hp:64 * hp + 64, k, l, :],
                         peT_bf[64 * hp:64 * hp + 64, k * 32 + l:k * 32 + l + 1], l == 0, l == 31,
                         ["W1", "peT_bf"], ["M0"])
        if "pe" not in SKIP:
            B.cp("vector", peterm[:], Mb[0][:, 0:2], ["M0"], ["peterm"])

        LK = max(S, PAST)
        ksT = sb("ksT", [128, LK + 4], BF16)
        vs = sb("vs", [128, LK // 128 + 1, 2, 65], BF16)
        kwT = sb("kwT", [128, 8, 128], BF16)
        vw = sb("vw", [128, 8, 2, 65], BF16)
        kcT = sb("kcT", [128, 512], BF16)
        hgT = sb("hgT", [128, 2, 512], BF16)
        vc = sb("vc", [128, 4, 2, 65], BF16)
        cmpT = sb("cmpT", [128, 2, 16 + 1024], BF16)
        if "ms" not in SKIP:
            B.memset("vector", vs[:, :, :, 64:65], 1.0, ["vs"])
            B.memset("vector", vw[:, :, :, 64:65], 1.0, ["vw"])
            B.memset("vector", vc[:, :, :, 64:65], 1.0, ["vc"])
        Sf = sb("Sf", [64, 4, 128], F32)
        Sbf = sb("Sbf", [64, 4, 128], BF16)
        xs_ = [sb("xs0", [128, 1024], F32)]
        ropeN = sb("ropeN", [128, 32], F32)
        ropeR = sb("ropeR", [128, 128], F32)
        st1 = sb("st1", [128, 16], F32)
        junk = sb("junk", [128, 1024], BF16)
        xn = sb("xn", [128, 1024], BF16)
        xnT = sb("xnT", [128, 8, 128], BF16)
        qn = sb("qn", [128, 512], F32)
        q_bf = sb("q_bf", [128, 8, 128], BF16)
        B.memset("vector", q_bf[:], 0.0, ["q_bf"])
        qT = sb("qT", [128, 8, 128], BF16)
        kvo = sb("kvo", [128, 768], F32)
        k_bf = sb("k_bf", [128, 3, 128], BF16)
        cv_bf = sb("cv_bf", [128, 128], BF16)
        rtmp = sb("rtmp", [128, 256], F32)
        rtmp2 = sb("rtmp2", [128, 256], F32)
        rqk_bf = sb("rqk_bf", [128, 512], BF16)
        rqT = sb("rqT", [64, 4, 128], BF16)
        rkT = sb("rkT", [64, 4, 128], BF16)
        rv_bf = sb("rv_bf", [128, 512], BF16)
        rg_f = sb("rg_f", [128, 512], F32)
        oret = sb("oret", [128, 512], F32)
        rg_e = oret

        gates = sb("gates", [128, 24], F32)
        ebuf = [sb("e0", [128, 512], BF16), sb("e1", [128, 512], BF16)]
        hx = sb("hx", [128, 32], F32)
        hg_bf = sb("hg_bf", [128, 2, 128], BF16)
        hwork = sb("hwork", [128, 3, 256], F32)
        ocmp = sb("ocmp", [128, 4, 64], F32)
        osel = sb("osel", [128, 4, 64], F32)
        owin = sb("owin", [128, 4, 64], F32)
        rs_c = sb("rs_c", [128, 12], F32)
        imp = sb("imp", [128, 128], F32)
        imp2 = sb("imp2", [128, 128], F32)
        mx = sb("mx", [128, 16], F32)
        negq = sb("negq", [128, 128], BF16)
        negT4 = sb("negT4", [64, 2, 512], BF16)
        mixed = sb("mixed", [128, 1024], BF16)
        mixT = xnT
        sTm = sb("sTm", [128, 512], BF16)
        qcr = sb("qcr", [64, 4, 128], BF16)
        kwk = sb("kwk", [128, 256], BF16)


        B.memset("vector", Sf[:], 0.0, ["Sf"])
        B.memset("vector", Sbf[:], 0.0, ["Sbf"])
        B.memset("vector", cmpT[:, :, 0:16], 0.0, ["cmpT"])

        ecnt = [0]
        scnt = [0]

        def project(R, x_dram, xslot, ropeN_d, ropeR_d, kv_douts, tg):
            xr = "xs%d" % xslot
            x_sb = xs_[xslot]
            B.dma(x_sb[0:R, :], x_dram, [], [xr], xr)
            B.dma(ropeN[0:R, :], ropeN_d, [], ["ropeN"], "ropeN")
            B.dma(ropeR[0:R, :], ropeR_d, [], ["ropeR"], "ropeR")
            B.act(junk[0:R, :], x_sb[0:R, :], AF.Square, [xr], ["junk", "st1"], accum=st1[0:R, 0:1])
            B.rsqrt(st1[0:R, 1:2], st1[0:R, 0:1], 1.0 / 1024, ["st1"], ["st1"])
            if PSTOP <= 0:
                return
            B.act(xn[0:R, :], x_sb[0:R, :], AF.Identity, [xr, "st1"], ["xn"], scale=st1[0:R, 1:2])
            if PSTOP <= 1:
                return
            for k in range(8):
                B.tr(T0[:, k * 128:k * 128 + R], xn[0:R, k * 128:(k + 1) * 128], ident[0:R, 0:R], ["xn", "k_ident"], ["T0"])
            B.cp("vector", xnT[:, :, 0:R], T0[:, :].rearrange("p (k t) -> p k t", k=8)[:, :, 0:R], ["T0"], ["xnT"])
            if PSTOP <= 2:
                return
            def zchunk(n, bank):
                c0 = n * 512
                w = min(512, D_IN - c0)
                for k in range(8):
                    B.mm(Mb[bank][0:R, 0:w], xnT[:, k, 0:R], w_in_bf[:, k, c0:c0 + w], k == 0, k == 7,
                         ["xnT", "w_in_bf"], ["M%d" % bank])
                return w
            zchunk(0, 0)
            zchunk(1, 1)
            zq = Mb[0]
            if PSTOP <= 3:
                return
            B.act(junk[0:R, 0:512], zq[0:R, :], AF.Square, ["M0"], ["junk"])
            B.red(st1[0:R, 2:10], junk[0:R, 0:512].rearrange("p (h d) -> p h d", h=8), ["junk"], ["st1"])
            B.rsqrt(st1[0:R, 2:10], st1[0:R, 2:10], 1.0 / 64, ["st1"], ["st1"])
            qn3 = qn[0:R, :].rearrange("p (h d) -> p h d", h=8)
            B.tt("vector", qn3, zq[0:R, :].rearrange("p (h d) -> p h d", h=8),
                 st1[0:R, 2:10].unsqueeze(2).broadcast_to([R, 8, 64]), ALU.mult, ["M0", "st1"], ["qn"])
            B.tt("gpsimd", qn3, qn3, gq_sb[0:R, :].unsqueeze(1).broadcast_to([R, 8, 64]), ALU.mult, ["qn", "gq_sb"], ["qn"])
            if PSTOP <= 4:
                return
            rope16(R, qn3, 8, "qn")
            if PSTOP <= 5:
                return
            for g_ in range(2):
                B.cp("scalar", q_bf[0:R, 4 * g_:4 * g_ + 4, 64 * g_:64 * g_ + 64], qn3[:, 4 * g_:4 * g_ + 4, :], ["qn"], ["q_bf"])
            B.cp("scalar", kvo[0:R, 0:512], Mb[1][0:R, :], ["M1"], ["kvo"])
            if PSTOP <= 6:
                return
            zchunk(2, 0)
            B.cp("scalar", kvo[0:R, 512:768], Mb[0][0:R, 0:256], ["M0"], ["kvo"])
            ropeR64(R, Mb[0][0:R, 256:512], rqk_bf[0:R, 0:256], "M0")
            if PSTOP <= 7:
                return
            zchunk(3, 1)
            if "c3r" not in SKIP:
                ropeR64(R, Mb[1][0:R, 0:256], rqk_bf[0:R, 256:512], "M1")
            if "c3a" not in SKIP:
                B.cp("scalar", rv_bf[0:R, 0:256], Mb[1][0:R, 256:512], ["M1"], ["rv_bf"])
            if PSTOP2 <= 1:
                return
            zchunk(4, 0)
            B.cp("scalar", rv_bf[0:R, 256:512], Mb[0][0:R, 0:256], ["M0"], ["rv_bf"])
            B.cp("vector", rg_f[0:R, 0:256], Mb[0][0:R, 256:512], ["M0"], ["rg_f"])
            B.act(rg_e[0:R, 0:256], Mb[0][0:R, 256:512], AF.Exp, ["M0"], ["oret"], scale=-1.0)
            if PSTOP2 <= 2:
                return
            zchunk(5, 1)
            B.cp("vector", rg_f[0:R, 256:512], Mb[1][0:R, 0:256], ["M1"], ["rg_f"])
            B.act(rg_e[0:R, 256:512], Mb[1][0:R, 0:256], AF.Exp, ["M1"], ["oret"], scale=-1.0)
            B.act(gates[0:R, :], Mb[1][0:R, 256:280], AF.Exp, ["M1"], ["gates"], scale=-1.0)
            B.ts1("vector", gates[0:R, :], gates[0:R, :], 1.0, ALU.add, ["gates"], ["gates"])
            B.recip(gates[0:R, :], gates[0:R, :], ["gates"], ["gates"])
            if PSTOP2 <= 3:
                return
            B.ts1("gpsimd", rg_e[0:R, :], rg_e[0:R, :], 1.0, ALU.add, ["oret"], ["oret"])
            B.recip(rg_e[0:R, :], rg_e[0:R, :], ["oret"], ["oret"])
            B.tt("gpsimd", rg_f[0:R, :], rg_f[0:R, :], rg_e[0:R, :], ALU.mult, ["rg_f", "oret"], ["rg_f"])
            if PSTOP <= 8:
                return
            k4 = kvo[0:R, :].rearrange("p (b v g d) -> p b v g d", b=3, v=2, g=2)[:, :, 0, :, :]
            B.act(junk[0:R, 0:384].rearrange("p (b g d) -> p b g d", b=3, g=2), k4, AF.Square, ["kvo"], ["junk"])
            B.red(st1[0:R, 10:16], junk[0:R, 0:384].rearrange("p (h d) -> p h d", h=6), ["junk"], ["st1"])
            B.rsqrt(st1[0:R, 10:16], st1[0:R, 10:16], 1.0 / 64, ["st1"], ["st1"])
            B.tt("vector", k4, k4, st1[0:R, 10:16].rearrange("p (b g) -> p b g", b=3).unsqueeze(3).broadcast_to([R, 3, 2, 64]),
                 ALU.mult, ["kvo", "st1"], ["kvo"])
            B.tt("gpsimd", k4, k4, gk_sb[0:R, :].rearrange("p (b d) -> p b d", b=3).unsqueeze(2).broadcast_to([R, 3, 2, 64]),
                 ALU.mult, ["kvo", "gk_sb"], ["kvo"])
            for b_ in range(3):
                rope16(R, k4[:, b_, :, :], 2, "kvo")
            if PSTOP <= 9:
                return
            for i_, d_ in enumerate(kv_douts):
                if d_ is not None:
                    B.dma(d_, kvo[0:R, i_ * 256:(i_ + 1) * 256], ["kvo"], [], "kvo_out")
            B.cp("vector", k_bf[0:R, :, :].rearrange("p b (g d) -> p b g d", g=2), k4, ["kvo"], ["k_bf"])
            B.cp("gpsimd", cv_bf[0:R, :], kvo[0:R, 128:256], ["kvo"], ["cv_bf"])

        def rope16(R, x3, H, res):
            C = ropeN[0:R, 0:16].unsqueeze(1).broadcast_to([R, H, 16])
            nsn = ropeN[0:R, 16:24].unsqueeze(1).broadcast_to([R, H, 8])
            psn = ropeN[0:R, 24:32].unsqueeze(1).broadcast_to([R, H, 8])
            t1 = rtmp[0:R, 0:H * 16].rearrange("p (h d) -> p h d", h=H)
            t2 = rtmp2[0:R, 0:H * 16].rearrange("p (h d) -> p h d", h=H)
            B.tt("vector", t1, x3[:, :, 0:16], C, ALU.mult, [res, "ropeN"], ["rtmp"])
            B.tt("gpsimd", t2[:, :, 0:8], x3[:, :, 8:16], nsn, ALU.mult, [res, "ropeN"], ["rtmp2"])
            B.tt("gpsimd", t2[:, :, 8:16], x3[:, :, 0:8], psn, ALU.mult, [res, "ropeN"], ["rtmp2"])
            B.tt("vector", x3[:, :, 0:16], t1, t2, ALU.add, ["rtmp", "rtmp2"], [res])

        def ropeR64(R, zsrc, dst_bf, res):
            H = 4
            z3 = zsrc.rearrange("p (h d) -> p h d", h=H)
            C = ropeR[0:R, 0:64].unsqueeze(1).broadcast_to([R, H, 64])
            nsn = ropeR[0:R, 64:96].unsqueeze(1).broadcast_to([R, H, 32])
            psn = ropeR[0:R, 96:128].unsqueeze(1).broadcast_to([R, H, 32])
            t1 = rtmp[0:R, 0:256].rearrange("p (h d) -> p h d", h=H)
            t2 = rtmp2[0:R, 0:256].rearrange("p (h d) -> p h d", h=H)
            B.tt("vector", t1, z3, C, ALU.mult, [res, "ropeR"], ["rtmp"])
            B.tt("vector", t2[:, :, 0:32], z3[:, :, 32:64], nsn, ALU.mult, [res, "ropeR"], ["rtmp2"])
            B.tt("vector", t2[:, :, 32:64], z3[:, :, 0:32], psn, ALU.mult, [res, "ropeR"], ["rtmp2"])
            B.tt("gpsimd", dst_bf.rearrange("p (h d) -> p h d", h=H), t1, t2, ALU.add, ["rtmp", "rtmp2"], ["rqk_bf"])

        def compress(ncols, first_c, skip_first):
            n0 = 1 if skip_first else 0
            nb = ncols - n0
            c0 = first_c + n0
            for kv in range(2):
                for g in range(2):
                    i = 0
                    for a in range(2):
                        for s in range(16):
                            l = 16 * a + s
                            off = 16 * (n0 + a) + s
                            rhs = cmpT[64 * g:64 * g + 64, kv, off:off + 16 * (nb - 1) + 1:16]
                            B.mm(Ab[2][64 * g:64 * g + 64, kv * 128:kv * 128 + nb], W1[64 * g:64 * g + 64, kv, l, :], rhs,
                                 i == 0, i == 31, ["W1", "cmpT"], ["A2"])
                            i += 1
            if 'cg' in SKIP:
                return
            hw = hwork
            for kv in range(2):
                B.act(hw[:, 0, kv * 128:kv * 128 + nb], Ab[2][:, kv * 128:kv * 128 + nb], AF.Identity, ["A2", "peterm"], ["hwork"],
                      bias=peterm[:, kv:kv + 1])
            xx = hw[:, 0, :].rearrange("p (k c) -> p k c", k=2)[:, :, 0:nb]
            t_ = hw[:, 1, :].rearrange("p (k c) -> p k c", k=2)[:, :, 0:nb]
            u_ = hw[:, 2, :].rearrange("p (k c) -> p k c", k=2)[:, :, 0:nb]
            B.tt("vector", t_, xx, xx, ALU.mult, ["hwork"], ["hwork"])
            B.ts("vector", t_, t_, 0.044715, 1.0, ALU.mult, ALU.add, ["hwork"], ["hwork"])
            B.tt("vector", t_, t_, xx, ALU.mult, ["hwork"], ["hwork"])
            B.act(u_, t_, AF.Exp, ["hwork"], ["hwork"], scale=-2.0 * math.sqrt(2.0 / math.pi))
            B.ts1("vector", u_, u_, 1.0, ALU.add, ["hwork"], ["hwork"])
            B.recip(u_, u_, ["hwork"], ["hwork"])
            B.tt("vector", hgT[:, :, c0:c0 + nb], xx, u_, ALU.mult, ["hwork"], ["hgT"])
            if 'ck' in SKIP:
                return
            B.mm(Ab[2][:, 256:256 + nb], w2[:, 0:128], hgT[:, 0, c0:c0 + nb], True, True, ["w2", "hgT"], ["A2"])
            B.cp("vector", kcT[:, c0:c0 + nb], Ab[2][:, 256:256 + nb], ["A2"], ["kcT"])
            if 'cv' in SKIP:
                return
            c_last = c0 + nb - 1
            for ctile in range(c0 // 128, c_last // 128 + 1):
                M = min(128, c_last + 1 - 128 * ctile)
                B.mm(Ab[2][0:M, 0:128], hgT[:, 1, 128 * ctile:128 * ctile + M], w2[:, 128:256], True, True, ["w2", "hgT"], ["A2"])
                B.cp("vector", vc[0:M, ctile, :, 0:64], Ab[2][0:M, 0:128].rearrange("p (g d) -> p g d", g=2), ["A2"], ["vc"])

        def emask(kt, QN):
            return (ct["E"][:, (kt % 32) * 128:(kt % 32 + 1) * 128], negT4[:, kt // 32, 0:4 * QN], ["k_E", "negT4"])

        def branch(QN, qrhs, qres, tiles, acc, accres, W, tagU=None):
            n = len(tiles)
            NQ = 4 * QN
            CH = max(1, 512 // NQ)
            first = True
            for c0 in range(0, n, CH):
                chunk = tiles[c0:c0 + CH]
                sbk = scnt[0] % 2
                scnt[0] += 1
                Sx = Sb[sbk]
                sres = "S%d" % sbk
                Mmax = max(tl["M"] for tl in chunk)
                for j, tl in enumerate(chunk):
                    M = tl["M"]
                    masks = tl.get("masks", [])
                    B.mm(Sx[0:M, j * NQ:(j + 1) * NQ], tl["kT"], qrhs, True, len(masks) == 0, [tl["kres"], qres], [sres])
                    for mi, (ml, mr, mres) in enumerate(masks):
                        B.mm(Sx[0:M, j * NQ:(j + 1) * NQ], ml, mr, False, mi == len(masks) - 1, mres, [sres])
                eb = ecnt[0] % 2
                ecnt[0] += 1
                eres = "e%d" % eb
                B.act(ebuf[eb][0:Mmax, 0:len(chunk) * NQ], Sx[0:Mmax, 0:len(chunk) * NQ], AF.Exp, [sres], [eres])
                for j, tl in enumerate(chunk):
                    M = tl["M"]
                    last = (c0 + j == n - 1)
                    for r in range(4):
                        B.mm(acc[:, r, :], ebuf[eb][0:M, j * NQ + r * QN:j * NQ + (r + 1) * QN], tl["v"], first and r == 0, last,
                             [eres, tl["vres"]], [accres], nocheck=True)
                    if tagU is not None:
                        for r in range(4):
                            B.mm(tagU[0][:, r, :], ebuf[eb][0:M, j * NQ + r * QN:j * NQ + (r + 1) * QN], tl["u"], first and r == 0, last,
                                 [eres, "k_ov"], [tagU[1]], nocheck=True)
                    first = False

        def finish_branch(QN, acc, accres, odst, ores, col):
            B.ts1("vector", rs_c[0:QN, col:col + 4], acc[:, :, 64], 1e-30, ALU.max, [accres], ["rs_c"])
            B.recip(rs_c[0:QN, col:col + 4], rs_c[0:QN, col:col + 4], ["rs_c"], ["rs_c"])
            B.tt("vector", odst[0:QN, :, :], acc[:, :, 0:64], rs_c[0:QN, col:col + 4].unsqueeze(2).broadcast_to([QN, 4, 64]),
                 ALU.mult, [accres, "rs_c"], [ores])

        def combine(QN, g, gates_ap):
            g3 = gates_ap.rearrange("p (h b) -> p h b", b=3)[:, 4 * g:4 * g + 4, :]
            for bi, (o_, ores) in enumerate(((ocmp, "ocmp"), (osel, "osel"), (owin, "owin"))):
                B.tt("vector" if bi != 1 else "gpsimd", o_[0:QN, :, :], o_[0:QN, :, :],
                     g3[:, :, bi:bi + 1].broadcast_to([QN, 4, 64]), ALU.mult, [ores, "gates"], [ores])
            B.tt("vector", ocmp[0:QN, :, :], ocmp[0:QN, :, :], osel[0:QN, :, :], ALU.add, ["ocmp", "osel"], ["ocmp"])
            B.tt("vector", mixed[0:QN, g * 256:(g + 1) * 256].rearrange("p (r d) -> p r d", r=4), ocmp[0:QN, :, :], owin[0:QN, :, :],
                 ALU.add, ["ocmp", "owin"], ["mixed"])

        def select_blocks(QN, accU, ures, rs_col, bonus_fn, kth):
            B.ts1("vector", imp[0:QN, :], accU[:, 0, :], rs_c[0:QN, rs_col:rs_col + 1], ALU.mult, [ures, "rs_c"], ["imp"])
            for r in range(1, 4):
                B.stt("vector", imp[0:QN, :], accU[:, r, :], rs_c[0:QN, rs_col + r:rs_col + r + 1], imp[0:QN, :],
                      ALU.mult, ALU.add, [ures, "rs_c", "imp"], ["imp"])
            bonus_fn()
            B.P.add("vector", lambda e: e.max(out=mx[0:QN, 0:8], in_=imp[0:QN, :]), ["imp"], ["mx"])
            B.P.add("vector", lambda e: e.match_replace(out=imp2[0:QN, :], in_to_replace=mx[0:QN, 0:8], in_values=imp[0:QN, :],
                                                        imm_value=-1e30), ["imp", "mx"], ["imp2"])
            B.P.add("vector", lambda e: e.max(out=mx[0:QN, 8:16], in_=imp2[0:QN, :]), ["imp2"], ["mx"])
            B.ts1("vector", imp2[0:QN, :], imp[0:QN, :], mx[0:QN, 8 + kth - 9:8 + kth - 8], ALU.is_ge, ["imp", "mx"], ["imp2"])
            B.ts("vector", negq[0:QN, :], imp2[0:QN, :], -1.0, -NEGM, ALU.add, ALU.mult, ["imp2"], ["negq"])
            for hf in range(2):
                B.tr(T0[0:64, hf * 128:hf * 128 + QN], negq[0:QN, hf * 64:(hf + 1) * 64], ident[0:QN, 0:QN], ["negq", "k_ident"], ["T0"])
            for hf in range(2):
                B.cp("vector", negT4[:, hf, 0:4 * QN].rearrange("p (r q) -> p r q", r=4),
                     T0[0:64, hf * 128:hf * 128 + QN].unsqueeze(1).broadcast_to([64, 4, QN]), ["T0"], ["negT4"])

        def retention(R, t):
            gam = [1.0 - 2.0 ** (-5.0 - h) for h in range(4)]
            for h in range(4):
                B.mm(Mb[0][0:R, h * 128:h * 128 + R], rkT[:, h, 0:R], rqT[:, h, 0:R], True, True, ["rkT", "rqT"], ["M0"])
            B.tt("vector", sTm[0:R, :].rearrange("p (h i) -> p h i", h=4)[:, :, 0:R],
                 Mb[0][0:R, :].rearrange("p (h i) -> p h i", h=4)[:, :, 0:R],
                 ct["decT"][0:R, :].rearrange("p (h i) -> p h i", h=4)[:, :, 0:R], ALU.mult, ["M0", "k_decT"], ["sTm"])
            B.tt("gpsimd", qcr[:, :, 0:R], rqT[:, :, 0:R],
                 ct["crossT"][:, :].rearrange("p (h i) -> p h i", h=4)[:, :, 0:R], ALU.mult, ["rqT", "k_crossT"], ["qcr"])
            for h in range(4):
                B.mm(Mb[1][0:R, h * 128:(h + 1) * 128], sTm[0:R, h * 128:h * 128 + R], rv_bf[0:R, h * 128:(h + 1) * 128],
                     True, False, ["sTm", "rv_bf"], ["M1"])
                B.mm(Mb[1][0:R, h * 128:(h + 1) * 128], qcr[:, h, 0:R], Sbf[:, h, :], False, True, ["qcr", "Sbf"], ["M1"])
            B.tt("vector", kwk[0:R, :].rearrange("p (h d) -> p h d", h=4), rqk_bf[0:R, 256:512].rearrange("p (h d) -> p h d", h=4),
                 ct["wtab"][0:R, :].unsqueeze(2).broadcast_to([R, 4, 64]), ALU.mult, ["rqk_bf", "k_wtab"], ["kwk"])
            for h in range(4):
                B.mm(Mb[0][0:64, h * 128:(h + 1) * 128], kwk[0:R, h * 64:(h + 1) * 64],
                     rv_bf[0:R, h * 128:(h + 1) * 128], True, True, ["kwk", "rv_bf", "sTm"], ["M0"])
            for h in range(4):
                B.stt("vector", Sf[:, h, :], Sf[:, h, :], float(gam[h] ** 128), Mb[0][0:64, h * 128:(h + 1) * 128],
                      ALU.mult, ALU.add, ["Sf", "M0"], ["Sf"])
            B.cp("gpsimd", Sbf[:, :, :], Sf[:, :, :], ["Sf"], ["Sbf"])
            B.act(junk[0:R, 0:512], Mb[1][0:R, :], AF.Square, ["M1"], ["junk"])
            B.red(st1[0:R, 2:6], junk[0:R, 0:512].rearrange("p (h e) -> p h e", h=4), ["junk"], ["st1"])
            B.rsqrt(st1[0:R, 2:6], st1[0:R, 2:6], 1.0 / 128, ["st1"], ["st1"])
            B.tt("vector", oret[0:R, :].rearrange("p (h e) -> p h e", h=4), Mb[1][0:R, :].rearrange("p (h e) -> p h e", h=4),
                 st1[0:R, 2:6].unsqueeze(2).broadcast_to([R, 4, 128]), ALU.mult, ["M1", "st1"], ["oret"])
            B.tt("gpsimd", mixed[0:R, 512:1024], oret[0:R, :], rg_f[0:R, :], ALU.mult, ["oret", "rg_f"], ["mixed"])

        def out_proj(R, x_sb, xr, y_dst):
            for k in range(8):
                B.tr(T0[:, k * 128:k * 128 + R], mixed[0:R, k * 128:(k + 1) * 128], ident[0:R, 0:R], ["mixed", "k_ident"], ["T0"])
            B.cp("vector", mixT[:, :, 0:R], T0[:, :].rearrange("p (k t) -> p k t", k=8)[:, :, 0:R], ["T0"], ["xnT"])
            for nh in range(2):
                for k in range(8):
                    B.mm(Mb[nh][0:R, :], mixT[:, k, 0:R], w_out_bf[:, k, nh * 512:(nh + 1) * 512], k == 0, k == 7,
                         ["xnT", "w_out_bf"], ["M%d" % nh])
                B.tt("vector", x_sb[0:R, nh * 512:(nh + 1) * 512], Mb[nh][0:R, :], x_sb[0:R, nh * 512:(nh + 1) * 512], ALU.add,
                     ["M%d" % nh, xr], [xr])
            B.dma(y_dst, x_sb[0:R, :], [xr], [], xr)

        if NT > 0 and DBG >= 2:
            for t in range(NT):
                xslot = 0
                last4 = t >= NT - 4
                kvd = [o_kvc_p[t * 128:(t + 1) * 128, :], o_kvs_p[t * 128:(t + 1) * 128, :],
                       o_kvw_p[(t - (NT - 4)) * 128:(t - (NT - 4) + 1) * 128, :] if (last4 and NT >= 4) else None]
                if NT < 4:
                    kvd[2] = o_kvw_p[t * 128:(t + 1) * 128, :]
                project(128, x_p[t * 128:(t + 1) * 128, :], xslot, cd["ropeN_p"][t * 128:(t + 1) * 128, :],
                        cd["ropeR_p"][t * 128:(t + 1) * 128, :], kvd, "p%d" % t)
                if DBG < 3:
                    continue
                for h in range(8):
                    B.tr(T0[:, h * 128:(h + 1) * 128], q_bf[:, h, :], ident[:, :], ["q_bf", "k_ident"], ["T0"])
                B.cp("vector", qT[:, :, :], T0[:, :].rearrange("p (h t) -> p h t", h=8), ["T0"], ["qT"])
                B.tr(T0[:, 0:128], k_bf[:, 1, :], ident[:, :], ["k_bf", "k_ident"], ["T0"])
                B.tr(T0[:, 128:256], k_bf[:, 2, :], ident[:, :], ["k_bf", "k_ident"], ["T0"])
                B.tr(T0[:, 4 * 128:5 * 128], k_bf[:, 0, :], ident[:, :], ["k_bf", "k_ident"], ["T0"])
                B.tr(T0[:, 5 * 128:6 * 128], cv_bf[:, :], ident[:, :], ["cv_bf", "k_ident"], ["T0"])
                wsl = t % 8
                B.cp("vector", ksT[:, t * 128:(t + 1) * 128], T0[:, 0:128], ["T0"], ["ksT"])
                B.cp("vector", kwT[:, wsl, :], T0[:, 128:256], ["T0"], ["kwT"])
                if t > 0:
                    B.cp("vector", cmpT[:, :, 0:16], cmpT[:, :, 128:144], ["cmpT"], ["cmpT"])
                B.cp("vector", cmpT[:, :, 16:144], T0[:, 512:768].rearrange("p (k t) -> p k t", k=2), ["T0"], ["cmpT"])
                B.cp("gpsimd", vs[:, t, :, 0:64], kvo[:, 256:512].rearrange("p (v g d) -> p v g d", v=2, g=2)[:, 1, :, :], ["kvo"], ["vs"])
                B.cp("gpsimd", vw[:, wsl, :, 0:64], kvo[:, 512:768].rearrange("p (v g d) -> p v g d", v=2, g=2)[:, 1, :, :], ["kvo"], ["vw"])
                for j in range(8):
                    B.tr(T0[0:64, j * 128:(j + 1) * 128], rqk_bf[:, j * 64:(j + 1) * 64], ident[:, :], ["rqk_bf", "k_ident"], ["T0"])
                B.cp("vector", rqT[:, :, :], T0[0:64, 0:512].rearrange("p (h t) -> p h t", h=4), ["T0"], ["rqT"])
                B.cp("vector", rkT[:, :, :], T0[0:64, 512:1024].rearrange("p (h t) -> p h t", h=4), ["T0"], ["rkT"])
                if 'cmp' not in SKIP:
                    compress(8, 8 * t - 1, t == 0)
                ncb = 8 * t + 7
                for g in range(2 if DBG >= 4 else 0):
                    qrhs = qT[:, 4 * g:4 * g + 4, :].rearrange("p r t -> p (r t)")
                    tiles = []
                    nct = (ncb + 127) // 128
                    for ctile in range(nct):
                        M = min(128, ncb - 128 * ctile)
                        delta = t - 16 * ctile
                        masks = []
                        if delta <= 16:
                            a = 129 - 8 * delta
                            masks.append((ct["shid"][0:8, a:a + M], ct["cbase4"][0:8, :], ["k_shid", "k_cbase4"]))
                        tiles.append(dict(M=M, kT=kcT[:, 128 * ctile:128 * ctile + M], kres="kcT", masks=masks,
                                          v=vc[0:M, ctile, g, :], vres="vc", u=ct["ov"][0:M, ctile * 128:(ctile + 1) * 128]))
                    accC = Ab[0][:, 0:260].rearrange("p (r w) -> p r w", r=4)
                    accU = Ab[1][:, :].rearrange("p (r w) -> p r w", r=4)
                    branch(128, qrhs, "qT", tiles, accC, "A0", 65, tagU=(accU, "A1"))
                    finish_branch(128, accC, "A0", ocmp, "ocmp", 0)

                    def bonus(t=t):
                        lo = max(0, 2 * t - 1)
                        po = lo - (2 * t - 1)
                        B.tt("vector", imp[:, lo:2 * t + 2], imp[:, lo:2 * t + 2], ct["pat"][:, po:3], ALU.add, ["imp", "k_pat"], ["imp"])
                        B.ts1("vector", imp[:, 0:1], imp[:, 0:1], 1e4, ALU.add, ["imp"], ["imp"])
                    select_blocks(128, accU, "A1", 0, bonus, 16)
                    tiles = []
                    for wt in range(max(0, t - 4), t + 1):
                        masks = []
                        if wt == t - 4:
                            masks.append((ident[:, :], ct["winfirst4"][:, :], ["k_ident", "k_winfirst4"]))
                        if wt == t:
                            masks.append((ident[:, :], ct["causal4"][:, :], ["k_ident", "k_causal4"]))
                        tiles.append(dict(M=128, kT=kwT[:, wt % 8, :], kres="kwT", masks=masks,
                                          v=vw[:, wt % 8, g, :], vres="vw"))
                    accW = Ab[2][:, 0:260].rearrange("p (r w) -> p r w", r=4)
                    branch(128, qrhs, "qT", tiles, accW, "A2", 65)
                    finish_branch(128, accW, "A2", owin, "owin", 8)
                    tiles = []
                    for kt in range(t + 1):
                        masks = [emask(kt, 128)]
                        if kt == t:
                            masks.append((ident[:, :], ct["causal4"][:, :], ["k_ident", "k_causal4"]))
                        tiles.append(dict(M=128, kT=ksT[:, kt * 128:(kt + 1) * 128], kres="ksT", masks=masks,
                                          v=vs[:, kt, g, :], vres="vs"))
                    accS = Ab[0][:, 0:260].rearrange("p (r w) -> p r w", r=4)
                    branch(128, qrhs, "qT", tiles, accS, "A0", 65)
                    finish_branch(128, accS, "A0", osel, "osel", 4)
                    combine(128, g, gates[:, :])
                if DBG < 5:
                    continue
                retention(128, t)
                out_proj(128, xs_[xslot], "xs%d" % xslot, y_p[t * 128:(t + 1) * 128, :])
            if DBG >= 5:
                B.dma(o_ret_p, Sf[:, :, :].rearrange("p h e -> p (h e)"), ["Sf"], [], "o_ret_p")


        if NB > 0 and DBG >= 7:
            NS = PAST // 64
            GP = min(8, NPG)
            xslot = 0
            xr = "xs%d" % xslot
            qTs = sb("qTs", [128, NB, 8, 4], BF16)
            ksTn = sb("ksTn", [128, RS], BF16)
            kwTn = sb("kwTn", [128, RS], BF16)
            vsrc = xn
            vrow = sb("vrow", [4, 1024], BF16)
            pt_i = sb("pt_i", [128, NPG], I32)
            pt_f = sb("pt_f", [128, NPG], F32)
            idx = sb("idx", [128, NPG], I32)
            idx64 = sb("idx64", [128, NPG], I32)
            kwTs = sb("kwTs", [128, 516], BF16)
            vws = sb("vws", [128, 5, 2, 65], BF16)
            wstV = qn
            oall = sb("oall", [128, 3, 2, 256], BF16)
            ob16 = sb("ob16", [4, 3, 256], BF16)
            Sfb = Sf
            Sbb = Sbf
            sTm4 = sb("sTm4", [4, 16], BF16)
            kwk4 = sb("kwk4", [4, 256], BF16)
            oret4 = sb("oret4", [4, 512], F32)
            oretS = oret
            B.memset("vector", vws[:, :, :, 64:65], 1.0, ["vws"])
            pgb = [stage[0][:, 0:512], stage[0][:, 512:1024], stage[1][:, 512:1024]]
            wstK = stage[1][:, 0:512]

            project(RS, x_s, xslot, cd["ropeN_s"], cd["ropeR_s"], [o_kvc_s, o_kvs_s, None], "s")
            B.dma(o_kvw_s[:, 0:508, :], st_win[:, 4:512, :], [], [], "kvw_d2d")
            for b in range(NB):
                B.dma(o_kvw_s[b, 508:512, :], kvo[4 * b:4 * b + 4, 512:768], ["kvo"], [], "kvo_out")
            idr = ident[0:RS, 0:RS]
            for h in range(8):
                B.tr(T0[:, h * 128:h * 128 + RS], q_bf[0:RS, h, :], idr, ["q_bf", "k_ident"], ["T0"])
            B.cp("vector", qTs[:, :, :, :].rearrange("p b h s -> p h b s"),
                 T0[:, :].rearrange("p (h t) -> p h t", h=8)[:, :, 0:RS].rearrange("p h (b s) -> p h b s", s=4), ["T0"], ["qTs"])
            B.tr(T0[:, 0:RS], k_bf[0:RS, 1, :], idr, ["k_bf", "k_ident"], ["T0"])
            B.tr(T0[:, 128:128 + RS], k_bf[0:RS, 2, :], idr, ["k_bf", "k_ident"], ["T0"])
            B.cp("vector", ksTn[:, :], T0[:, 0:RS], ["T0"], ["ksTn"])
            B.cp("vector", kwTn[:, :], T0[:, 128:128 + RS], ["T0"], ["kwTn"])
            for j in range(8):
                B.tr(T0[0:64, j * 128:j * 128 + RS], rqk_bf[0:RS, j * 64:(j + 1) * 64], idr, ["rqk_bf", "k_ident"], ["T0"])
            B.cp("vector", rqT[:, :, 0:RS], T0[0:64, 0:512].rearrange("p (h t) -> p h t", h=4)[:, :, 0:RS], ["T0"], ["rqT"])
            B.cp("vector", rkT[:, :, 0:RS], T0[0:64, 512:1024].rearrange("p (h t) -> p h t", h=4)[:, :, 0:RS], ["T0"], ["rkT"])
            B.tt("vector", qcr[:, :, 0:RS].rearrange("p h (b s) -> p h b s", s=4), rqT[:, :, 0:RS].rearrange("p h (b s) -> p h b s", s=4),
                 ct["cross4T"][:, :].rearrange("p (h s) -> p h s", h=4).unsqueeze(2).broadcast_to([64, 4, NB, 4]), ALU.mult,
                 ["rqT", "k_cross4T"], ["qcr"])
            B.cp("vector", vsrc[0:RS, 0:128], kvo[0:RS, 384:512], ["kvo"], ["xn"])
            B.cp("vector", vsrc[0:RS, 128:256], kvo[0:RS, 640:768], ["kvo"], ["xn"])
            B.cp("vector", vsrc[0:RS, 256:768], rv_bf[0:RS, :], ["rv_bf"], ["xn"])
            B.cp("vector", vsrc[0:RS, 768:1024], rqk_bf[0:RS, 256:512], ["rqk_bf"], ["xn"])
            gam = [1.0 - 2.0 ** (-5.0 - h) for h in range(4)]

            for b in range(NB):
                B.dma(vrow[0:4, :], vsrc[4 * b:4 * b + 4, :], ["xn"], ["vrow"], "vrow")
                B.dma(pt_i[:, :], ptab[0:1, b * NPG:(b + 1) * NPG].partition_broadcast(128), [], ["pt_i"], "pt_i")
                B.cp("vector", pt_f[:, :], pt_i[:, :], ["pt_i"], ["pt_f"])
                B.ts("vector", pt_f[:, :], pt_f[:, :], 128.0, ct["iota"][:, 0:1], ALU.mult, ALU.add, ["pt_f", "k_iota"], ["pt_f"])
                B.cp("vector", idx[:, :], pt_f[:, :], ["pt_f"], ["idx"])
                B.cp("vector", pt_f[:, :], pt_i[:, :], ["pt_i", "idx"], ["pt_f"])
                B.ts("vector", pt_f[:, :], pt_f[:, :], 64.0, ct["iota"][:, 0:1], ALU.mult, ALU.add, ["pt_f", "k_iota"], ["pt_f"])
                B.cp("vector", idx64[:, :], pt_f[:, :], ["pt_f"], ["idx64"])
                B.dma(wstK, st_winK[b], [], ["wstK"], "wstK")
                B.cp("vector", kwTs[:, 0:512], wstK, ["wstK"], ["kwTs"])
                B.cp("vector", kwTs[:, 512:516], kwTn[:, 4 * b:4 * b + 4], ["kwTn"], ["kwTs"])
                B.dma(wstV[:, :].rearrange("p (t c) -> p t c", t=4), st_winV[b].rearrange("(t p) c -> p t c", p=128), [], ["qn"], "qn")
                B.cp("vector", vws[:, 0:4, :, 0:64], wstV[:, :].rearrange("p (t g d) -> p t g d", t=4, g=2), ["qn"], ["vws"])
                B.cp("vector", vws[0:4, 4, :, 0:64], vrow[0:4, 128:256].rearrange("p (g d) -> p g d", g=2), ["vrow"], ["vws"])
                B.cp("vector", ksT[:, PAST:PAST + 4], ksTn[:, 4 * b:4 * b + 4], ["ksTn"], ["ksT"])
                B.cp("vector", vs[0:4, NPG, :, 0:64], vrow[0:4, 0:128].rearrange("p (g d) -> p g d", g=2), ["vrow"], ["vs"])
                for pg in range(NPG):
                    sl = pg % 3
                    B.P.add("gpsimd", (lambda o_, i_: (lambda e: e.indirect_dma_start(out=o_, out_offset=None, in_=c_pg,
                            in_offset=bass.IndirectOffsetOnAxis(ap=i_, axis=0))))(pgb[sl], idx[:, pg:pg + 1]),
                            ["idx"], ["pgb%d" % sl], dma="pgb%d" % sl)
                    lc = 16 + (pg % GP) * 128
                    B.cp("vector", cmpT[:, :, lc:lc + 128], pgb[sl][:, 0:256].rearrange("p (k r) -> p k r", k=2), ["pgb%d" % sl], ["cmpT"])
                    B.cp("scalar", ksT[:, pg * 128:(pg + 1) * 128], pgb[sl][:, 256:384], ["pgb%d" % sl], ["ksT"])
                    B.cp("gpsimd", vs[:, pg, :, 0:64], pgb[sl][:, 384:512].rearrange("p (g d) -> p g d", g=2), ["pgb%d" % sl], ["vs"])
                    if (pg + 1) % GP == 0:
                        Gi = pg // GP
                        compress(8 * GP, 8 * GP * Gi - 1, Gi == 0)
                        B.cp("vector", cmpT[:, :, 0:16], cmpT[:, :, 128 * GP:128 * GP + 16], ["cmpT"], ["cmpT"])
                for g in range(2):
                    qrhs = qTs[:, b, 4 * g:4 * g + 4, :].rearrange("p r s -> p (r s)")
                    accC = Ab[0][0:4, 0:260].rearrange("p (r w) -> p r w", r=4)
                    accU = Ab[1][0:4, :].rearrange("p (r w) -> p r w", r=4)
                    tiles = []
                    for ctile in range((NCS + 127) // 128):
                        M = min(128, NCS - 128 * ctile)
                        tiles.append(dict(M=M, kT=kcT[:, 128 * ctile:128 * ctile + M], kres="kcT", masks=[],
                                          v=vc[0:M, ctile, g, :], vres="vc", u=ct["ov"][0:M, ctile * 128:(ctile + 1) * 128]))
                    branch(4, qrhs, "qTs", tiles, accC, "A0", 65, tagU=(accU, "A1"))
                    finish_branch(4, accC, "A0", ocmp, "ocmp", 0)

                    def bonus_s():
                        B.ts1("vector", imp[0:4, 0:1], imp[0:4, 0:1], 1e4, ALU.add, ["imp"], ["imp"])
                        B.ts1("vector", imp[0:4, NS - 1:NS], imp[0:4, NS - 1:NS], 1e4, ALU.add, ["imp"], ["imp"])
                    select_blocks(4, accU, "A1", 0, bonus_s, 15)
                    tiles = []
                    for wt in range(4):
                        masks = [(ident[:, :], ct["swinfirst4"][:, :], ["k_ident", "k_swinfirst4"])] if wt == 0 else []
                        tiles.append(dict(M=128, kT=kwTs[:, wt * 128:(wt + 1) * 128], kres="kwTs", masks=masks,
                                          v=vws[:, wt, g, :], vres="vws"))
                    tiles.append(dict(M=4, kT=kwTs[:, 512:516], kres="kwTs",
                                      masks=[(ident[0:4, 0:4], ct["scausal4"][0:4, :], ["k_ident", "k_scausal4"])],
                                      v=vws[0:4, 4, g, :], vres="vws"))
                    accW = Ab[2][0:4, 0:260].rearrange("p (r w) -> p r w", r=4)
                    branch(4, qrhs, "qTs", tiles, accW, "A2", 65)
                    finish_branch(4, accW, "A2", owin, "owin", 8)
                    tiles = []
                    for kt in range(NPG):
                        tiles.append(dict(M=128, kT=ksT[:, kt * 128:(kt + 1) * 128], kres="ksT", masks=[emask(kt, 4)],
                                          v=vs[:, kt, g, :], vres="vs"))
                    tiles.append(dict(M=4, kT=ksT[:, PAST:PAST + 4], kres="ksT",
                                      masks=[(ident[0:4, 0:4], ct["scausal4"][0:4, :], ["k_ident", "k_scausal4"])],
                                      v=vs[0:4, NPG, g, :], vres="vs"))
                    accS = Ab[0][0:4, 0:260].rearrange("p (r w) -> p r w", r=4)
                    branch(4, qrhs, "qTs", tiles, accS, "A0", 65)
                    finish_branch(4, accS, "A0", osel, "osel", 4)
                    for bi, (o_, ores) in enumerate(((ocmp, "ocmp"), (osel, "osel"), (owin, "owin"))):
                        B.cp("vector", ob16[0:4, bi, :], o_[0:4, :, :].rearrange("p r d -> p (r d)"), [ores], ["ob16_%d" % bi])
                        B.dma(oall[4 * b:4 * b + 4, bi, g, :], ob16[0:4, bi, :], ["ob16_%d" % bi], ["oall"], "ob16_%d" % bi)
                B.dma(Sfb[:, :, :].rearrange("p h e -> p (h e)"), st_ret[:, b * 512:(b + 1) * 512], [], ["Sf"], "Sf")
                B.cp("vector", Sbb[:, :, :], Sfb[:, :, :], ["Sf"], ["Sbf"])
                for h in range(4):
                    B.mm(Mb[0][0:4, h * 4:h * 4 + 4], rkT[:, h, 4 * b:4 * b + 4], rqT[:, h, 4 * b:4 * b + 4], True, True, ["rkT", "rqT"], ["M0"])
                B.tt("vector", sTm4[0:4, :], Mb[0][0:4, 0:16], ct["dec4T"][0:4, :], ALU.mult, ["M0", "k_dec4T"], ["sTm4"])
                for h in range(4):
                    B.mm(Mb[1][0:4, h * 128:(h + 1) * 128], sTm4[0:4, h * 4:h * 4 + 4], vrow[0:4, 256 + h * 128:256 + (h + 1) * 128],
                         True, False, ["sTm4", "vrow"], ["M1"])
                    B.mm(Mb[1][0:4, h * 128:(h + 1) * 128], qcr[:, h, 4 * b:4 * b + 4], Sbb[:, h, :], False, True, ["qcr", "Sbf"], ["M1"])
                B.tt("vector", kwk4[0:4, :].rearrange("p (h d) -> p h d", h=4), vrow[0:4, 768:1024].rearrange("p (h d) -> p h d", h=4),
                     ct["w4tab"][0:4, :].unsqueeze(2).broadcast_to([4, 4, 64]), ALU.mult, ["vrow", "k_w4tab"], ["kwk4"])
                for h in range(4):
                    B.mm(Mb[0][0:64, h * 128:(h + 1) * 128], kwk4[0:4, h * 64:(h + 1) * 64], vrow[0:4, 256 + h * 128:256 + (h + 1) * 128],
                         True, True, ["kwk4", "vrow"], ["M0"])
                for h in range(4):
                    B.stt("vector", Sfb[:, h, :], Sfb[:, h, :], float(gam[h] ** 4), Mb[0][0:64, h * 128:(h + 1) * 128],
                          ALU.mult, ALU.add, ["Sf", "M0"], ["Sf"])
                B.dma(o_ret_s[:, b * 512:(b + 1) * 512], Sfb[:, :, :].rearrange("p h e -> p (h e)"), ["Sf"], [], "Sf")
                B.cp("vector", oret4[0:4, :], Mb[1][0:4, :], ["M1"], ["oret4"])
                B.dma(oretS[4 * b:4 * b + 4, :], oret4[0:4, :], ["oret4"], ["oret"], "oret4")
            B.act(junk[0:RS, 0:512], oretS[0:RS, :], AF.Square, ["oret"], ["junk"])
            B.red(st1[0:RS, 2:6], junk[0:RS, 0:512].rearrange("p (h e) -> p h e", h=4), ["junk"], ["st1"])
            B.rsqrt(st1[0:RS, 2:6], st1[0:RS, 2:6], 1.0 / 128, ["st1"], ["st1"])
            B.tt("vector", oret[0:RS, :].rearrange("p (h e) -> p h e", h=4), oretS[0:RS, :].rearrange("p (h e) -> p h e", h=4),
                 st1[0:RS, 2:6].unsqueeze(2).broadcast_to([RS, 4, 128]), ALU.mult, ["oret", "st1"], ["oret"])
            B.tt("vector", mixed[0:RS, 512:1024], oret[0:RS, :], rg_f[0:RS, :], ALU.mult, ["oret", "rg_f"], ["mixed"])
            for g in range(2):
                for bi, (o_, ores) in enumerate(((ocmp, "ocmp"), (osel, "osel"), (owin, "owin"))):
                    B.cp("vector", o_[0:RS, :, :], oall[0:RS, bi, g, :].rearrange("p (r d) -> p r d", r=4), ["oall"], [ores])
                combine(RS, g, gates[0:RS, :])
            out_proj(RS, xs_[xslot], xr, y_s)
        print("sbuf bytes remaining pass1", nc.sbuf_bytes_remaining)
        print("pass1 ops", P.emit(nc, "a"))

    P2 = Prog()
    with ExitStack() as es:
      if not os.environ.get('K_NOP2'):
          B = Bld(nc, P2, es)
          sb, ps = B.sb, B.ps
          G = 256
          Ub = [ps("U0", [128, 512], F32), ps("U1", [128, 512], F32)]
          Vb = [ps("V0", [128, 512], F32), ps("V1", [128, 512], F32)]
          T0 = ps("T0b", [128, 1024], BF16)
          Yb = [ps("Y0", [128, 512], F32), ps("Y1", [128, 512], F32)]
          ident = sb("ident2", [128, 128], BF16)
          B.dma(ident[:], cd["ident"], [], ["ident2"], "ident2")
          epsc = sb("epsc2", [128, 1], F32)
          B.memset("vector", epsc[:], EPS, ["epsc2"])
          B.eps_ap = epsc
          wfi = sb("wfi", [128, 8, 2 * D_FF], BF16)
          wfo = sb("wfo", [128, NFC, 1024], BF16)
          nmf = sb("nmf", [128, 8], F32)
          wcv = sb("wcv", [128, NFC, 4], F32)
          stage = [sb("stg0", [128, D_FF], F32), sb("stg1", [128, D_FF], F32)]
          B.dma(nmf[:], nm_ffn, [], ["nmf"], "nmf")
          B.dma(wcv[:, :, :].rearrange("p c j -> p (c j)"), wconv, [], ["wcv"], "wcv")
          si = 0
          for k in range(8):
              for hf in range(2):
                  s_ = si % 2
                  B.dma(stage[s_][:, :], w_fi[k * 128:(k + 1) * 128, hf * D_FF:(hf + 1) * D_FF], [], ["stg%d" % s_], "stg%d" % s_)
                  B.ts1("vector" if si % 2 == 0 else "gpsimd", wfi[:, k, hf * D_FF:(hf + 1) * D_FF], stage[s_][:, :], nmf[:, k:k + 1], ALU.mult,
                        ["stg%d" % s_, "nmf"], ["wfi"])
                  si += 1
          for fc in range(0, NFC, 2):
              s_ = si % 2
              B.dma(stage[s_][:, 0:2048].rearrange("p (c n) -> p c n", c=2),
                    w_fo[fc * 128:(fc + 2) * 128, :].rearrange("(c p) n -> p c n", p=128), [], ["stg%d" % s_], "stg%d" % s_)
              B.cp("vector" if si % 2 == 0 else "gpsimd", wfo[:, fc:fc + 2, :], stage[s_][:, 0:2048].rearrange("p (c n) -> p c n", c=2),
                   ["stg%d" % s_], ["wfo"])
              si += 1
          hs = [sb("hs%d" % i, [128, 1024], F32) for i in range(2)]
          hn = sb("hn", [128, 1024], BF16)
          hnT = sb("hnT", [128, 8, G], BF16)
          junk = sb("junk2", [128, 1024], F32)
          st1 = sb("st2", [128, 4], F32)
          ubufs = [sb("ubuf0", [128, G + 2], F32), sb("ubuf1", [128, G + 2], F32)]
          carry = sb("carry", [128, NFC, 2], F32)
          ucs = [sb("uc0", [128, G], F32), sb("uc1", [128, G], F32)]
          ues = [sb("ue0", [128, G], F32), sb("ue1", [128, G], F32)]
          actb = sb("actb", [128, NFC, G], BF16)
          y_sb = sb("y_sb", [128, 1024], F32)
          B.memset("vector", carry[:], 0.0, ["carry"])

          def ffn_group(tiles_rows, loads, stores, conv_w, carry_mode):
              ncol = sum(tiles_rows)
              col = 0
              cols = []
              for i, R in enumerate(tiles_rows):
                  hr = "hs%d" % i
                  B.dma(hs[i][0:R, :], loads[i], [], [hr], hr)
                  B.act(junk[0:R, :], hs[i][0:R, :], AF.Square, [hr], ["junk2", "st2"], accum=st1[0:R, 0:1])
                  B.rsqrt(st1[0:R, 1:2], st1[0:R, 0:1], 1.0 / 1024, ["st2"], ["st2"])
                  B.act(hn[0:R, :], hs[i][0:R, :], AF.Identity, [hr, "st2"], ["hn"], scale=st1[0:R, 1:2])
                  for k in range(8):
                      B.tr(T0[:, k * 128:k * 128 + R], hn[0:R, k * 128:(k + 1) * 128], ident[0:R, 0:R], ["hn", "ident2"], ["T0b"])
                  B.cp("vector", hnT[:, :, col:col + R], T0[:, :].rearrange("p (k t) -> p k t", k=8)[:, :, 0:R], ["T0b"], ["hnT"])
                  cols.append(col)
                  col += R
              for fc in range(NFC):
                  bk = fc % 2
                  for k in range(8):
                      B.mm(Ub[bk][:, 0:ncol], wfi[:, k, fc * 128:(fc + 1) * 128], hnT[:, k, 0:ncol], k == 0, k == 7, ["wfi", "hnT"], ["U%d" % bk])
                  for k in range(8):
                      B.mm(Vb[bk][:, 0:ncol], wfi[:, k, D_FF + fc * 128:D_FF + (fc + 1) * 128], hnT[:, k, 0:ncol], k == 0, k == 7,
                           ["wfi", "hnT"], ["V%d" % bk])
                  uc, ue, ubuf = ucs[bk], ues[bk], ubufs[bk]
                  ucr, uer, ubr = "uc%d" % bk, "ue%d" % bk, "ubuf%d" % bk
                  conv_w(fc, Ub[bk], "U%d" % bk, ncol, uc, ucr, ubuf, ubr)
                  B.act(ue[:, 0:ncol], uc[:, 0:ncol], AF.Silu, [ucr], [uer])
                  B.tt("vector", actb[:, fc, 0:ncol], ue[:, 0:ncol], Vb[bk][:, 0:ncol], ALU.mult, [uer, "V%d" % bk], ["actb%d" % fc])
              for i, R in enumerate(tiles_rows):
                  hr = "hs%d" % i
                  for nh in range(2):
                      for fc in range(NFC):
                          B.mm(Yb[nh][0:R, :], actb[:, fc, cols[i]:cols[i] + R], wfo[:, fc, nh * 512:(nh + 1) * 512], fc == 0, fc == NFC - 1,
                               ["actb%d" % fc, "wfo"], ["Y%d" % nh])
                      B.tt("vector", y_sb[0:R, nh * 512:(nh + 1) * 512], Yb[nh][0:R, :], hs[i][0:R, nh * 512:(nh + 1) * 512], ALU.add,
                           ["Y%d" % nh, hr], ["y_sb"])
                  B.dma(stores[i], y_sb[0:R, :], ["y_sb"], [], "y_out")

          def conv_seq(fc, Ux, ures, ncol, uc, ucr, ubuf, ubr):
              B.cp("scalar", ubuf[:, 0:2], carry[:, fc, :], ["carry"], [ubr])
              B.cp("scalar", ubuf[:, 2:2 + ncol], Ux[:, 0:ncol], [ures], [ubr])
              B.ts("vector", uc[:, 0:ncol], ubuf[:, 2:2 + ncol], wcv[:, fc, 2:3], wcv[:, fc, 3:4], ALU.mult, ALU.add, [ubr, "wcv"], [ucr])
              B.stt("vector", uc[:, 0:ncol], ubuf[:, 1:1 + ncol], wcv[:, fc, 1:2], uc[:, 0:ncol], ALU.mult, ALU.add, [ubr, "wcv", ucr], [ucr])
              B.stt("vector", uc[:, 0:ncol], ubuf[:, 0:ncol], wcv[:, fc, 0:1], uc[:, 0:ncol], ALU.mult, ALU.add, [ubr, "wcv", ucr], [ucr])
              B.cp("scalar", carry[:, fc, :], ubuf[:, ncol:ncol + 2], [ubr], ["carry"])

          if NT > 0 and DBG >= 6:
              tpg = G // 128
              for gi in range(NT // tpg):
                  rows = [128] * tpg
                  loads = [y_p[(gi * tpg + i) * 128:(gi * tpg + i + 1) * 128, :] for i in range(tpg)]
                  ffn_group(rows, loads, loads, conv_seq, "seq")
              B.dma(o_conv_p, carry[:, :, :].rearrange("p c j -> p (c j)"), ["carry"], [], "o_conv_p")

          if NB > 0 and DBG >= 7:
              stc = sb("stc", [128, NFC, NB, 2], F32)
              convS = sb("convS", [128, NFC, NB, 2], F32)
              B.dma(stc[:, :, :, :].rearrange("p c b j -> p (c b j)"), st_conv, [], ["stc"], "stc")

              def conv_smp(fc, Ux, ures, ncol, uc, ucr, ubuf, ubr):
                  ub = ubuf[:, 0:NB * 6].rearrange("p (b j) -> p b j", j=6)
                  B.cp("scalar", ub[:, :, 0:2], stc[:, fc, :, :], ["stc"], [ubr])
                  B.cp("scalar", ub[:, :, 2:6], Ux[:, 0:RS].rearrange("p (b s) -> p b s", s=4), [ures], [ubr])
                  uc3 = uc[:, 0:RS].rearrange("p (b s) -> p b s", s=4)
                  B.ts("vector", uc3, ub[:, :, 2:6], wcv[:, fc, 2:3], wcv[:, fc, 3:4], ALU.mult, ALU.add, [ubr, "wcv"], [ucr])
                  B.stt("vector", uc3, ub[:, :, 1:5], wcv[:, fc, 1:2], uc3, ALU.mult, ALU.add, [ubr, "wcv", ucr], [ucr])
                  B.stt("vector", uc3, ub[:, :, 0:4], wcv[:, fc, 0:1], uc3, ALU.mult, ALU.add, [ubr, "wcv", ucr], [ucr])
                  B.cp("scalar", convS[:, fc, :, :], ub[:, :, 4:6], [ubr], ["convS"])
              ffn_group([RS], [y_s], [y_s], conv_smp, "smp")
              B.dma(o_conv_s, convS[:, :, :, :].rearrange("p c b j -> p (c b j)"), ["convS"], [], "o_conv_s")

          print("pass2 ops", P2.emit(nc, "b"))
    return nc, consts


_CACHE = {}


def _w2h(w2):
    o = np.zeros((128, 256), np.float32)
    for g in range(2):
        o[64 * g:64 * g + 64, 64 * g:64 * g + 64] = w2[0]
        o[64 * g:64 * g + 64, 128 + 64 * g:128 + 64 * g + 64] = w2[1]
    return o


def _run(inputs, S, NB, PAST, NPOOL, n_cores=8):
    key = (S, NB, PAST, NPOOL)
    if key not in _CACHE:
        _CACHE[key] = build(S, NB, PAST, NPOOL)
    nc, consts = _CACHE[key]
    f32 = np.float32
    NPG = PAST // 128
    RS = NB * 4
    A = lambda v: np.ascontiguousarray(np.asarray(v))
    x_prompt, x_sample = A(inputs["x_prompt"]), A(inputs["x_sample"])
    cache_cmp, cache_sel = A(inputs["cache_kv_cmp"])[0], A(inputs["cache_kv_sel"])[0]
    c_pg = np.empty((NPOOL, 128, 512), np.float32)
    c_pg[:, :, 0:256] = cache_cmp.reshape(NPOOL, 128, 2, 128).transpose(0, 3, 2, 1).reshape(NPOOL, 128, 256)
    c_pg[:, :, 256:384] = cache_sel[:, :, 0].reshape(NPOOL, 128, 128).transpose(0, 2, 1)
    c_pg[:, :, 384:512] = cache_sel[:, :, 1].reshape(NPOOL, 128, 128)
    c_pg = c_pg.reshape(NPOOL * 128, 512)
    st_win_all = A(inputs["state_kv_win"])[0]
    st_ret_all = A(inputs["state_ret"])[0]
    st_conv_all = A(inputs["state_conv"])[0]
    pt_all = A(inputs["page_table"]).astype(np.int32)
    shared = {("c_" + k): v for k, v in consts.items()}
    shared.update({
        "c_pg": c_pg,
        "nm_mix": A(inputs["norm_mix"])[0].reshape(8, 128).T.copy(),
        "w_in": A(inputs["w_in"])[0],
        "gq": np.tile(A(inputs["norm_q"])[0][None, :], (128, 1)).astype(f32),
        "gk": np.tile(A(inputs["norm_k"])[0].reshape(1, 192), (128, 1)).astype(f32),
        "w1h": np.tile(A(inputs["w_cmp1"])[0].reshape(2, 32, 64, 64).transpose(2, 0, 1, 3).reshape(64, 4096), (2, 1)).astype(f32),
        "w2h": _w2h(A(inputs["w_cmp2"])[0]),
        "peT": np.tile(A(inputs["pe_cmp"])[0].transpose(2, 0, 1).reshape(64, 64), (2, 1)).astype(f32),
        "w_out": A(inputs["w_out"])[0],
        "nm_ffn": A(inputs["norm_ffn"])[0].reshape(8, 128).T.copy(),
        "w_fi": A(inputs["w_ffn_in"])[0],
        "wconv": np.concatenate([A(inputs["w_conv"])[0], A(inputs["b_conv"])], axis=0).reshape(4, NFC, 128).transpose(2, 1, 0).reshape(128, NFC * 4).copy(),
        "w_fo": A(inputs["w_ffn_out"])[0],
    })
    in_maps = []
    for c in range(n_cores):
        bs = slice(c * NB, (c + 1) * NB)
        sw = st_win_all[bs]
        m = dict(shared)
        m["x_p"] = x_prompt[c]
        m["x_s"] = x_sample[bs].reshape(RS, 1024)
        m["st_winK"] = np.ascontiguousarray(sw[:, :, 0].reshape(NB, 512, 128).transpose(0, 2, 1))
        m["st_winV"] = np.ascontiguousarray(sw[:, :, 1].reshape(NB, 512, 128))
        m["st_win"] = np.ascontiguousarray(sw.reshape(NB, 512, 256))
        m["st_ret"] = np.ascontiguousarray(st_ret_all[bs].transpose(2, 0, 1, 3)).reshape(64, NB * 512)
        m["st_conv"] = np.ascontiguousarray(st_conv_all[bs].reshape(NB, 2, NFC, 128).transpose(3, 2, 0, 1)).reshape(128, NFC * NB * 2)
        m["ptab"] = np.ascontiguousarray(pt_all[bs].reshape(1, NB * NPG))
        in_maps.append(m)
    res = run_bass_kernel_spmd(nc, in_maps, core_ids=list(range(n_cores)))
    R = res.results
    cat = lambda k: np.stack([r[k] for r in R], 0)
    nB = n_cores
    DB = n_cores * NB
    y_prompt = cat("y_p")
    y_sample = cat("y_s").reshape(DB, 4, 1024)
    kvc_p = cat("o_kvc_p").reshape(1, nB, S, 2, 2, 64)
    kvs_p = cat("o_kvs_p").reshape(1, nB, S, 2, 2, 64)
    kvw_p = cat("o_kvw_p").reshape(1, nB, 512, 2, 2, 64)
    kvc_s = cat("o_kvc_s").reshape(1, DB, 4, 2, 2, 64)
    kvs_s = cat("o_kvs_s").reshape(1, DB, 4, 2, 2, 64)
    kvw_s = cat("o_kvw_s").reshape(1, DB, 512, 2, 2, 64)
    ret_p = cat("o_ret_p").reshape(nB, 64, 4, 128).transpose(0, 2, 1, 3).reshape(1, nB, 4, 64, 128)
    ret_s = cat("o_ret_s").reshape(nB, 64, NB, 4, 128).transpose(0, 2, 3, 1, 4).reshape(1, DB, 4, 64, 128)
    conv_p = cat("o_conv_p").reshape(nB, 128, NFC, 2).transpose(0, 3, 2, 1).reshape(1, nB, 2, D_FF)
    conv_s = cat("o_conv_s").reshape(nB, 128, NFC, NB, 2).transpose(0, 3, 4, 2, 1).reshape(1, DB, 2, D_FF)
    outs = (y_prompt, y_sample, kvc_p, kvc_s, kvs_p, kvs_s, kvw_p, kvw_s, ret_p, ret_s, conv_p, conv_s)
    return tuple(np.ascontiguousarray(o, dtype=np.float32) for o in outs)


def kernel(**inputs):
    S = int(np.asarray(inputs["x_prompt"]).shape[1])
    DB = int(np.asarray(inputs["x_sample"]).shape[0])
    NPG = int(np.asarray(inputs["page_table"]).shape[1])
    NPOOL = int(np.asarray(inputs["cache_kv_cmp"]).shape[1])
    return _run(inputs, S, DB // 8, NPG * 128, NPOOL)
```

```python
import math
from contextlib import ExitStack
import numpy as np
import ml_dtypes
import concourse.bass as bass
import concourse.mybir as mybir
from concourse.bass_utils import run_bass_kernel_spmd

F32 = mybir.dt.float32
BF16 = mybir.dt.bfloat16
I32 = mybir.dt.int32
AF = mybir.ActivationFunctionType
ALU = mybir.AluOpType
AX = mybir.AxisListType
NPBF = ml_dtypes.bfloat16

import os
POOL_ENG = os.environ.get('K_POOL', 'vector')
SKIP = os.environ.get('K_SKIP', '')
PSTOP = int(os.environ.get('K_PSTOP', '99'))
PSTOP2 = int(os.environ.get('K_PSTOP2', '99'))
ENGS = ("tensor", "vector", "scalar", "gpsimd", "sync")
NEGM = -32768.0
EPS = 1e-6
D_IN = 2840
D_FF = 2816
NFC = 22


class Op:
    __slots__ = ("eng", "fn", "deps", "signal", "semkey", "idx", "count", "isdma")

    def __init__(self, eng, fn, isdma, semkey):
        self.eng = eng
        self.fn = fn
        self.deps = []
        self.signal = False
        self.semkey = semkey
        self.isdma = isdma
        self.count = None


class Prog:
    def __init__(self):
        self.ops = []
        self.last_w = {}
        self.readers = {}

    EXCL = ("S0", "S1", "A0", "A1", "A2", "T0", "M0", "M1", "U0", "U1", "V0", "V1", "T0b", "Y0", "Y1")

    def add(self, eng, fn, reads=(), writes=(), dma=None):
        xr = [r for r in reads if r in self.EXCL]
        if xr:
            writes = list(writes) + [r for r in xr if r not in writes]
        op = Op(eng, fn, dma is not None, ("dma", dma) if dma is not None else ("eng", eng))
        deps = set()
        for r in reads:
            w = self.last_w.get(r)
            if w is not None:
                deps.add(w)
        for r in writes:
            w = self.last_w.get(r)
            if w is not None:
                deps.add(w)
            for rd in self.readers.get(r, ()):
                deps.add(rd)
        if eng == "tensor":
            deps = {d for d in deps if not (d.eng == "tensor" and not d.isdma)}
        op.deps = list(deps)
        for d in op.deps:
            d.signal = True
        for r in reads:
            self.readers.setdefault(r, []).append(op)
        for r in writes:
            self.last_w[r] = op
            self.readers[r] = []
        op.idx = len(self.ops)
        self.ops.append(op)
        return op

    def emit(self, nc, tag):
        for op in self.ops:
            if op.isdma:
                op.signal = True
        lastop = {}
        for op in self.ops:
            if not op.isdma:
                lastop[op.eng] = op
        for op in lastop.values():
            op.signal = True
        counts = {}
        for op in self.ops:
            if op.signal:
                inc = 16 if op.isdma else 1
                counts[op.semkey] = counts.get(op.semkey, 0) + inc
                op.count = counts[op.semkey]
        keys = list(counts.keys())
        with ExitStack() as es:
            sems = {}
            for i, k in enumerate(keys):
                sems[k] = es.enter_context(nc.semaphore("%s%d" % (tag, i)))
            block = es.enter_context(nc.Block())
            per_eng = {e: [op for op in self.ops if op.eng == e] for e in ENGS}

            def make(ename):
                ops = per_eng[ename]

                def body(eng):
                    waited = {}
                    for op in ops:
                        need = {}
                        for d in op.deps:
                            if need.get(d.semkey, 0) < d.count:
                                need[d.semkey] = d.count
                        for k, v in need.items():
                            if waited.get(k, 0) < v:
                                eng.wait_ge(sems[k], v)
                                waited[k] = v
                        ins = op.fn(eng)
                        if op.signal:
                            ins.then_inc(sems[op.semkey], 16 if op.isdma else 1)
                    for k, v in counts.items():
                        if waited.get(k, 0) < v:
                            eng.wait_ge(sems[k], v)
                return body

            for ename in ENGS:
                getattr(block, ename)(make(ename))
        return len(self.ops), len(keys)


class Bld:
    def __init__(self, nc, P, es):
        self.nc, self.P, self.es = nc, P, es
        self.rr = 0

    def sb(self, name, shape, dt):
        return self.es.enter_context(self.nc.sbuf_tensor(name, shape, dt))

    def ps(self, name, shape, dt):
        return self.es.enter_context(self.nc.psum_tensor(name, shape, dt))

    def dma(self, out, in_, r, w, key, eng="sync"):
        self.P.add(eng, lambda e: e.dma_start(out=out, in_=in_), r, w, dma=key)

    def mm(self, out, lhsT, rhs, start, stop, r, w, nocheck=False):
        if nocheck:
            self.P.add("tensor", lambda e: e.matmul(out, lhsT=lhsT, rhs=rhs, start=start, stop=stop, skip_group_check=True), r, w)
        else:
            self.P.add("tensor", lambda e: e.matmul(out, lhsT=lhsT, rhs=rhs, start=start, stop=stop), r, w)

    def tr(self, out, in_, ident, r, w):
        self.P.add("tensor", lambda e: e.transpose(out=out, in_=in_, identity=ident), r, w)

    def act(self, out, in_, func, r, w, bias=None, scale=None, accum=None):
        kw = {}
        if bias is not None:
            kw["bias"] = bias
        if scale is not None:
            kw["scale"] = scale
        if accum is not None:
            kw["accum_out"] = accum
        self.P.add("scalar", lambda e: e.activation(out=out, in_=in_, func=func, **kw), r, w)

    def _e(self, eng):
        return POOL_ENG if eng == "gpsimd" else eng

    def tt(self, eng, out, in0, in1, op, r, w):
        eng = self._e(eng)
        self.P.add(eng, lambda e: e.tensor_tensor(out=out, in0=in0, in1=in1, op=op), r, w)

    def ts(self, eng, out, in0, s1, s2, op0, op1, r, w):
        eng = self._e(eng)
        self.P.add(eng, lambda e: e.tensor_scalar(out=out, in0=in0, scalar1=s1, scalar2=s2, op0=op0, op1=op1), r, w)

    def ts1(self, eng, out, in_, s, op, r, w):
        eng = self._e(eng)
        self.P.add(eng, lambda e: e.tensor_single_scalar(out=out, in_=in_, scalar=s, op=op), r, w)

    def stt(self, eng, out, in0, scalar, in1, op0, op1, r, w):
        eng = self._e(eng)
        self.P.add(eng, lambda e: e.scalar_tensor_tensor(out=out, in0=in0, scalar=scalar, in1=in1, op0=op0, op1=op1), r, w)

    def cp(self, eng, out, in_, r, w):
        eng = self._e(eng)
        if eng == "scalar":
            self.P.add(eng, lambda e: e.copy(out=out, in_=in_), r, w)
        else:
            self.P.add(eng, lambda e: e.tensor_copy(out=out, in_=in_), r, w)

    def red(self, out, in_, r, w):
        self.P.add("vector", lambda e: e.reduce_sum(out=out, in_=in_, axis=AX.X), r, w)

    def recip(self, out, in_, r, w):
        self.P.add("vector", lambda e: e.reciprocal(out=out, in_=in_), r, w)

    def memset(self, eng, ap, val, w):
        eng = self._e(eng)
        self.P.add(eng, lambda e: e.memset(ap, val), (), w)

    def rsqrt(self, out, in_, scale, r, w):
        self.act(out, in_, AF.Ln, r, w, bias=self.eps_ap[0:out.shape[0], :], scale=scale)
        self.act(out, out, AF.Exp, w, w, scale=-0.5)


def _gammas():
    return 1.0 - 2.0 ** (-5.0 - np.arange(4, dtype=np.float64))


def host_consts(S, NB, PAST):
    RS = NB * 4
    c = {}
    c["ident"] = np.eye(128, dtype=np.float32).astype(NPBF)
    c["identf"] = np.eye(128, dtype=np.float32)

    def ropeN(pos):
        inv = 500000.0 ** (-np.arange(8, dtype=np.float32) / 8)
        ang = pos.astype(np.float32)[:, None] * inv[None, :]
        cs, sn = np.cos(ang), np.sin(ang)
        return np.concatenate([cs, cs, -sn, sn], axis=1).astype(np.float32)

    def ropeR(pos):
        inv = 10000.0 ** (-np.arange(32, dtype=np.float32) / 32)
        ang = pos.astype(np.float32)[:, None] * inv[None, :]
        cs, sn = np.cos(ang), np.sin(ang)
        return np.concatenate([cs, cs, -sn, sn], axis=1).astype(np.float32)

    pos_p = np.arange(S)
    pos_s = PAST + (np.arange(RS) % 4)
    c["ropeN_p"], c["ropeR_p"] = ropeN(pos_p), ropeR(pos_p)
    c["ropeN_s"], c["ropeR_s"] = ropeN(pos_s), ropeR(pos_s)
    k = np.arange(128)[:, None]
    q = np.arange(128)[None, :]
    causal = np.where(k <= q, 0.0, NEGM).astype(np.float32)
    winfirst = np.where(k > q, 0.0, NEGM).astype(np.float32)
    c["causal4"] = np.tile(causal, (1, 4)).astype(NPBF)
    c["winfirst4"] = np.tile(winfirst, (1, 4)).astype(NPBF)
    kk = np.arange(8)[:, None]
    base = np.where(q >= 16 * kk + 15, 0.0, NEGM).astype(np.float32)
    c["cbase4"] = np.tile(base, (1, 4)).astype(NPBF)
    sh = np.zeros((8, 272), np.float32)
    for i in range(8):
        sh[i, i + 128] = 1.0
    c["shid"] = sh.astype(NPBF)
    E = np.zeros((64, 4096), np.float32)
    for j in range(64):
        E[j, 64 * j:64 * j + 64] = 1.0
    c["E"] = E.astype(NPBF)
    ov = np.zeros((128, 4, 128), np.float32)
    for cc in range(512):
        for j in range(128):
            lo, hi = max(16 * cc, 64 * j), min(16 * cc + 32, 64 * j + 64)
            if hi > lo:
                ov[cc % 128, cc // 128, j] = (hi - lo) / 32.0
    c["ov"] = ov.reshape(128, 512).astype(NPBF)
    g = _gammas()
    i_ = np.arange(128)
    dec = np.zeros((128, 4, 128), np.float64)
    for h in range(4):
        d = i_[None, :] - i_[:, None]
        dec[:, h, :] = np.where(d >= 0, g[h] ** np.maximum(d, 0), 0.0) / 8.0
    c["decT"] = dec.reshape(128, 512).astype(np.float32)
    cross = np.zeros((64, 4, 128), np.float64)
    for h in range(4):
        cross[:, h, :] = (g[h] ** (i_ + 1.0))[None, :]
    c["crossT"] = cross.reshape(64, 512).astype(np.float32)
    c["wtab"] = np.stack([g[h] ** (127.0 - i_) / 8.0 for h in range(4)], axis=1).astype(np.float32)
    pat = np.zeros((128, 3), np.float32)
    pat[:64, 0] = 1e4
    pat[:, 1] = 1e4
    pat[64:, 2] = 1e4
    c["pat"] = pat
    c["iota"] = np.arange(128, dtype=np.float32).reshape(128, 1)
    s4 = np.arange(4)
    scaus = np.where(s4[:, None] <= s4[None, :], 0.0, NEGM).astype(np.float32)
    c["scausal4"] = np.tile(scaus, (1, 4)).astype(NPBF)
    swf = np.zeros((128, 4), np.float32)
    for w in range(4):
        for s in range(4):
            swf[w, s] = 0.0 if w > s else NEGM
    c["swinfirst4"] = np.tile(swf, (1, 4)).astype(NPBF)
    dec4 = np.zeros((4, 4, 4), np.float64)
    for h in range(4):
        d = s4[None, :] - s4[:, None]
        dec4[:, h, :] = np.where(d >= 0, g[h] ** np.maximum(d, 0), 0.0) / 8.0
    c["dec4T"] = dec4.reshape(4, 16).astype(np.float32)
    cross4 = np.zeros((64, 4, 4), np.float64)
    for h in range(4):
        cross4[:, h, :] = (g[h] ** (s4 + 1.0))[None, :]
    c["cross4T"] = cross4.reshape(64, 16).astype(np.float32)
    c["w4tab"] = np.stack([g[h] ** (3.0 - s4) / 8.0 for h in range(4)], axis=1).astype(np.float32)
    spat = np.zeros((4, 2), np.float32)
    spat[:, 0] = 1e4
    spat[:, 1] = 1e4
    c["spat"] = spat
    return c


CONST_SHAPES = None


import os
DBG = int(os.environ.get('K_DBG', '9'))


def build(S, NB, PAST, NPOOL, do_sample=True):
    NT = S // 128
    NPG = PAST // 128
    RS = NB * 4
    NCS = PAST // 16 - 1
    nc = bass.Bass("TRN2", target_bir_lowering=False)
    consts = host_consts(S, NB, PAST)

    def din(name, shape, dt=F32):
        return nc.dram_tensor(name, list(shape), dt, kind="ExternalInput").ap()

    def dout(name, shape):
        return nc.dram_tensor(name, list(shape), F32, kind="ExternalOutput").ap()

    cd = {}
    for k, v in consts.items():
        cd[k] = din("c_" + k, v.shape, BF16 if v.dtype == NPBF else F32)
    x_p = din("x_p", [S, 1024])
    x_s = din("x_s", [RS, 1024])
    c_pg = din("c_pg", [NPOOL * 128, 512])
    st_winK = din("st_winK", [NB, 128, 512])
    st_winV = din("st_winV", [NB, 512, 128])
    st_win = din("st_win", [NB, 512, 256])
    st_ret = din("st_ret", [64, NB * 512])
    st_conv = din("st_conv", [128, NFC * NB * 2])
    ptab = din("ptab", [1, NB * NPG], I32)
    nm_mix = din("nm_mix", [128, 8])
    w_in = din("w_in", [1024, D_IN])
    gq = din("gq", [128, 64])
    gk = din("gk", [128, 192])
    w1h = din("w1h", [128, 4096])
    w2h = din("w2h", [128, 128 + 128])
    peT = din("peT", [128, 64])
    w_out = din("w_out", [1024, 1024])
    nm_ffn = din("nm_ffn", [128, 8])
    w_fi = din("w_fi", [1024, 2 * D_FF])
    wconv = din("wconv", [128, NFC * 4])
    w_fo = din("w_fo", [D_FF, 1024])

    y_p = dout("y_p", [S, 1024])
    y_s = dout("y_s", [RS, 1024])
    o_kvc_p = dout("o_kvc_p", [S, 256])
    o_kvs_p = dout("o_kvs_p", [S, 256])
    o_kvw_p = dout("o_kvw_p", [512, 256])
    o_kvc_s = dout("o_kvc_s", [RS, 256])
    o_kvs_s = dout("o_kvs_s", [RS, 256])
    o_kvw_s = dout("o_kvw_s", [NB, 512, 256])
    o_ret_p = dout("o_ret_p", [64, 512])
    o_ret_s = dout("o_ret_s", [64, NB * 512])
    o_conv_p = dout("o_conv_p", [128, NFC * 2])
    o_conv_s = dout("o_conv_s", [128, NFC * NB * 2])

    P = Prog()
    with ExitStack() as es:
        B = Bld(nc, P, es)
        sb, ps = B.sb, B.ps
        Sb = [ps("S0", [128, 512], F32), ps("S1", [128, 512], F32)]
        Ab = [ps("A0", [128, 512], F32), ps("A1", [128, 512], F32), ps("A2", [128, 512], F32)]
        T0 = ps("T0", [128, 1024], BF16)
        Mb = [ps("M0", [128, 512], F32), ps("M1", [128, 512], F32)]
        ct = {}
        for k, v in consts.items():
            if k in ("ropeN_p", "ropeR_p", "ropeN_s", "ropeR_s", "identf"):
                continue
            ct[k] = sb("k_" + k, list(v.shape), BF16 if v.dtype == NPBF else F32)
            B.dma(ct[k][:], cd[k], [], ["k_" + k], "k_" + k)
        ident = ct["ident"]
        epsc = sb("epsc", [128, 1], F32)
        B.memset("vector", epsc[:], EPS, ["epsc"])
        B.eps_ap = epsc
        w_in_bf = sb("w_in_bf", [128, 8, D_IN], BF16)
        w_out_bf = sb("w_out_bf", [128, 8, 1024], BF16)
        W1 = sb("W1", [128, 2, 32, 64], BF16)
        w2 = sb("w2", [128, 256], BF16)
        peT_bf = sb("peT_bf", [128, 64], BF16)
        peterm = sb("peterm", [128, 2], F32)
        nm_sb = sb("nm_sb", [128, 8], F32)
        gq_sb = sb("gq_sb", [128, 64], F32)
        gk_sb = sb("gk_sb", [128, 192], F32)
        stage = [sb("stage0", [128, 1024], F32), sb("stage1", [128, 1024], F32)]
        B.dma(nm_sb[:], nm_mix, [], ["nm_sb"], "nm_sb")
        B.dma(gq_sb[:], gq, [], ["gq_sb"], "gq_sb")
        B.dma(gk_sb[:], gk, [], ["gk_sb"], "gk_sb")
        B.ts1("vector", gq_sb[:], gq_sb[:], 0.125, ALU.mult, ["gq_sb"], ["gq_sb"])
        si = 0
        for k in range(8):
            for c0 in range(0, D_IN, 1024):
                s_ = si % 2
                c1 = min(D_IN, c0 + 1024)
                B.dma(stage[s_][:, 0:c1 - c0], w_in[k * 128:(k + 1) * 128, c0:c1], [], ["stage%d" % s_], "stage%d" % s_)
                eng = "vector" if si % 2 == 0 else "gpsimd"
                B.ts1(eng, w_in_bf[:, k, c0:c1], stage[s_][:, 0:c1 - c0], nm_sb[:, k:k + 1], ALU.mult,
                      ["stage%d" % s_, "nm_sb"], ["w_in_bf"])
                si += 1
        for k in range(8):
            s_ = si % 2
            B.dma(stage[s_][:, 0:1024], w_out[k * 128:(k + 1) * 128, :], [], ["stage%d" % s_], "stage%d" % s_)
            B.cp("vector" if k % 2 == 0 else "gpsimd", w_out_bf[:, k, :], stage[s_][:, 0:1024], ["stage%d" % s_], ["w_out_bf"])
            si += 1
        for k in range(4):
            s_ = si % 2
            B.dma(stage[s_][:, 0:1024], w1h[:, k * 1024:(k + 1) * 1024], [], ["stage%d" % s_], "stage%d" % s_)
            B.cp("vector", W1[:, k // 2, (k % 2) * 16:(k % 2) * 16 + 16, :].rearrange("p l o -> p (l o)"), stage[s_][:, 0:1024],
                 ["stage%d" % s_], ["W1"])
            si += 1
        s_ = si % 2
        B.dma(stage[s_][:, 0:256], w2h, [], ["stage%d" % s_], "stage%d" % s_)
        B.cp("vector", w2[:, :], stage[s_][:, 0:256], ["stage%d" % s_], ["w2"])
        si += 1
        s_ = si % 2
        B.dma(stage[s_][:, 0:64], peT, [], ["stage%d" % s_], "stage%d" % s_)
        B.cp("vector", peT_bf[:], stage[s_][:, 0:64], ["stage%d" % s_], ["peT_bf"])
        si += 1
        for hp in range(0 if 'pe' in SKIP else 2):
            for k in range(2):
                for l in range(32):
                    B.mm(Mb[0][64 * hp:64 * hp + 64, k:k + 1], W1[64 * hp:64 * hp + 64, k, l, :],
                         peT_bf[64 * hp:64 * hp + 64, k * 32 + l:k * 32 + l + 1], l == 0, l == 31,
                         ["W1", "peT_bf"], ["M0"])
        if "pe" not in SKIP:
            B.cp("vector", peterm[:], Mb[0][:, 0:2], ["M0"], ["peterm"])

        LK = max(S, PAST)
        ksT = sb("ksT", [128, LK + 4], BF16)
        vs = sb("vs", [128, LK // 128 + 1, 2, 65], BF16)
        kwT = sb("kwT", [128, 8, 128], BF16)
        vw = sb("vw", [128, 8, 2, 65], BF16)
        kcT = sb("kcT", [128, 512], BF16)
        hgT = sb("hgT", [128, 2, 512], BF16)
        vc = sb("vc", [128, 4, 2, 65], BF16)
        cmpT = sb("cmpT", [128, 2, 16 + 1024], BF16)
        if "ms" not in SKIP:
            B.memset("vector", vs[:, :, :, 64:65], 1.0, ["vs"])
            B.memset("vector", vw[:, :, :, 64:65], 1.0, ["vw"])
            B.memset("vector", vc[:, :, :, 64:65], 1.0, ["vc"])
        Sf = sb("Sf", [64, 4, 128], F32)
        Sbf = sb("Sbf", [64, 4, 128], BF16)
        xs_ = [sb("xs0", [128, 1024], F32), stage[0]]
        xres = ["xs0", "stage0"]
        ropeN = sb("ropeN", [128, 32], F32)
        ropeR = sb("ropeR", [128, 128], F32)
        st1 = sb("st1", [128, 16], F32)
        junk = sb("junk", [128, 1024], BF16)
        xn = sb("xn", [128, 1024], BF16)
        xnT = sb("xnT", [128, 8, 128], BF16)
        qn = sb("qn", [128, 512], F32)
        q_bf = sb("q_bf", [128, 8, 128], BF16)
        B.memset("vector", q_bf[:], 0.0, ["q_bf"])
        qT = sb("qT", [128, 8, 128], BF16)
        kvo = sb("kvo", [128, 768], F32)
        k_bf = sb("k_bf", [128, 3, 128], BF16)
        cv_bf = sb("cv_bf", [128, 128], BF16)
        rtmp = sb("rtmp", [128, 256], F32)
        rtmp2 = sb("rtmp2", [128, 256], F32)
        rqk_bf = sb("rqk_bf", [128, 512], BF16)
        rqT = sb("rqT", [64, 4, 128], BF16)
        rkT = sb("rkT", [64, 4, 128], BF16)
        rv_bf = sb("rv_bf", [128, 512], BF16)
        rg_f = sb("rg_f", [128, 512], F32)
        oret = sb("oret", [128, 512], F32)
        rg_e = oret

        gates_l = [sb("gates0", [128, 24], F32), sb("gates1", [128, 24], F32)]
        ebuf = [sb("e0", [128, 512], BF16), sb("e1", [128, 512], BF16)]
        hx = sb("hx", [128, 32], F32)
        hg_bf = sb("hg_bf", [128, 2, 128], BF16)
        hwork = sb("hwork", [128, 3, 256], F32)
        ocmp = sb("ocmp", [128, 4, 64], F32)
        osel = sb("osel", [128, 4, 64], F32)
        owin = sb("owin", [128, 4, 64], F32)
        rs_c = sb("rs_c", [128, 12], F32)
        imp = sb("imp", [128, 128], F32)
        imp2 = sb("imp2", [128, 128], F32)
        mx = sb("mx", [128, 16], F32)
        negq = sb("negq", [128, 128], BF16)
        negT4 = sb("negT4", [64, 2, 512], BF16)
        mixed = sb("mixed", [128, 1024], BF16)
        mixT = xnT
        sTm = sb("sTm", [128, 512], BF16)
        qcr = sb("qcr", [64, 4, 128], BF16)
        kwk = sb("kwk", [128, 256], BF16)


        B.memset("vector", Sf[:], 0.0, ["Sf"])
        B.memset("vector", Sbf[:], 0.0, ["Sbf"])
        B.memset("vector", cmpT[:, :, 0:16], 0.0, ["cmpT"])

        ecnt = [0]
        scnt = [0]

        def project(R, x_dram, xslot, ropeN_d, ropeR_d, kv_douts, tg, gslot=0):
            gates = gates_l[gslot]
            gres = "gates%d" % gslot
            xr = xres[xslot]
            x_sb = xs_[xslot]
            B.dma(x_sb[0:R, :], x_dram, [], [xr], xr)
            B.dma(ropeN[0:R, :], ropeN_d, [], ["ropeN"], "ropeN")
            B.dma(ropeR[0:R, :], ropeR_d, [], ["ropeR"], "ropeR")
            B.act(junk[0:R, :], x_sb[0:R, :], AF.Square, [xr], ["junk", "st1"], accum=st1[0:R, 0:1])
            B.rsqrt(st1[0:R, 1:2], st1[0:R, 0:1], 1.0 / 1024, ["st1"], ["st1"])
            yield
            B.act(xn[0:R, :], x_sb[0:R, :], AF.Identity, [xr, "st1"], ["xn"], scale=st1[0:R, 1:2])
            yield
            for k in range(8):
                B.tr(T0[:, k * 128:k * 128 + R], xn[0:R, k * 128:(k + 1) * 128], ident[0:R, 0:R], ["xn", "k_ident"], ["T0"])
            B.cp("vector", xnT[:, :, 0:R], T0[:, :].rearrange("p (k t) -> p k t", k=8)[:, :, 0:R], ["T0"], ["xnT"])
            yield
            def zchunk(n, bank):
                c0 = n * 512
                w = min(512, D_IN - c0)
                for k in range(8):
                    B.mm(Mb[bank][0:R, 0:w], xnT[:, k, 0:R], w_in_bf[:, k, c0:c0 + w], k == 0, k == 7,
                         ["xnT", "w_in_bf"], ["M%d" % bank])
                return w
            zchunk(0, 0)
            zchunk(1, 1)
            zq = Mb[0]
            yield
            B.act(junk[0:R, 0:512], zq[0:R, :], AF.Square, ["M0"], ["junk"])
            B.red(st1[0:R, 2:10], junk[0:R, 0:512].rearrange("p (h d) -> p h d", h=8), ["junk"], ["st1"])
            B.rsqrt(st1[0:R, 2:10], st1[0:R, 2:10], 1.0 / 64, ["st1"], ["st1"])
            qn3 = qn[0:R, :].rearrange("p (h d) -> p h d", h=8)
            B.tt("vector", qn3, zq[0:R, :].rearrange("p (h d) -> p h d", h=8),
                 st1[0:R, 2:10].unsqueeze(2).broadcast_to([R, 8, 64]), ALU.mult, ["M0", "st1"], ["qn"])
            B.tt("gpsimd", qn3, qn3, gq_sb[0:R, :].unsqueeze(1).broadcast_to([R, 8, 64]), ALU.mult, ["qn", "gq_sb"], ["qn"])
            yield
            rope16(R, qn3, 8, "qn")
            yield
            for g_ in range(2):
                B.cp("scalar", q_bf[0:R, 4 * g_:4 * g_ + 4, 64 * g_:64 * g_ + 64], qn3[:, 4 * g_:4 * g_ + 4, :], ["qn"], ["q_bf"])
            B.cp("scalar", kvo[0:R, 0:512], Mb[1][0:R, :], ["M1"], ["kvo"])
            yield
            zchunk(2, 0)
            B.cp("scalar", kvo[0:R, 512:768], Mb[0][0:R, 0:256], ["M0"], ["kvo"])
            ropeR64(R, Mb[0][0:R, 256:512], rqk_bf[0:R, 0:256], "M0")
            yield
            zchunk(3, 1)
            if "c3r" not in SKIP:
                ropeR64(R, Mb[1][0:R, 0:256], rqk_bf[0:R, 256:512], "M1")
            if "c3a" not in SKIP:
                B.cp("scalar", rv_bf[0:R, 0:256], Mb[1][0:R, 256:512], ["M1"], ["rv_bf"])
            yield
            zchunk(4, 0)
            B.cp("scalar", rv_bf[0:R, 256:512], Mb[0][0:R, 0:256], ["M0"], ["rv_bf"])
            B.cp("vector", rg_f[0:R, 0:256], Mb[0][0:R, 256:512], ["M0"], ["rg_f"])
            B.act(rg_e[0:R, 0:256], Mb[0][0:R, 256:512], AF.Exp, ["M0"], ["oret"], scale=-1.0)
            yield
            zchunk(5, 1)
            B.cp("vector", rg_f[0:R, 256:512], Mb[1][0:R, 0:256], ["M1"], ["rg_f"])
            B.act(rg_e[0:R, 256:512], Mb[1][0:R, 0:256], AF.Exp, ["M1"], ["oret"], scale=-1.0)
            B.act(gates[0:R, :], Mb[1][0:R, 256:280], AF.Exp, ["M1"], [gres], scale=-1.0)
            B.ts1("vector", gates[0:R, :], gates[0:R, :], 1.0, ALU.add, [gres], [gres])
            B.recip(gates[0:R, :], gates[0:R, :], [gres], [gres])
            yield
            B.ts1("gpsimd", rg_e[0:R, :], rg_e[0:R, :], 1.0, ALU.add, ["oret"], ["oret"])
            B.recip(rg_e[0:R, :], rg_e[0:R, :], ["oret"], ["oret"])
            B.tt("gpsimd", rg_f[0:R, :], rg_f[0:R, :], rg_e[0:R, :], ALU.mult, ["rg_f", "oret"], ["rg_f"])
            yield
            k4 = kvo[0:R, :].rearrange("p (b v g d) -> p b v g d", b=3, v=2, g=2)[:, :, 0, :, :]
            B.act(junk[0:R, 0:384].rearrange("p (b g d) -> p b g d", b=3, g=2), k4, AF.Square, ["kvo"], ["junk"])
            B.red(st1[0:R, 10:16], junk[0:R, 0:384].rearrange("p (h d) -> p h d", h=6), ["junk"], ["st1"])
            B.rsqrt(st1[0:R, 10:16], st1[0:R, 10:16], 1.0 / 64, ["st1"], ["st1"])
            B.tt("vector", k4, k4, st1[0:R, 10:16].rearrange("p (b g) -> p b g", b=3).unsqueeze(3).broadcast_to([R, 3, 2, 64]),
                 ALU.mult, ["kvo", "st1"], ["kvo"])
            B.tt("gpsimd", k4, k4, gk_sb[0:R, :].rearrange("p (b d) -> p b d", b=3).unsqueeze(2).broadcast_to([R, 3, 2, 64]),
                 ALU.mult, ["kvo", "gk_sb"], ["kvo"])
            for b_ in range(3):
                rope16(R, k4[:, b_, :, :], 2, "kvo")
            yield
            for i_, d_ in enumerate(kv_douts):
                if d_ is not None:
                    B.dma(d_, kvo[0:R, i_ * 256:(i_ + 1) * 256], ["kvo"], [], "kvo_out")
            B.cp("vector", k_bf[0:R, :, :].rearrange("p b (g d) -> p b g d", g=2), k4, ["kvo"], ["k_bf"])
            B.cp("gpsimd", cv_bf[0:R, :], kvo[0:R, 128:256], ["kvo"], ["cv_bf"])

        def rope16(R, x3, H, res):
            C = ropeN[0:R, 0:16].unsqueeze(1).broadcast_to([R, H, 16])
            nsn = ropeN[0:R, 16:24].unsqueeze(1).broadcast_to([R, H, 8])
            psn = ropeN[0:R, 24:32].unsqueeze(1).broadcast_to([R, H, 8])
            t1 = rtmp[0:R, 0:H * 16].rearrange("p (h d) -> p h d", h=H)
            t2 = rtmp2[0:R, 0:H * 16].rearrange("p (h d) -> p h d", h=H)
            B.tt("vector", t1, x3[:, :, 0:16], C, ALU.mult, [res, "ropeN"], ["rtmp"])
            B.tt("gpsimd", t2[:, :, 0:8], x3[:, :, 8:16], nsn, ALU.mult, [res, "ropeN"], ["rtmp2"])
            B.tt("gpsimd", t2[:, :, 8:16], x3[:, :, 0:8], psn, ALU.mult, [res, "ropeN"], ["rtmp2"])
            B.tt("vector", x3[:, :, 0:16], t1, t2, ALU.add, ["rtmp", "rtmp2"], [res])

        def ropeR64(R, zsrc, dst_bf, res):
            H = 4
            z3 = zsrc.rearrange("p (h d) -> p h d", h=H)
            C = ropeR[0:R, 0:64].unsqueeze(1).broadcast_to([R, H, 64])
            nsn = ropeR[0:R, 64:96].unsqueeze(1).broadcast_to([R, H, 32])
            psn = ropeR[0:R, 96:128].unsqueeze(1).broadcast_to([R, H, 32])
            t1 = rtmp[0:R, 0:256].rearrange("p (h d) -> p h d", h=H)
            t2 = rtmp2[0:R, 0:256].rearrange("p (h d) -> p h d", h=H)
            B.tt("vector", t1, z3, C, ALU.mult, [res, "ropeR"], ["rtmp"])
            B.tt("vector", t2[:, :, 0:32], z3[:, :, 32:64], nsn, ALU.mult, [res, "ropeR"], ["rtmp2"])
            B.tt("vector", t2[:, :, 32:64], z3[:, :, 0:32], psn, ALU.mult, [res, "ropeR"], ["rtmp2"])
            B.tt("gpsimd", dst_bf.rearrange("p (h d) -> p h d", h=H), t1, t2, ALU.add, ["rtmp", "rtmp2"], ["rqk_bf"])

        def compress(ncols, first_c, skip_first):
            n0 = 1 if skip_first else 0
            nb = ncols - n0
            c0 = first_c + n0
            for kv in range(2):
                for g in range(2):
                    i = 0
                    for a in range(2):
                        for s in range(16):
                            l = 16 * a + s
                            off = 16 * (n0 + a) + s
                            rhs = cmpT[64 * g:64 * g + 64, kv, off:off + 16 * (nb - 1) + 1:16]
                            B.mm(Ab[2][64 * g:64 * g + 64, kv * 128:kv * 128 + nb], W1[64 * g:64 * g + 64, kv, l, :], rhs,
                                 i == 0, i == 31, ["W1", "cmpT"], ["A2"])
                            i += 1
            if 'cg' in SKIP:
                return
            hw = hwork
            for kv in range(2):
                B.act(hw[:, 0, kv * 128:kv * 128 + nb], Ab[2][:, kv * 128:kv * 128 + nb], AF.Identity, ["A2", "peterm"], ["hwork"],
                      bias=peterm[:, kv:kv + 1])
            xx = hw[:, 0, :].rearrange("p (k c) -> p k c", k=2)[:, :, 0:nb]
            t_ = hw[:, 1, :].rearrange("p (k c) -> p k c", k=2)[:, :, 0:nb]
            u_ = hw[:, 2, :].rearrange("p (k c) -> p k c", k=2)[:, :, 0:nb]
            B.tt("vector", t_, xx, xx, ALU.mult, ["hwork"], ["hwork"])
            B.ts("vector", t_, t_, 0.044715, 1.0, ALU.mult, ALU.add, ["hwork"], ["hwork"])
            B.tt("vector", t_, t_, xx, ALU.mult, ["hwork"], ["hwork"])
            B.act(u_, t_, AF.Exp, ["hwork"], ["hwork"], scale=-2.0 * math.sqrt(2.0 / math.pi))
            B.ts1("vector", u_, u_, 1.0, ALU.add, ["hwork"], ["hwork"])
            B.recip(u_, u_, ["hwork"], ["hwork"])
            B.tt("vector", hgT[:, :, c0:c0 + nb], xx, u_, ALU.mult, ["hwork"], ["hgT"])
            if 'ck' in SKIP:
                return
            B.mm(Ab[2][:, 256:256 + nb], w2[:, 0:128], hgT[:, 0, c0:c0 + nb], True, True, ["w2", "hgT"], ["A2"])
            B.cp("vector", kcT[:, c0:c0 + nb], Ab[2][:, 256:256 + nb], ["A2"], ["kcT"])
            if 'cv' in SKIP:
                return
            c_last = c0 + nb - 1
            for ctile in range(c0 // 128, c_last // 128 + 1):
                M = min(128, c_last + 1 - 128 * ctile)
                B.mm(Ab[2][0:M, 0:128], hgT[:, 1, 128 * ctile:128 * ctile + M], w2[:, 128:256], True, True, ["w2", "hgT"], ["A2"])
                B.cp("vector", vc[0:M, ctile, :, 0:64], Ab[2][0:M, 0:128].rearrange("p (g d) -> p g d", g=2), ["A2"], ["vc"])

        def emask(kt, QN):
            return (ct["E"][:, (kt % 32) * 128:(kt % 32 + 1) * 128], negT4[:, kt // 32, 0:4 * QN], ["k_E", "negT4"])

        def branch(QN, qrhs, qres, tiles, acc, accres, W, tagU=None):
            n = len(tiles)
            NQ = 4 * QN
            CH = max(1, 512 // NQ)
            first = True
            for c0 in range(0, n, CH):
                chunk = tiles[c0:c0 + CH]
                sbk = scnt[0] % 2
                scnt[0] += 1
                Sx = Sb[sbk]
                sres = "S%d" % sbk
                Mmax = max(tl["M"] for tl in chunk)
                for j, tl in enumerate(chunk):
                    M = tl["M"]
                    masks = tl.get("masks", [])
                    B.mm(Sx[0:M, j * NQ:(j + 1) * NQ], tl["kT"], qrhs, True, len(masks) == 0, [tl["kres"], qres], [sres])
                    for mi, (ml, mr, mres) in enumerate(masks):
                        B.mm(Sx[0:M, j * NQ:(j + 1) * NQ], ml, mr, False, mi == len(masks) - 1, mres, [sres])
                eb = ecnt[0] % 2
                ecnt[0] += 1
                eres = "e%d" % eb
                B.act(ebuf[eb][0:Mmax, 0:len(chunk) * NQ], Sx[0:Mmax, 0:len(chunk) * NQ], AF.Exp, [sres], [eres])
                for j, tl in enumerate(chunk):
                    M = tl["M"]
                    last = (c0 + j == n - 1)
                    for r in range(4):
                        B.mm(acc[:, r, :], ebuf[eb][0:M, j * NQ + r * QN:j * NQ + (r + 1) * QN], tl["v"], first and r == 0, last,
                             [eres, tl["vres"]], [accres], nocheck=True)
                    if tagU is not None:
                        for r in range(4):
                            B.mm(tagU[0][:, r, :], ebuf[eb][0:M, j * NQ + r * QN:j * NQ + (r + 1) * QN], tl["u"], first and r == 0, last,
                                 [eres, "k_ov"], [tagU[1]], nocheck=True)
                    first = False
                yield

        def finish_branch(QN, acc, accres, odst, ores, col):
            B.ts1("vector", rs_c[0:QN, col:col + 4], acc[:, :, 64], 1e-30, ALU.max, [accres], ["rs_c"])
            B.recip(rs_c[0:QN, col:col + 4], rs_c[0:QN, col:col + 4], ["rs_c"], ["rs_c"])
            B.tt("vector", odst[0:QN, :, :], acc[:, :, 0:64], rs_c[0:QN, col:col + 4].unsqueeze(2).broadcast_to([QN, 4, 64]),
                 ALU.mult, [accres, "rs_c"], [ores])

        def combine(QN, g, gates_ap, gres="gates0"):
            g3 = gates_ap.rearrange("p (h b) -> p h b", b=3)[:, 4 * g:4 * g + 4, :]
            for bi, (o_, ores) in enumerate(((ocmp, "ocmp"), (osel, "osel"), (owin, "owin"))):
                B.tt("vector" if bi != 1 else "gpsimd", o_[0:QN, :, :], o_[0:QN, :, :],
                     g3[:, :, bi:bi + 1].broadcast_to([QN, 4, 64]), ALU.mult, [ores, gres], [ores])
            B.tt("vector", ocmp[0:QN, :, :], ocmp[0:QN, :, :], osel[0:QN, :, :], ALU.add, ["ocmp", "osel"], ["ocmp"])
            B.tt("vector", mixed[0:QN, g * 256:(g + 1) * 256].rearrange("p (r d) -> p r d", r=4), ocmp[0:QN, :, :], owin[0:QN, :, :],
                 ALU.add, ["ocmp", "owin"], ["mixed"])

        def select_blocks(QN, accU, ures, rs_col, bonus_fn, kth):
            B.ts1("vector", imp[0:QN, :], accU[:, 0, :], rs_c[0:QN, rs_col:rs_col + 1], ALU.mult, [ures, "rs_c"], ["imp"])
            for r in range(1, 4):
                B.stt("vector", imp[0:QN, :], accU[:, r, :], rs_c[0:QN, rs_col + r:rs_col + r + 1], imp[0:QN, :],
                      ALU.mult, ALU.add, [ures, "rs_c", "imp"], ["imp"])
            bonus_fn()
            B.P.add("vector", lambda e: e.max(out=mx[0:QN, 0:8], in_=imp[0:QN, :]), ["imp"], ["mx"])
            B.P.add("vector", lambda e: e.match_replace(out=imp2[0:QN, :], in_to_replace=mx[0:QN, 0:8], in_values=imp[0:QN, :],
                                                        imm_value=-1e30), ["imp", "mx"], ["imp2"])
            B.P.add("vector", lambda e: e.max(out=mx[0:QN, 8:16], in_=imp2[0:QN, :]), ["imp2"], ["mx"])
            B.ts1("vector", imp2[0:QN, :], imp[0:QN, :], mx[0:QN, 8 + kth - 9:8 + kth - 8], ALU.is_ge, ["imp", "mx"], ["imp2"])
            B.ts("vector", negq[0:QN, :], imp2[0:QN, :], -1.0, -NEGM, ALU.add, ALU.mult, ["imp2"], ["negq"])
            for hf in range(2):
                B.tr(T0[0:64, hf * 128:hf * 128 + QN], negq[0:QN, hf * 64:(hf + 1) * 64], ident[0:QN, 0:QN], ["negq", "k_ident"], ["T0"])
            for hf in range(2):
                B.cp("vector", negT4[:, hf, 0:4 * QN].rearrange("p (r q) -> p r q", r=4),
                     T0[0:64, hf * 128:hf * 128 + QN].unsqueeze(1).broadcast_to([64, 4, QN]), ["T0"], ["negT4"])

        def retention(R, t):
            gam = [1.0 - 2.0 ** (-5.0 - h) for h in range(4)]
            for h in range(4):
                B.mm(Mb[0][0:R, h * 128:h * 128 + R], rkT[:, h, 0:R], rqT[:, h, 0:R], True, True, ["rkT", "rqT"], ["M0"])
            B.tt("vector", sTm[0:R, :].rearrange("p (h i) -> p h i", h=4)[:, :, 0:R],
                 Mb[0][0:R, :].rearrange("p (h i) -> p h i", h=4)[:, :, 0:R],
                 ct["decT"][0:R, :].rearrange("p (h i) -> p h i", h=4)[:, :, 0:R], ALU.mult, ["M0", "k_decT"], ["sTm"])
            B.tt("gpsimd", qcr[:, :, 0:R], rqT[:, :, 0:R],
                 ct["crossT"][:, :].rearrange("p (h i) -> p h i", h=4)[:, :, 0:R], ALU.mult, ["rqT", "k_crossT"], ["qcr"])
            for h in range(4):
                B.mm(Mb[1][0:R, h * 128:(h + 1) * 128], sTm[0:R, h * 128:h * 128 + R], rv_bf[0:R, h * 128:(h + 1) * 128],
                     True, False, ["sTm", "rv_bf"], ["M1"])
                B.mm(Mb[1][0:R, h * 128:(h + 1) * 128], qcr[:, h, 0:R], Sbf[:, h, :], False, True, ["qcr", "Sbf"], ["M1"])
            B.tt("vector", kwk[0:R, :].rearrange("p (h d) -> p h d", h=4), rqk_bf[0:R, 256:512].rearrange("p (h d) -> p h d", h=4),
                 ct["wtab"][0:R, :].unsqueeze(2).broadcast_to([R, 4, 64]), ALU.mult, ["rqk_bf", "k_wtab"], ["kwk"])
            for h in range(4):
                B.mm(Mb[0][0:64, h * 128:(h + 1) * 128], kwk[0:R, h * 64:(h + 1) * 64],
                     rv_bf[0:R, h * 128:(h + 1) * 128], True, True, ["kwk", "rv_bf", "sTm"], ["M0"])
            for h in range(4):
                B.stt("vector", Sf[:, h, :], Sf[:, h, :], float(gam[h] ** 128), Mb[0][0:64, h * 128:(h + 1) * 128],
                      ALU.mult, ALU.add, ["Sf", "M0"], ["Sf"])
            B.cp("gpsimd", Sbf[:, :, :], Sf[:, :, :], ["Sf"], ["Sbf"])
            B.act(junk[0:R, 0:512], Mb[1][0:R, :], AF.Square, ["M1"], ["junk"])
            B.red(st1[0:R, 2:6], junk[0:R, 0:512].rearrange("p (h e) -> p h e", h=4), ["junk"], ["st1"])
            B.rsqrt(st1[0:R, 2:6], st1[0:R, 2:6], 1.0 / 128, ["st1"], ["st1"])
            B.tt("vector", oret[0:R, :].rearrange("p (h e) -> p h e", h=4), Mb[1][0:R, :].rearrange("p (h e) -> p h e", h=4),
                 st1[0:R, 2:6].unsqueeze(2).broadcast_to([R, 4, 128]), ALU.mult, ["M1", "st1"], ["oret"])
            B.tt("gpsimd", mixed[0:R, 512:1024], oret[0:R, :], rg_f[0:R, :], ALU.mult, ["oret", "rg_f"], ["mixed"])

        def out_proj(R, x_sb, xr, y_dst):
            for k in range(8):
                B.tr(T0[:, k * 128:k * 128 + R], mixed[0:R, k * 128:(k + 1) * 128], ident[0:R, 0:R], ["mixed", "k_ident"], ["T0"])
            B.cp("vector", mixT[:, :, 0:R], T0[:, :].rearrange("p (k t) -> p k t", k=8)[:, :, 0:R], ["T0"], ["xnT"])
            for nh in range(2):
                for k in range(8):
                    B.mm(Mb[nh][0:R, :], mixT[:, k, 0:R], w_out_bf[:, k, nh * 512:(nh + 1) * 512], k == 0, k == 7,
                         ["xnT", "w_out_bf"], ["M%d" % nh])
                B.tt("vector", x_sb[0:R, nh * 512:(nh + 1) * 512], Mb[nh][0:R, :], x_sb[0:R, nh * 512:(nh + 1) * 512], ALU.add,
                     ["M%d" % nh, xr], [xr])
            B.dma(y_dst, x_sb[0:R, :], [xr], [], xr)

        def run(gen):
            for _ in gen:
                pass

        def make_proj(t):
            last4 = t >= NT - 4
            kvd = [o_kvc_p[t * 128:(t + 1) * 128, :], o_kvs_p[t * 128:(t + 1) * 128, :],
                   o_kvw_p[(t - (NT - 4)) * 128:(t - (NT - 4) + 1) * 128, :] if (last4 and NT >= 4) else None]
            if NT < 4:
                kvd[2] = o_kvw_p[t * 128:(t + 1) * 128, :]
            return project(128, x_p[t * 128:(t + 1) * 128, :], t % 2, cd["ropeN_p"][t * 128:(t + 1) * 128, :],
                           cd["ropeR_p"][t * 128:(t + 1) * 128, :], kvd, "p%d" % t, gslot=t % 2)

        def interleave(gA, gP, nA):
            if gP is None:
                run(gA)
                return
            k = max(1, nA // 12)
            while True:
                try:
                    next(gP)
                except StopIteration:
                    break
                for _ in range(k):
                    try:
                        next(gA)
                    except StopIteration:
                        break
            run(gA)

        def prep(t):
            for h in range(8):
                B.tr(T0[:, h * 128:(h + 1) * 128], q_bf[:, h, :], ident[:, :], ["q_bf", "k_ident"], ["T0"])
            B.cp("vector", qT[:, :, :], T0[:, :].rearrange("p (h t) -> p h t", h=8), ["T0"], ["qT"])
            B.tr(T0[:, 0:128], k_bf[:, 1, :], ident[:, :], ["k_bf", "k_ident"], ["T0"])
            B.tr(T0[:, 128:256], k_bf[:, 2, :], ident[:, :], ["k_bf", "k_ident"], ["T0"])
            B.tr(T0[:, 4 * 128:5 * 128], k_bf[:, 0, :], ident[:, :], ["k_bf", "k_ident"], ["T0"])
            B.tr(T0[:, 5 * 128:6 * 128], cv_bf[:, :], ident[:, :], ["cv_bf", "k_ident"], ["T0"])
            wsl = t % 8
            B.cp("vector", ksT[:, t * 128:(t + 1) * 128], T0[:, 0:128], ["T0"], ["ksT"])
            B.cp("vector", kwT[:, wsl, :], T0[:, 128:256], ["T0"], ["kwT"])
            if t > 0:
                B.cp("vector", cmpT[:, :, 0:16], cmpT[:, :, 128:144], ["cmpT"], ["cmpT"])
            B.cp("vector", cmpT[:, :, 16:144], T0[:, 512:768].rearrange("p (k t) -> p k t", k=2), ["T0"], ["cmpT"])
            B.cp("gpsimd", vs[:, t, :, 0:64], kvo[:, 256:512].rearrange("p (v g d) -> p v g d", v=2, g=2)[:, 1, :, :], ["kvo"], ["vs"])
            B.cp("gpsimd", vw[:, wsl, :, 0:64], kvo[:, 512:768].rearrange("p (v g d) -> p v g d", v=2, g=2)[:, 1, :, :], ["kvo"], ["vw"])
            for j in range(8):
                B.tr(T0[0:64, j * 128:(j + 1) * 128], rqk_bf[:, j * 64:(j + 1) * 64], ident[:, :], ["rqk_bf", "k_ident"], ["T0"])
            B.cp("vector", rqT[:, :, :], T0[0:64, 0:512].rearrange("p (h t) -> p h t", h=4), ["T0"], ["rqT"])
            B.cp("vector", rkT[:, :, :], T0[0:64, 512:1024].rearrange("p (h t) -> p h t", h=4), ["T0"], ["rkT"])
            if 'cmp' not in SKIP:
                compress(8, 8 * t - 1, t == 0)

        def attn(t):
            ncb = 8 * t + 7
            for g in range(2 if DBG >= 4 else 0):
                qrhs = qT[:, 4 * g:4 * g + 4, :].rearrange("p r t -> p (r t)")
                tiles = []
                nct = (ncb + 127) // 128
                for ctile in range(nct):
                    M = min(128, ncb - 128 * ctile)
                    delta = t - 16 * ctile
                    masks = []
                    if delta <= 16:
                        a = 129 - 8 * delta
                        masks.append((ct["shid"][0:8, a:a + M], ct["cbase4"][0:8, :], ["k_shid", "k_cbase4"]))
                    tiles.append(dict(M=M, kT=kcT[:, 128 * ctile:128 * ctile + M], kres="kcT", masks=masks,
                                      v=vc[0:M, ctile, g, :], vres="vc", u=ct["ov"][0:M, ctile * 128:(ctile + 1) * 128]))
                accC = Ab[0][:, 0:260].rearrange("p (r w) -> p r w", r=4)
                accU = Ab[1][:, :].rearrange("p (r w) -> p r w", r=4)
                yield from branch(128, qrhs, "qT", tiles, accC, "A0", 65, tagU=(accU, "A1"))
                finish_branch(128, accC, "A0", ocmp, "ocmp", 0)

                def bonus(t=t):
                    lo = max(0, 2 * t - 1)
                    po = lo - (2 * t - 1)
                    B.tt("vector", imp[:, lo:2 * t + 2], imp[:, lo:2 * t + 2], ct["pat"][:, po:3], ALU.add, ["imp", "k_pat"], ["imp"])
                    B.ts1("vector", imp[:, 0:1], imp[:, 0:1], 1e4, ALU.add, ["imp"], ["imp"])
                select_blocks(128, accU, "A1", 0, bonus, 16)
                tiles = []
                for wt in range(max(0, t - 4), t + 1):
                    masks = []
                    if wt == t - 4:
                        masks.append((ident[:, :], ct["winfirst4"][:, :], ["k_ident", "k_winfirst4"]))
                    if wt == t:
                        masks.append((ident[:, :], ct["causal4"][:, :], ["k_ident", "k_causal4"]))
                    tiles.append(dict(M=128, kT=kwT[:, wt % 8, :], kres="kwT", masks=masks,
                                      v=vw[:, wt % 8, g, :], vres="vw"))
                accW = Ab[2][:, 0:260].rearrange("p (r w) -> p r w", r=4)
                yield from branch(128, qrhs, "qT", tiles, accW, "A2", 65)
                finish_branch(128, accW, "A2", owin, "owin", 8)
                tiles = []
                for kt in range(t + 1):
                    masks = [emask(kt, 128)]
                    if kt == t:
                        masks.append((ident[:, :], ct["causal4"][:, :], ["k_ident", "k_causal4"]))
                    tiles.append(dict(M=128, kT=ksT[:, kt * 128:(kt + 1) * 128], kres="ksT", masks=masks,
                                      v=vs[:, kt, g, :], vres="vs"))
                accS = Ab[0][:, 0:260].rearrange("p (r w) -> p r w", r=4)
                yield from branch(128, qrhs, "qT", tiles, accS, "A0", 65)
                finish_branch(128, accS, "A0", osel, "osel", 4)
                combine(128, g, gates_l[t % 2][:, :], "gates%d" % (t % 2))

        if NT > 0 and DBG >= 2:
            run(make_proj(0))
            for t in range(NT):
                prep(t)
                retention(128, t)
                interleave(attn(t), make_proj(t + 1) if t + 1 < NT else None, 2 * (t + 8))
                out_proj(128, xs_[t % 2], xres[t % 2], y_p[t * 128:(t + 1) * 128, :])
            if True:
                B.dma(o_ret_p, Sf[:, :, :].rearrange("p h e -> p (h e)"), ["Sf"], [], "o_ret_p")


        if NB > 0 and DBG >= 7:
            NS = PAST // 64
            GP = min(8, NPG)
            xslot = 0
            xr = xres[xslot]
            qTs = sb("qTs", [128, NB, 8, 4], BF16)
            ksTn = sb("ksTn", [128, RS], BF16)
            kwTn = sb("kwTn", [128, RS], BF16)
            vsrc = xn
            vrow = sb("vrow", [4, 1024], BF16)
            pt_i = sb("pt_i", [128, NPG], I32)
            pt_f = sb("pt_f", [128, NPG], F32)
            idx = sb("idx", [128, NPG], I32)
            idx64 = sb("idx64", [128, NPG], I32)
            kwTs = sb("kwTs", [128, 516], BF16)
            vws = sb("vws", [128, 5, 2, 65], BF16)
            wstV = qn
            oall = sb("oall", [128, 3, 2, 256], BF16)
            ob16 = sb("ob16", [4, 3, 256], BF16)
            Sfb = Sf
            Sbb = Sbf
            sTm4 = sb("sTm4", [4, 16], BF16)
            kwk4 = sb("kwk4", [4, 256], BF16)
            oret4 = sb("oret4", [4, 512], F32)
            oretS = oret
            B.memset("vector", vws[:, :, :, 64:65], 1.0, ["vws"])
            pgb = [stage[0][:, 0:512], stage[0][:, 512:1024], stage[1][:, 512:1024]]
            wstK = stage[1][:, 0:512]

            run(project(RS, x_s, xslot, cd["ropeN_s"], cd["ropeR_s"], [o_kvc_s, o_kvs_s, None], "s", gslot=0))
            B.dma(o_kvw_s[:, 0:508, :], st_win[:, 4:512, :], [], [], "kvw_d2d")
            for b in range(NB):
                B.dma(o_kvw_s[b, 508:512, :], kvo[4 * b:4 * b + 4, 512:768], ["kvo"], [], "kvo_out")
            idr = ident[0:RS, 0:RS]
            for h in range(8):
                B.tr(T0[:, h * 128:h * 128 + RS], q_bf[0:RS, h, :], idr, ["q_bf", "k_ident"], ["T0"])
            B.cp("vector", qTs[:, :, :, :].rearrange("p b h s -> p h b s"),
                 T0[:, :].rearrange("p (h t) -> p h t", h=8)[:, :, 0:RS].rearrange("p h (b s) -> p h b s", s=4), ["T0"], ["qTs"])
            B.tr(T0[:, 0:RS], k_bf[0:RS, 1, :], idr, ["k_bf", "k_ident"], ["T0"])
            B.tr(T0[:, 128:128 + RS], k_bf[0:RS, 2, :], idr, ["k_bf", "k_ident"], ["T0"])
            B.cp("vector", ksTn[:, :], T0[:, 0:RS], ["T0"], ["ksTn"])
            B.cp("vector", kwTn[:, :], T0[:, 128:128 + RS], ["T0"], ["kwTn"])
            for j in range(8):
                B.tr(T0[0:64, j * 128:j * 128 + RS], rqk_bf[0:RS, j * 64:(j + 1) * 64], idr, ["rqk_bf", "k_ident"], ["T0"])
            B.cp("vector", rqT[:, :, 0:RS], T0[0:64, 0:512].rearrange("p (h t) -> p h t", h=4)[:, :, 0:RS], ["T0"], ["rqT"])
            B.cp("vector", rkT[:, :, 0:RS], T0[0:64, 512:1024].rearrange("p (h t) -> p h t", h=4)[:, :, 0:RS], ["T0"], ["rkT"])
            B.tt("vector", qcr[:, :, 0:RS].rearrange("p h (b s) -> p h b s", s=4), rqT[:, :, 0:RS].rearrange("p h (b s) -> p h b s", s=4),
                 ct["cross4T"][:, :].rearrange("p (h s) -> p h s", h=4).unsqueeze(2).broadcast_to([64, 4, NB, 4]), ALU.mult,
                 ["rqT", "k_cross4T"], ["qcr"])
            B.cp("vector", vsrc[0:RS, 0:128], kvo[0:RS, 384:512], ["kvo"], ["xn"])
            B.cp("vector", vsrc[0:RS, 128:256], kvo[0:RS, 640:768], ["kvo"], ["xn"])
            B.cp("vector", vsrc[0:RS, 256:768], rv_bf[0:RS, :], ["rv_bf"], ["xn"])
            B.cp("vector", vsrc[0:RS, 768:1024], rqk_bf[0:RS, 256:512], ["rqk_bf"], ["xn"])
            gam = [1.0 - 2.0 ** (-5.0 - h) for h in range(4)]

            for b in range(NB):
                B.dma(vrow[0:4, :], vsrc[4 * b:4 * b + 4, :], ["xn"], ["vrow"], "vrow")
                B.dma(pt_i[:, :], ptab[0:1, b * NPG:(b + 1) * NPG].partition_broadcast(128), [], ["pt_i"], "pt_i")
                B.cp("vector", pt_f[:, :], pt_i[:, :], ["pt_i"], ["pt_f"])
                B.ts("vector", pt_f[:, :], pt_f[:, :], 128.0, ct["iota"][:, 0:1], ALU.mult, ALU.add, ["pt_f", "k_iota"], ["pt_f"])
                B.cp("vector", idx[:, :], pt_f[:, :], ["pt_f"], ["idx"])
                B.cp("vector", pt_f[:, :], pt_i[:, :], ["pt_i", "idx"], ["pt_f"])
                B.ts("vector", pt_f[:, :], pt_f[:, :], 64.0, ct["iota"][:, 0:1], ALU.mult, ALU.add, ["pt_f", "k_iota"], ["pt_f"])
                B.cp("vector", idx64[:, :], pt_f[:, :], ["pt_f"], ["idx64"])
                B.dma(wstK, st_winK[b], [], ["wstK"], "wstK")
                B.cp("vector", kwTs[:, 0:512], wstK, ["wstK"], ["kwTs"])
                B.cp("vector", kwTs[:, 512:516], kwTn[:, 4 * b:4 * b + 4], ["kwTn"], ["kwTs"])
                B.dma(wstV[:, :].rearrange("p (t c) -> p t c", t=4), st_winV[b].rearrange("(t p) c -> p t c", p=128), [], ["qn"], "qn")
                B.cp("vector", vws[:, 0:4, :, 0:64], wstV[:, :].rearrange("p (t g d) -> p t g d", t=4, g=2), ["qn"], ["vws"])
                B.cp("vector", vws[0:4, 4, :, 0:64], vrow[0:4, 128:256].rearrange("p (g d) -> p g d", g=2), ["vrow"], ["vws"])
                B.cp("vector", ksT[:, PAST:PAST + 4], ksTn[:, 4 * b:4 * b + 4], ["ksTn"], ["ksT"])
                B.cp("vector", vs[0:4, NPG, :, 0:64], vrow[0:4, 0:128].rearrange("p (g d) -> p g d", g=2), ["vrow"], ["vs"])
                for pg in range(NPG):
                    sl = pg % 3
                    B.P.add("gpsimd", (lambda o_, i_: (lambda e: e.indirect_dma_start(out=o_, out_offset=None, in_=c_pg,
                            in_offset=bass.IndirectOffsetOnAxis(ap=i_, axis=0))))(pgb[sl], idx[:, pg:pg + 1]),
                            ["idx"], ["pgb%d" % sl] + (["stage0"] if (b == 0 and pg < 2) else []), dma="pgb%d" % sl)
                    lc = 16 + (pg % GP) * 128
                    B.cp("vector", cmpT[:, :, lc:lc + 128], pgb[sl][:, 0:256].rearrange("p (k r) -> p k r", k=2), ["pgb%d" % sl], ["cmpT"])
                    B.cp("scalar", ksT[:, pg * 128:(pg + 1) * 128], pgb[sl][:, 256:384], ["pgb%d" % sl], ["ksT"])
                    B.cp("gpsimd", vs[:, pg, :, 0:64], pgb[sl][:, 384:512].rearrange("p (g d) -> p g d", g=2), ["pgb%d" % sl], ["vs"])
                    if (pg + 1) % GP == 0:
                        Gi = pg // GP
                        compress(8 * GP, 8 * GP * Gi - 1, Gi == 0)
                        B.cp("vector", cmpT[:, :, 0:16], cmpT[:, :, 128 * GP:128 * GP + 16], ["cmpT"], ["cmpT"])
                for g in range(2):
                    qrhs = qTs[:, b, 4 * g:4 * g + 4, :].rearrange("p r s -> p (r s)")
                    accC = Ab[0][0:4, 0:260].rearrange("p (r w) -> p r w", r=4)
                    accU = Ab[1][0:4, :].rearrange("p (r w) -> p r w", r=4)
                    tiles = []
                    for ctile in range((NCS + 127) // 128):
                        M = min(128, NCS - 128 * ctile)
                        tiles.append(dict(M=M, kT=kcT[:, 128 * ctile:128 * ctile + M], kres="kcT", masks=[],
                                          v=vc[0:M, ctile, g, :], vres="vc", u=ct["ov"][0:M, ctile * 128:(ctile + 1) * 128]))
                    run(branch(4, qrhs, "qTs", tiles, accC, "A0", 65, tagU=(accU, "A1")))
                    finish_branch(4, accC, "A0", ocmp, "ocmp", 0)

                    def bonus_s():
                        B.ts1("vector", imp[0:4, 0:1], imp[0:4, 0:1], 1e4, ALU.add, ["imp"], ["imp"])
                        B.ts1("vector", imp[0:4, NS - 1:NS], imp[0:4, NS - 1:NS], 1e4, ALU.add, ["imp"], ["imp"])
                    select_blocks(4, accU, "A1", 0, bonus_s, 15)
                    tiles = []
                    for wt in range(4):
                        masks = [(ident[:, :], ct["swinfirst4"][:, :], ["k_ident", "k_swinfirst4"])] if wt == 0 else []
                        tiles.append(dict(M=128, kT=kwTs[:, wt * 128:(wt + 1) * 128], kres="kwTs", masks=masks,
                                          v=vws[:, wt, g, :], vres="vws"))
                    tiles.append(dict(M=4, kT=kwTs[:, 512:516], kres="kwTs",
                                      masks=[(ident[0:4, 0:4], ct["scausal4"][0:4, :], ["k_ident", "k_scausal4"])],
                                      v=vws[0:4, 4, g, :], vres="vws"))
                    accW = Ab[2][0:4, 0:260].rearrange("p (r w) -> p r w", r=4)
                    run(branch(4, qrhs, "qTs", tiles, accW, "A2", 65))
                    finish_branch(4, accW, "A2", owin, "owin", 8)
                    tiles = []
                    for kt in range(NPG):
                        tiles.append(dict(M=128, kT=ksT[:, kt * 128:(kt + 1) * 128], kres="ksT", masks=[emask(kt, 4)],
                                          v=vs[:, kt, g, :], vres="vs"))
                    tiles.append(dict(M=4, kT=ksT[:, PAST:PAST + 4], kres="ksT",
                                      masks=[(ident[0:4, 0:4], ct["scausal4"][0:4, :], ["k_ident", "k_scausal4"])],
                                      v=vs[0:4, NPG, g, :], vres="vs"))
                    accS = Ab[0][0:4, 0:260].rearrange("p (r w) -> p r w", r=4)
                    run(branch(4, qrhs, "qTs", tiles, accS, "A0", 65))
                    finish_branch(4, accS, "A0", osel, "osel", 4)
                    for bi, (o_, ores) in enumerate(((ocmp, "ocmp"), (osel, "osel"), (owin, "owin"))):
                        B.cp("vector", ob16[0:4, bi, :], o_[0:4, :, :].rearrange("p r d -> p (r d)"), [ores], ["ob16_%d" % bi])
                        B.dma(oall[4 * b:4 * b + 4, bi, g, :], ob16[0:4, bi, :], ["ob16_%d" % bi], ["oall"], "ob16_%d" % bi)
                B.dma(Sfb[:, :, :].rearrange("p h e -> p (h e)"), st_ret[:, b * 512:(b + 1) * 512], [], ["Sf"], "Sf")
                B.cp("vector", Sbb[:, :, :], Sfb[:, :, :], ["Sf"], ["Sbf"])
                for h in range(4):
                    B.mm(Mb[0][0:4, h * 4:h * 4 + 4], rkT[:, h, 4 * b:4 * b + 4], rqT[:, h, 4 * b:4 * b + 4], True, True, ["rkT", "rqT"], ["M0"])
                B.tt("vector", sTm4[0:4, :], Mb[0][0:4, 0:16], ct["dec4T"][0:4, :], ALU.mult, ["M0", "k_dec4T"], ["sTm4"])
                for h in range(4):
                    B.mm(Mb[1][0:4, h * 128:(h + 1) * 128], sTm4[0:4, h * 4:h * 4 + 4], vrow[0:4, 256 + h * 128:256 + (h + 1) * 128],
                         True, False, ["sTm4", "vrow"], ["M1"])
                    B.mm(Mb[1][0:4, h * 128:(h + 1) * 128], qcr[:, h, 4 * b:4 * b + 4], Sbb[:, h, :], False, True, ["qcr", "Sbf"], ["M1"])
                B.tt("vector", kwk4[0:4, :].rearrange("p (h d) -> p h d", h=4), vrow[0:4, 768:1024].rearrange("p (h d) -> p h d", h=4),
                     ct["w4tab"][0:4, :].unsqueeze(2).broadcast_to([4, 4, 64]), ALU.mult, ["vrow", "k_w4tab"], ["kwk4"])
                for h in range(4):
                    B.mm(Mb[0][0:64, h * 128:(h + 1) * 128], kwk4[0:4, h * 64:(h + 1) * 64], vrow[0:4, 256 + h * 128:256 + (h + 1) * 128],
                         True, True, ["kwk4", "vrow"], ["M0"])
                for h in range(4):
                    B.stt("vector", Sfb[:, h, :], Sfb[:, h, :], float(gam[h] ** 4), Mb[0][0:64, h * 128:(h + 1) * 128],
                          ALU.mult, ALU.add, ["Sf", "M0"], ["Sf"])
                B.dma(o_ret_s[:, b * 512:(b + 1) * 512], Sfb[:, :, :].rearrange("p h e -> p (h e)"), ["Sf"], [], "Sf")
                B.cp("vector", oret4[0:4, :], Mb[1][0:4, :], ["M1"], ["oret4"])
                B.dma(oretS[4 * b:4 * b + 4, :], oret4[0:4, :], ["oret4"], ["oret"], "oret4")
            B.act(junk[0:RS, 0:512], oretS[0:RS, :], AF.Square, ["oret"], ["junk"])
            B.red(st1[0:RS, 2:6], junk[0:RS, 0:512].rearrange("p (h e) -> p h e", h=4), ["junk"], ["st1"])
            B.rsqrt(st1[0:RS, 2:6], st1[0:RS, 2:6], 1.0 / 128, ["st1"], ["st1"])
            B.tt("vector", oret[0:RS, :].rearrange("p (h e) -> p h e", h=4), oretS[0:RS, :].rearrange("p (h e) -> p h e", h=4),
                 st1[0:RS, 2:6].unsqueeze(2).broadcast_to([RS, 4, 128]), ALU.mult, ["oret", "st1"], ["oret"])
            B.tt("vector", mixed[0:RS, 512:1024], oret[0:RS, :], rg_f[0:RS, :], ALU.mult, ["oret", "rg_f"], ["mixed"])
            for g in range(2):
                for bi, (o_, ores) in enumerate(((ocmp, "ocmp"), (osel, "osel"), (owin, "owin"))):
                    B.cp("vector", o_[0:RS, :, :], oall[0:RS, bi, g, :].rearrange("p (r d) -> p r d", r=4), ["oall"], [ores])
                combine(RS, g, gates_l[0][0:RS, :], "gates0")
            out_proj(RS, xs_[xslot], xr, y_s)
        print("sbuf bytes remaining pass1", nc.sbuf_bytes_remaining)
        print("pass1 ops", P.emit(nc, "a"))

    P2 = Prog()
    with ExitStack() as es:
      if not os.environ.get('K_NOP2'):
          B = Bld(nc, P2, es)
          sb, ps = B.sb, B.ps
          G = 256
          Ub = [ps("U0", [128, 512], F32), ps("U1", [128, 512], F32)]
          Vb = [ps("V0", [128, 512], F32), ps("V1", [128, 512], F32)]
          T0 = ps("T0b", [128, 1024], BF16)
          Yb = [ps("Y0", [128, 512], F32), ps("Y1", [128, 512], F32)]
          ident = sb("ident2", [128, 128], BF16)
          B.dma(ident[:], cd["ident"], [], ["ident2"], "ident2")
          epsc = sb("epsc2", [128, 1], F32)
          B.memset("vector", epsc[:], EPS, ["epsc2"])
          B.eps_ap = epsc
          wfi = sb("wfi", [128, 8, 2 * D_FF], BF16)
          wfo = sb("wfo", [128, NFC, 1024], BF16)
          nmf = sb("nmf", [128, 8], F32)
          wcv = sb("wcv", [128, NFC, 4], F32)
          stage = [sb("stg0", [128, D_FF], F32), sb("stg1", [128, D_FF], F32)]
          B.dma(nmf[:], nm_ffn, [], ["nmf"], "nmf")
          B.dma(wcv[:, :, :].rearrange("p c j -> p (c j)"), wconv, [], ["wcv"], "wcv")
          si = 0
          for k in range(8):
              for hf in range(2):
                  s_ = si % 2
                  B.dma(stage[s_][:, :], w_fi[k * 128:(k + 1) * 128, hf * D_FF:(hf + 1) * D_FF], [], ["stg%d" % s_], "stg%d" % s_)
                  B.ts1("vector" if si % 2 == 0 else "gpsimd", wfi[:, k, hf * D_FF:(hf + 1) * D_FF], stage[s_][:, :], nmf[:, k:k + 1], ALU.mult,
                        ["stg%d" % s_, "nmf"], ["wfi"])
                  si += 1
          for fc in range(0, NFC, 2):
              s_ = si % 2
              B.dma(stage[s_][:, 0:2048].rearrange("p (c n) -> p c n", c=2),
                    w_fo[fc * 128:(fc + 2) * 128, :].rearrange("(c p) n -> p c n", p=128), [], ["stg%d" % s_], "stg%d" % s_)
              B.cp("vector" if si % 2 == 0 else "gpsimd", wfo[:, fc:fc + 2, :], stage[s_][:, 0:2048].rearrange("p (c n) -> p c n", c=2),
                   ["stg%d" % s_], ["wfo"])
              si += 1
          hs = [sb("hs%d" % i, [128, 1024], F32) for i in range(2)]
          hn = sb("hn", [128, 1024], BF16)
          hnT = sb("hnT", [128, 8, G], BF16)
          junk = sb("junk2", [128, 1024], F32)
          st1 = sb("st2", [128, 4], F32)
          ubufs = [sb("ubuf0", [128, G + 2], F32), sb("ubuf1", [128, G + 2], F32)]
          carry = sb("carry", [128, NFC, 2], F32)
          ucs = [sb("uc0", [128, G], F32), sb("uc1", [128, G], F32)]
          ues = [sb("ue0", [128, G], F32), sb("ue1", [128, G], F32)]
          actb = sb("actb", [128, NFC, G], BF16)
          y_sb = sb("y_sb", [128, 1024], F32)
          B.memset("vector", carry[:], 0.0, ["carry"])

          def ffn_group(tiles_rows, loads, stores, conv_w, carry_mode):
              ncol = sum(tiles_rows)
              col = 0
              cols = []
              for i, R in enumerate(tiles_rows):
                  hr = "hs%d" % i
                  B.dma(hs[i][0:R, :], loads[i], [], [hr], hr)
                  B.act(junk[0:R, :], hs[i][0:R, :], AF.Square, [hr], ["junk2", "st2"], accum=st1[0:R, 0:1])
                  B.rsqrt(st1[0:R, 1:2], st1[0:R, 0:1], 1.0 / 1024, ["st2"], ["st2"])
                  B.act(hn[0:R, :], hs[i][0:R, :], AF.Identity, [hr, "st2"], ["hn"], scale=st1[0:R, 1:2])
                  for k in range(8):
                      B.tr(T0[:, k * 128:k * 128 + R], hn[0:R, k * 128:(k + 1) * 128], ident[0:R, 0:R], ["hn", "ident2"], ["T0b"])
                  B.cp("vector", hnT[:, :, col:col + R], T0[:, :].rearrange("p (k t) -> p k t", k=8)[:, :, 0:R], ["T0b"], ["hnT"])
                  cols.append(col)
                  col += R
              for fc in range(NFC):
                  bk = fc % 2
                  for k in range(8):
                      B.mm(Ub[bk][:, 0:ncol], wfi[:, k, fc * 128:(fc + 1) * 128], hnT[:, k, 0:ncol], k == 0, k == 7, ["wfi", "hnT"], ["U%d" % bk])
                  for k in range(8):
                      B.mm(Vb[bk][:, 0:ncol], wfi[:, k, D_FF + fc * 128:D_FF + (fc + 1) * 128], hnT[:, k, 0:ncol], k == 0, k == 7,
                           ["wfi", "hnT"], ["V%d" % bk])
                  uc, ue, ubuf = ucs[bk], ues[bk], ubufs[bk]
                  ucr, uer, ubr = "uc%d" % bk, "ue%d" % bk, "ubuf%d" % bk
                  conv_w(fc, Ub[bk], "U%d" % bk, ncol, uc, ucr, ubuf, ubr)
                  B.act(ue[:, 0:ncol], uc[:, 0:ncol], AF.Silu, [ucr], [uer])
                  B.tt("vector", actb[:, fc, 0:ncol], ue[:, 0:ncol], Vb[bk][:, 0:ncol], ALU.mult, [uer, "V%d" % bk], ["actb%d" % fc])
              for i, R in enumerate(tiles_rows):
                  hr = "hs%d" % i
                  for nh in range(2):
                      for fc in range(NFC):
                          B.mm(Yb[nh][0:R, :], actb[:, fc, cols[i]:cols[i] + R], wfo[:, fc, nh * 512:(nh + 1) * 512], fc == 0, fc == NFC - 1,
                               ["actb%d" % fc, "wfo"], ["Y%d" % nh])
                      B.tt("vector", y_sb[0:R, nh * 512:(nh + 1) * 512], Yb[nh][0:R, :], hs[i][0:R, nh * 512:(nh + 1) * 512], ALU.add,
                           ["Y%d" % nh, hr], ["y_sb"])
                  B.dma(stores[i], y_sb[0:R, :], ["y_sb"], [], "y_out")

          def conv_seq(fc, Ux, ures, ncol, uc, ucr, ubuf, ubr):
              B.cp("scalar", ubuf[:, 0:2], carry[:, fc, :], ["carry"], [ubr])
              B.cp("scalar", ubuf[:, 2:2 + ncol], Ux[:, 0:ncol], [ures], [ubr])
              B.ts("vector", uc[:, 0:ncol], ubuf[:, 2:2 + ncol], wcv[:, fc, 2:3], wcv[:, fc, 3:4], ALU.mult, ALU.add, [ubr, "wcv"], [ucr])
              B.stt("vector", uc[:, 0:ncol], ubuf[:, 1:1 + ncol], wcv[:, fc, 1:2], uc[:, 0:ncol], ALU.mult, ALU.add, [ubr, "wcv", ucr], [ucr])
              B.stt("vector", uc[:, 0:ncol], ubuf[:, 0:ncol], wcv[:, fc, 0:1], uc[:, 0:ncol], ALU.mult, ALU.add, [ubr, "wcv", ucr], [ucr])
              B.cp("scalar", carry[:, fc, :], ubuf[:, ncol:ncol + 2], [ubr], ["carry"])

          if NT > 0 and DBG >= 6:
              tpg = G // 128
              for gi in range(NT // tpg):
                  rows = [128] * tpg
                  loads = [y_p[(gi * tpg + i) * 128:(gi * tpg + i + 1) * 128, :] for i in range(tpg)]
                  ffn_group(rows, loads, loads, conv_seq, "seq")
              B.dma(o_conv_p, carry[:, :, :].rearrange("p c j -> p (c j)"), ["carry"], [], "o_conv_p")

          if NB > 0 and DBG >= 7:
              stc = sb("stc", [128, NFC, NB, 2], F32)
              convS = sb("convS", [128, NFC, NB, 2], F32)
              B.dma(stc[:, :, :, :].rearrange("p c b j -> p (c b j)"), st_conv, [], ["stc"], "stc")

              def conv_smp(fc, Ux, ures, ncol, uc, ucr, ubuf, ubr):
                  ub = ubuf[:, 0:NB * 6].rearrange("p (b j) -> p b j", j=6)
                  B.cp("scalar", ub[:, :, 0:2], stc[:, fc, :, :], ["stc"], [ubr])
                  B.cp("scalar", ub[:, :, 2:6], Ux[:, 0:RS].rearrange("p (b s) -> p b s", s=4), [ures], [ubr])
                  uc3 = uc[:, 0:RS].rearrange("p (b s) -> p b s", s=4)
                  B.ts("vector", uc3, ub[:, :, 2:6], wcv[:, fc, 2:3], wcv[:, fc, 3:4], ALU.mult, ALU.add, [ubr, "wcv"], [ucr])
                  B.stt("vector", uc3, ub[:, :, 1:5], wcv[:, fc, 1:2], uc3, ALU.mult, ALU.add, [ubr, "wcv", ucr], [ucr])
                  B.stt("vector", uc3, ub[:, :, 0:4], wcv[:, fc, 0:1], uc3, ALU.mult, ALU.add, [ubr, "wcv", ucr], [ucr])
                  B.cp("scalar", convS[:, fc, :, :], ub[:, :, 4:6], [ubr], ["convS"])
              ffn_group([RS], [y_s], [y_s], conv_smp, "smp")
              B.dma(o_conv_s, convS[:, :, :, :].rearrange("p c b j -> p (c b j)"), ["convS"], [], "o_conv_s")

          print("pass2 ops", P2.emit(nc, "b"))
    return nc, consts


_CACHE = {}


def _w2h(w2):
    o = np.zeros((128, 256), np.float32)
    for g in range(2):
        o[64 * g:64 * g + 64, 64 * g:64 * g + 64] = w2[0]
        o[64 * g:64 * g + 64, 128 + 64 * g:128 + 64 * g + 64] = w2[1]
    return o


def _run(inputs, S, NB, PAST, NPOOL, n_cores=8):
    key = (S, NB, PAST, NPOOL)
    if key not in _CACHE:
        _CACHE[key] = build(S, NB, PAST, NPOOL)
    nc, consts = _CACHE[key]
    f32 = np.float32
    NPG = PAST // 128
    RS = NB * 4
    A = lambda v: np.ascontiguousarray(np.asarray(v))
    x_prompt, x_sample = A(inputs["x_prompt"]), A(inputs["x_sample"])
    cache_cmp, cache_sel = A(inputs["cache_kv_cmp"])[0], A(inputs["cache_kv_sel"])[0]
    c_pg = np.empty((NPOOL, 128, 512), np.float32)
    c_pg[:, :, 0:256] = cache_cmp.reshape(NPOOL, 128, 2, 128).transpose(0, 3, 2, 1).reshape(NPOOL, 128, 256)
    c_pg[:, :, 256:384] = cache_sel[:, :, 0].reshape(NPOOL, 128, 128).transpose(0, 2, 1)
    c_pg[:, :, 384:512] = cache_sel[:, :, 1].reshape(NPOOL, 128, 128)
    c_pg = c_pg.reshape(NPOOL * 128, 512)
    st_win_all = A(inputs["state_kv_win"])[0]
    st_ret_all = A(inputs["state_ret"])[0]
    st_conv_all = A(inputs["state_conv"])[0]
    pt_all = A(inputs["page_table"]).astype(np.int32)
    shared = {("c_" + k): v for k, v in consts.items()}
    shared.update({
        "c_pg": c_pg,
        "nm_mix": A(inputs["norm_mix"])[0].reshape(8, 128).T.copy(),
        "w_in": A(inputs["w_in"])[0],
        "gq": np.tile(A(inputs["norm_q"])[0][None, :], (128, 1)).astype(f32),
        "gk": np.tile(A(inputs["norm_k"])[0].reshape(1, 192), (128, 1)).astype(f32),
        "w1h": np.tile(A(inputs["w_cmp1"])[0].reshape(2, 32, 64, 64).transpose(2, 0, 1, 3).reshape(64, 4096), (2, 1)).astype(f32),
        "w2h": _w2h(A(inputs["w_cmp2"])[0]),
        "peT": np.tile(A(inputs["pe_cmp"])[0].transpose(2, 0, 1).reshape(64, 64), (2, 1)).astype(f32),
        "w_out": A(inputs["w_out"])[0],
        "nm_ffn": A(inputs["norm_ffn"])[0].reshape(8, 128).T.copy(),
        "w_fi": A(inputs["w_ffn_in"])[0],
        "wconv": np.concatenate([A(inputs["w_conv"])[0], A(inputs["b_conv"])], axis=0).reshape(4, NFC, 128).transpose(2, 1, 0).reshape(128, NFC * 4).copy(),
        "w_fo": A(inputs["w_ffn_out"])[0],
    })
    in_maps = []
    for c in range(n_cores):
        bs = slice(c * NB, (c + 1) * NB)
        sw = st_win_all[bs]
        m = dict(shared)
        m["x_p"] = x_prompt[c]
        m["x_s"] = x_sample[bs].reshape(RS, 1024)
        m["st_winK"] = np.ascontiguousarray(sw[:, :, 0].reshape(NB, 512, 128).transpose(0, 2, 1))
        m["st_winV"] = np.ascontiguousarray(sw[:, :, 1].reshape(NB, 512, 128))
        m["st_win"] = np.ascontiguousarray(sw.reshape(NB, 512, 256))
        m["st_ret"] = np.ascontiguousarray(st_ret_all[bs].transpose(2, 0, 1, 3)).reshape(64, NB * 512)
        m["st_conv"] = np.ascontiguousarray(st_conv_all[bs].reshape(NB, 2, NFC, 128).transpose(3, 2, 0, 1)).reshape(128, NFC * NB * 2)
        m["ptab"] = np.ascontiguousarray(pt_all[bs].reshape(1, NB * NPG))
        in_maps.append(m)
    res = run_bass_kernel_spmd(nc, in_maps, core_ids=list(range(n_cores)))
    R = res.results
    cat = lambda k: np.stack([r[k] for r in R], 0)
    nB = n_cores
    DB = n_cores * NB
    y_prompt = cat("y_p")
    y_sample = cat("y_s").reshape(DB, 4, 1024)
    kvc_p = cat("o_kvc_p").reshape(1, nB, S, 2, 2, 64)
    kvs_p = cat("o_kvs_p").reshape(1, nB, S, 2, 2, 64)
    kvw_p = cat("o_kvw_p").reshape(1, nB, 512, 2, 2, 64)
    kvc_s = cat("o_kvc_s").reshape(1, DB, 4, 2, 2, 64)
    kvs_s = cat("o_kvs_s").reshape(1, DB, 4, 2, 2, 64)
    kvw_s = cat("o_kvw_s").reshape(1, DB, 512, 2, 2, 64)
    ret_p = cat("o_ret_p").reshape(nB, 64, 4, 128).transpose(0, 2, 1, 3).reshape(1, nB, 4, 64, 128)
    ret_s = cat("o_ret_s").reshape(nB, 64, NB, 4, 128).transpose(0, 2, 3, 1, 4).reshape(1, DB, 4, 64, 128)
    conv_p = cat("o_conv_p").reshape(nB, 128, NFC, 2).transpose(0, 3, 2, 1).reshape(1, nB, 2, D_FF)
    conv_s = cat("o_conv_s").reshape(nB, 128, NFC, NB, 2).transpose(0, 3, 4, 2, 1).reshape(1, DB, 2, D_FF)
    outs = (y_prompt, y_sample, kvc_p, kvc_s, kvs_p, kvs_s, kvw_p, kvw_s, ret_p, ret_s, conv_p, conv_s)
    return tuple(np.ascontiguousarray(o, dtype=np.float32) for o in outs)


def kernel(**inputs):
    S = int(np.asarray(inputs["x_prompt"]).shape[1])
    DB = int(np.asarray(inputs["x_sample"]).shape[0])
    NPG = int(np.asarray(inputs["page_table"]).shape[1])
    NPOOL = int(np.asarray(inputs["cache_kv_cmp"]).shape[1])
    return _run(inputs, S, DB // 8, NPG * 128, NPOOL)
```

```python
import math
from contextlib import ExitStack
import numpy as np
import ml_dtypes
import concourse.bass as bass
import concourse.mybir as mybir
from concourse.bass_utils import run_bass_kernel_spmd

F32 = mybir.dt.float32
BF16 = mybir.dt.bfloat16
I32 = mybir.dt.int32
AF = mybir.ActivationFunctionType
ALU = mybir.AluOpType
AX = mybir.AxisListType
NPBF = ml_dtypes.bfloat16

import os
POOL_ENG = os.environ.get('K_POOL', 'vector')
SKIP = os.environ.get('K_SKIP', '')
PSTOP = int(os.environ.get('K_PSTOP', '99'))
PSTOP2 = int(os.environ.get('K_PSTOP2', '99'))
ENGS = ("tensor", "vector", "scalar", "gpsimd", "sync")
NEGM = -32768.0
EPS = 1e-6
D_IN = 2840
D_FF = 2816
NFC = 22


class Op:
    __slots__ = ("eng", "fn", "deps", "signal", "semkey", "idx", "count", "isdma")

    def __init__(self, eng, fn, isdma, semkey):
        self.eng = eng
        self.fn = fn
        self.deps = []
        self.signal = False
        self.semkey = semkey
        self.isdma = isdma
        self.count = None


class Prog:
    def __init__(self):
        self.ops = []
        self.last_w = {}
        self.readers = {}

    EXCL = ("S0", "S1", "A0", "A1", "A2", "T0", "M0", "M1", "U0", "U1", "V0", "V1", "T0b", "Y0", "Y1")

    def add(self, eng, fn, reads=(), writes=(), dma=None):
        xr = [r for r in reads if r in self.EXCL]
        if xr:
            writes = list(writes) + [r for r in xr if r not in writes]
        op = Op(eng, fn, dma is not None, ("dma", dma) if dma is not None else ("eng", eng))
        deps = set()
        for r in reads:
            w = self.last_w.get(r)
            if w is not None:
                deps.add(w)
        for r in writes:
            w = self.last_w.get(r)
            if w is not None:
                deps.add(w)
            for rd in self.readers.get(r, ()):
                deps.add(rd)
        if eng == "tensor":
            deps = {d for d in deps if not (d.eng == "tensor" and not d.isdma)}
        op.deps = list(deps)
        for d in op.deps:
            d.signal = True
        for r in reads:
            self.readers.setdefault(r, []).append(op)
        for r in writes:
            self.last_w[r] = op
            self.readers[r] = []
        op.idx = len(self.ops)
        self.ops.append(op)
        return op

    def emit(self, nc, tag):
        for op in self.ops:
            if op.isdma:
                op.signal = True
        lastop = {}
        for op in self.ops:
            if not op.isdma:
                lastop[op.eng] = op
        for op in lastop.values():
            op.signal = True
        counts = {}
        for op in self.ops:
            if op.signal:
                inc = 16 if op.isdma else 1
                counts[op.semkey] = counts.get(op.semkey, 0) + inc
                op.count = counts[op.semkey]
        keys = list(counts.keys())
        with ExitStack() as es:
            sems = {}
            for i, k in enumerate(keys):
                sems[k] = es.enter_context(nc.semaphore("%s%d" % (tag, i)))
            block = es.enter_context(nc.Block())
            per_eng = {e: [op for op in self.ops if op.eng == e] for e in ENGS}

            def make(ename):
                ops = per_eng[ename]

                def body(eng):
                    waited = {}
                    for op in ops:
                        need = {}
                        for d in op.deps:
                            if need.get(d.semkey, 0) < d.count:
                                need[d.semkey] = d.count
                        for k, v in need.items():
                            if waited.get(k, 0) < v:
                                eng.wait_ge(sems[k], v)
                                waited[k] = v
                        ins = op.fn(eng)
                        if op.signal:
                            ins.then_inc(sems[op.semkey], 16 if op.isdma else 1)
                    for k, v in counts.items():
                        if waited.get(k, 0) < v:
                            eng.wait_ge(sems[k], v)
                return body

            for ename in ENGS:
                getattr(block, ename)(make(ename))
        return len(self.ops), len(keys)


class Bld:
    def __init__(self, nc, P, es):
        self.nc, self.P, self.es = nc, P, es
        self.rr = 0

    def sb(self, name, shape, dt):
        return self.es.enter_context(self.nc.sbuf_tensor(name, shape, dt))

    def ps(self, name, shape, dt):
        return self.es.enter_context(self.nc.psum_tensor(name, shape, dt))

    def dma(self, out, in_, r, w, key, eng="sync"):
        self.P.add(eng, lambda e: e.dma_start(out=out, in_=in_), r, w, dma=key)

    def mm(self, out, lhsT, rhs, start, stop, r, w, nocheck=False):
        if nocheck:
            self.P.add("tensor", lambda e: e.matmul(out, lhsT=lhsT, rhs=rhs, start=start, stop=stop, skip_group_check=True), r, w)
        else:
            self.P.add("tensor", lambda e: e.matmul(out, lhsT=lhsT, rhs=rhs, start=start, stop=stop), r, w)

    def tr(self, out, in_, ident, r, w):
        self.P.add("tensor", lambda e: e.transpose(out=out, in_=in_, identity=ident), r, w)

    def act(self, out, in_, func, r, w, bias=None, scale=None, accum=None):
        kw = {}
        if bias is not None:
            kw["bias"] = bias
        if scale is not None:
            kw["scale"] = scale
        if accum is not None:
            kw["accum_out"] = accum
        self.P.add("scalar", lambda e: e.activation(out=out, in_=in_, func=func, **kw), r, w)

    def _e(self, eng):
        return POOL_ENG if eng == "gpsimd" else eng

    def tt(self, eng, out, in0, in1, op, r, w):
        eng = self._e(eng)
        self.P.add(eng, lambda e: e.tensor_tensor(out=out, in0=in0, in1=in1, op=op), r, w)

    def ts(self, eng, out, in0, s1, s2, op0, op1, r, w):
        eng = self._e(eng)
        self.P.add(eng, lambda e: e.tensor_scalar(out=out, in0=in0, scalar1=s1, scalar2=s2, op0=op0, op1=op1), r, w)

    def ts1(self, eng, out, in_, s, op, r, w):
        eng = self._e(eng)
        self.P.add(eng, lambda e: e.tensor_single_scalar(out=out, in_=in_, scalar=s, op=op), r, w)

    def stt(self, eng, out, in0, scalar, in1, op0, op1, r, w):
        eng = self._e(eng)
        self.P.add(eng, lambda e: e.scalar_tensor_tensor(out=out, in0=in0, scalar=scalar, in1=in1, op0=op0, op1=op1), r, w)

    def cp(self, eng, out, in_, r, w):
        eng = self._e(eng)
        if eng == "scalar":
            self.P.add(eng, lambda e: e.copy(out=out, in_=in_), r, w)
        else:
            self.P.add(eng, lambda e: e.tensor_copy(out=out, in_=in_), r, w)

    def red(self, out, in_, r, w):
        self.P.add("vector", lambda e: e.reduce_sum(out=out, in_=in_, axis=AX.X), r, w)

    def recip(self, out, in_, r, w):
        self.P.add("vector", lambda e: e.reciprocal(out=out, in_=in_), r, w)

    def memset(self, eng, ap, val, w):
        eng = self._e(eng)
        self.P.add(eng, lambda e: e.memset(ap, val), (), w)

    def rsqrt(self, out, in_, scale, r, w):
        self.act(out, in_, AF.Ln, r, w, bias=self.eps_ap[0:out.shape[0], :], scale=scale)
        self.act(out, out, AF.Exp, w, w, scale=-0.5)


def _gammas():
    return 1.0 - 2.0 ** (-5.0 - np.arange(4, dtype=np.float64))


def host_consts(S, NB, PAST):
    RS = NB * 4
    c = {}
    c["ident"] = np.eye(128, dtype=np.float32).astype(NPBF)
    c["identf"] = np.eye(128, dtype=np.float32)

    def ropeN(pos):
        inv = 500000.0 ** (-np.arange(8, dtype=np.float32) / 8)
        ang = pos.astype(np.float32)[:, None] * inv[None, :]
        cs, sn = np.cos(ang), np.sin(ang)
        return np.concatenate([cs, cs, -sn, sn], axis=1).astype(np.float32)

    def ropeR(pos):
        inv = 10000.0 ** (-np.arange(32, dtype=np.float32) / 32)
        ang = pos.astype(np.float32)[:, None] * inv[None, :]
        cs, sn = np.cos(ang), np.sin(ang)
        return np.concatenate([cs, cs, -sn, sn], axis=1).astype(np.float32)

    pos_p = np.arange(S)
    pos_s = PAST + (np.arange(RS) % 4)
    c["ropeN_p"], c["ropeR_p"] = ropeN(pos_p), ropeR(pos_p)
    c["ropeN_s"], c["ropeR_s"] = ropeN(pos_s), ropeR(pos_s)
    k = np.arange(128)[:, None]
    q = np.arange(128)[None, :]
    causal = np.where(k <= q, 0.0, NEGM).astype(np.float32)
    winfirst = np.where(k > q, 0.0, NEGM).astype(np.float32)
    c["causal4"] = np.tile(causal, (1, 4)).astype(NPBF)
    c["winfirst4"] = np.tile(winfirst, (1, 4)).astype(NPBF)
    kk = np.arange(8)[:, None]
    base = np.where(q >= 16 * kk + 15, 0.0, NEGM).astype(np.float32)
    c["cbase4"] = np.tile(base, (1, 4)).astype(NPBF)
    sh = np.zeros((8, 272), np.float32)
    for i in range(8):
        sh[i, i + 128] = 1.0
    c["shid"] = sh.astype(NPBF)
    E = np.zeros((64, 4096), np.float32)
    for j in range(64):
        E[j, 64 * j:64 * j + 64] = 1.0
    c["E"] = E.astype(NPBF)
    ov = np.zeros((128, 4, 128), np.float32)
    for cc in range(512):
        for j in range(128):
            lo, hi = max(16 * cc, 64 * j), min(16 * cc + 32, 64 * j + 64)
            if hi > lo:
                ov[cc % 128, cc // 128, j] = (hi - lo) / 32.0
    c["ov"] = ov.reshape(128, 512).astype(NPBF)
    g = _gammas()
    i_ = np.arange(128)
    dec = np.zeros((128, 4, 128), np.float64)
    for h in range(4):
        d = i_[None, :] - i_[:, None]
        dec[:, h, :] = np.where(d >= 0, g[h] ** np.maximum(d, 0), 0.0) / 8.0
    c["decT"] = dec.reshape(128, 512).astype(np.float32)
    cross = np.zeros((64, 4, 128), np.float64)
    for h in range(4):
        cross[:, h, :] = (g[h] ** (i_ + 1.0))[None, :]
    c["crossT"] = cross.reshape(64, 512).astype(np.float32)
    c["wtab"] = np.stack([g[h] ** (127.0 - i_) / 8.0 for h in range(4)], axis=1).astype(np.float32)
    pat = np.zeros((128, 3), np.float32)
    pat[:64, 0] = 1e4
    pat[:, 1] = 1e4
    pat[64:, 2] = 1e4
    c["pat"] = pat
    c["iota"] = np.arange(128, dtype=np.float32).reshape(128, 1)
    s4 = np.arange(4)
    scaus = np.where(s4[:, None] <= s4[None, :], 0.0, NEGM).astype(np.float32)
    c["scausal4"] = np.tile(scaus, (1, 4)).astype(NPBF)
    swf = np.zeros((128, 4), np.float32)
    for w in range(4):
        for s in range(4):
            swf[w, s] = 0.0 if w > s else NEGM
    c["swinfirst4"] = np.tile(swf, (1, 4)).astype(NPBF)
    dec4 = np.zeros((4, 4, 4), np.float64)
    for h in range(4):
        d = s4[None, :] - s4[:, None]
        dec4[:, h, :] = np.where(d >= 0, g[h] ** np.maximum(d, 0), 0.0) / 8.0
    c["dec4T"] = dec4.reshape(4, 16).astype(np.float32)
    cross4 = np.zeros((64, 4, 4), np.float64)
    for h in range(4):
        cross4[:, h, :] = (g[h] ** (s4 + 1.0))[None, :]
    c["cross4T"] = cross4.reshape(64, 16).astype(np.float32)
    c["w4tab"] = np.stack([g[h] ** (3.0 - s4) / 8.0 for h in range(4)], axis=1).astype(np.float32)
    spat = np.zeros((4, 2), np.float32)
    spat[:, 0] = 1e4
    spat[:, 1] = 1e4
    c["spat"] = spat
    return c


CONST_SHAPES = None


import os
DBG = int(os.environ.get('K_DBG', '9'))


def build(S, NB, PAST, NPOOL, do_sample=True):
    NT = S // 128
    NPG = PAST // 128
    RS = NB * 4
    NCS = PAST // 16 - 1
    nc = bass.Bass("TRN2", target_bir_lowering=False)
    consts = host_consts(S, NB, PAST)

    def din(name, shape, dt=F32):
        return nc.dram_tensor(name, list(shape), dt, kind="ExternalInput").ap()

    def dout(name, shape):
        return nc.dram_tensor(name, list(shape), F32, kind="ExternalOutput").ap()

    cd = {}
    for k, v in consts.items():
        cd[k] = din("c_" + k, v.shape, BF16 if v.dtype == NPBF else F32)
    x_p = din("x_p", [S, 1024])
    x_s = din("x_s", [RS, 1024])
    c_pg = din("c_pg", [NPOOL * 128, 512])
    st_winK = din("st_winK", [NB, 128, 512])
    st_winV = din("st_winV", [NB, 512, 128])
    st_win = din("st_win", [NB, 512, 256])
    st_ret = din("st_ret", [64, NB * 512])
    st_conv = din("st_conv", [128, NFC * NB * 2])
    ptab = din("ptab", [1, NB * NPG], I32)
    nm_mix = din("nm_mix", [128, 8])
    w_in = din("w_in", [1024, D_IN])
    gq = din("gq", [128, 64])
    gk = din("gk", [128, 192])
    w1h = din("w1h", [128, 4096])
    w2h = din("w2h", [128, 128 + 128])
    peT = din("peT", [128, 64])
    w_out = din("w_out", [1024, 1024])
    nm_ffn = din("nm_ffn", [128, 8])
    w_fi = din("w_fi", [1024, 2 * D_FF])
    wconv = din("wconv", [128, NFC * 4])
    w_fo = din("w_fo", [D_FF, 1024])

    y_p = dout("y_p", [S, 1024])
    y_s = dout("y_s", [RS, 1024])
    o_kvc_p = dout("o_kvc_p", [S, 256])
    o_kvs_p = dout("o_kvs_p", [S, 256])
    o_kvw_p = dout("o_kvw_p", [512, 256])
    o_kvc_s = dout("o_kvc_s", [RS, 256])
    o_kvs_s = dout("o_kvs_s", [RS, 256])
    o_kvw_s = dout("o_kvw_s", [NB, 512, 256])
    o_ret_p = dout("o_ret_p", [64, 512])
    o_ret_s = dout("o_ret_s", [64, NB * 512])
    o_conv_p = dout("o_conv_p", [128, NFC * 2])
    o_conv_s = dout("o_conv_s", [128, NFC * NB * 2])

    P = Prog()
    with ExitStack() as es:
        B = Bld(nc, P, es)
        sb, ps = B.sb, B.ps
        Sb = [ps("S0", [128, 512], F32), ps("S1", [128, 512], F32)]
        Ab = [ps("A0", [128, 512], F32), ps("A1", [128, 512], F32), ps("A2", [128, 512], F32)]
        T0 = ps("T0", [128, 1024], BF16)
        Mb = [ps("M0", [128, 512], F32), ps("M1", [128, 512], F32)]
        ct = {}
        for k, v in consts.items():
            if k in ("ropeN_p", "ropeR_p", "ropeN_s", "ropeR_s", "identf"):
                continue
            ct[k] = sb("k_" + k, list(v.shape), BF16 if v.dtype == NPBF else F32)
            B.dma(ct[k][:], cd[k], [], ["k_" + k], "k_" + k)
        ident = ct["ident"]
        epsc = sb("epsc", [128, 1], F32)
        B.memset("vector", epsc[:], EPS, ["epsc"])
        B.eps_ap = epsc
        w_in_bf = sb("w_in_bf", [128, 8, D_IN], BF16)
        w_out_bf = sb("w_out_bf", [128, 8, 1024], BF16)
        W1 = sb("W1", [128, 2, 32, 64], BF16)
        w2 = sb("w2", [128, 256], BF16)
        peT_bf = sb("peT_bf", [128, 64], BF16)
        peterm = sb("peterm", [128, 2], F32)
        nm_sb = sb("nm_sb", [128, 8], F32)
        gq_sb = sb("gq_sb", [128, 64], F32)
        gk_sb = sb("gk_sb", [128, 192], F32)
        stage = [sb("stage0", [128, 1024], F32), sb("stage1", [128, 1024], F32)]
        B.dma(nm_sb[:], nm_mix, [], ["nm_sb"], "nm_sb")
        B.dma(gq_sb[:], gq, [], ["gq_sb"], "gq_sb")
        B.dma(gk_sb[:], gk, [], ["gk_sb"], "gk_sb")
        B.ts1("vector", gq_sb[:], gq_sb[:], 0.125, ALU.mult, ["gq_sb"], ["gq_sb"])
        si = 0
        for k in range(8):
            for c0 in range(0, D_IN, 1024):
                s_ = si % 2
                c1 = min(D_IN, c0 + 1024)
                B.dma(stage[s_][:, 0:c1 - c0], w_in[k * 128:(k + 1) * 128, c0:c1], [], ["stage%d" % s_], "stage%d" % s_)
                eng = "vector" if si % 2 == 0 else "gpsimd"
                B.ts1(eng, w_in_bf[:, k, c0:c1], stage[s_][:, 0:c1 - c0], nm_sb[:, k:k + 1], ALU.mult,
                      ["stage%d" % s_, "nm_sb"], ["w_in_bf"])
                si += 1
        for k in range(8):
            s_ = si % 2
            B.dma(stage[s_][:, 0:1024], w_out[k * 128:(k + 1) * 128, :], [], ["stage%d" % s_], "stage%d" % s_)
            B.cp("vector" if k % 2 == 0 else "gpsimd", w_out_bf[:, k, :], stage[s_][:, 0:1024], ["stage%d" % s_], ["w_out_bf"])
            si += 1
        for k in range(4):
            s_ = si % 2
            B.dma(stage[s_][:, 0:1024], w1h[:, k * 1024:(k + 1) * 1024], [], ["stage%d" % s_], "stage%d" % s_)
            B.cp("vector", W1[:, k // 2, (k % 2) * 16:(k % 2) * 16 + 16, :].rearrange("p l o -> p (l o)"), stage[s_][:, 0:1024],
                 ["stage%d" % s_], ["W1"])
            si += 1
        s_ = si % 2
        B.dma(stage[s_][:, 0:256], w2h, [], ["stage%d" % s_], "stage%d" % s_)
        B.cp("vector", w2[:, :], stage[s_][:, 0:256], ["stage%d" % s_], ["w2"])
        si += 1
        s_ = si % 2
        B.dma(stage[s_][:, 0:64], peT, [], ["stage%d" % s_], "stage%d" % s_)
        B.cp("vector", peT_bf[:], stage[s_][:, 0:64], ["stage%d" % s_], ["peT_bf"])
        si += 1
        for hp in range(0 if 'pe' in SKIP else 2):
            for k in range(2):
                for l in range(32):
                    B.mm(Mb[0][64 * hp:64 * hp + 64, k:k + 1], W1[64 * hp:64 * hp + 64, k, l, :],
                         peT_bf[64 * hp:64 * hp + 64, k * 32 + l:k * 32 + l + 1], l == 0, l == 31,
                         ["W1", "peT_bf"], ["M0"])
        if "pe" not in SKIP:
            B.cp("vector", peterm[:], Mb[0][:, 0:2], ["M0"], ["peterm"])

        LK = max(S, PAST)
        ksT = sb("ksT", [128, LK + 4], BF16)
        vs = sb("vs", [128, LK // 128 + 1, 2, 65], BF16)
        kwT = sb("kwT", [128, 8, 128], BF16)
        vw = sb("vw", [128, 8, 2, 65], BF16)
        kcT = sb("kcT", [128, 512], BF16)
        hgT = sb("hgT", [128, 2, 512], BF16)
        vc = sb("vc", [128, 4, 2, 65], BF16)
        cmpT = sb("cmpT", [128, 2, 16 + 1024], BF16)
        if "ms" not in SKIP:
            B.memset("vector", vs[:, :, :, 64:65], 1.0, ["vs"])
            B.memset("vector", vw[:, :, :, 64:65], 1.0, ["vw"])
            B.memset("vector", vc[:, :, :, 64:65], 1.0, ["vc"])
        Sf = sb("Sf", [64, 4, 128], F32)
        Sbf = sb("Sbf", [64, 4, 128], BF16)
        xs_ = [sb("xs0", [128, 1024], F32), stage[0]]
        xres = ["xs0", "stage0"]
        ropeN = sb("ropeN", [128, 32], F32)
        ropeR = sb("ropeR", [128, 128], F32)
        st1 = sb("st1", [128, 16], F32)
        junk = sb("junk", [128, 1024], BF16)
        xn = sb("xn", [128, 1024], BF16)
        xnT = sb("xnT", [128, 8, 128], BF16)
        qn = sb("qn", [128, 512], F32)
        q_bf = sb("q_bf", [128, 8, 128], BF16)
        B.memset("vector", q_bf[:], 0.0, ["q_bf"])
        qT = sb("qT", [128, 8, 128], BF16)
        kvo = sb("kvo", [128, 768], F32)
        k_bf = sb("k_bf", [128, 3, 128], BF16)
        cv_bf = sb("cv_bf", [128, 128], BF16)
        rtmp = sb("rtmp", [128, 256], F32)
        rtmp2 = sb("rtmp2", [128, 256], F32)
        rqk_bf = sb("rqk_bf", [128, 512], BF16)
        rqT = sb("rqT", [64, 4, 128], BF16)
        rkT = sb("rkT", [64, 4, 128], BF16)
        rv_bf = sb("rv_bf", [128, 512], BF16)
        rg_f = sb("rg_f", [128, 512], F32)
        oret = sb("oret", [128, 512], F32)
        rg_e = oret

        gates_l = [sb("gates0", [128, 24], F32), sb("gates1", [128, 24], F32)]
        ebuf = [sb("e0", [128, 512], BF16), sb("e1", [128, 512], BF16)]
        hx = sb("hx", [128, 32], F32)
        hg_bf = sb("hg_bf", [128, 2, 128], BF16)
        hwork = sb("hwork", [128, 3, 256], F32)
        ocmp = sb("ocmp", [128, 4, 64], F32)
        osel = sb("osel", [128, 4, 64], F32)
        owin = sb("owin", [128, 4, 64], F32)
        rs_c = sb("rs_c", [128, 12], F32)
        imp = sb("imp", [128, 128], F32)
        imp2 = sb("imp2", [128, 128], F32)
        mx = sb("mx", [128, 16], F32)
        negq = sb("negq", [128, 128], BF16)
        negT4 = sb("negT4", [64, 2, 512], BF16)
        mixed = sb("mixed", [128, 1024], BF16)
        mixT = xnT
        sTm = sb("sTm", [128, 512], BF16)
        qcr = sb("qcr", [64, 4, 128], BF16)
        kwk = sb("kwk", [128, 256], BF16)


        B.memset("vector", Sf[:], 0.0, ["Sf"])
        B.memset("vector", Sbf[:], 0.0, ["Sbf"])
        B.memset("vector", cmpT[:, :, 0:16], 0.0, ["cmpT"])

        ecnt = [0]
        scnt = [0]

        def project(R, x_dram, xslot, ropeN_d, ropeR_d, kv_douts, tg, gslot=0):
            gates = gates_l[gslot]
            gres = "gates%d" % gslot
            xr = xres[xslot]
            x_sb = xs_[xslot]
            B.dma(x_sb[0:R, :], x_dram, [], [xr], xr)
            B.dma(ropeN[0:R, :], ropeN_d, [], ["ropeN"], "ropeN")
            B.dma(ropeR[0:R, :], ropeR_d, [], ["ropeR"], "ropeR")
            B.act(junk[0:R, :], x_sb[0:R, :], AF.Square, [xr], ["junk", "st1"], accum=st1[0:R, 0:1])
            B.rsqrt(st1[0:R, 1:2], st1[0:R, 0:1], 1.0 / 1024, ["st1"], ["st1"])
            yield
            B.act(xn[0:R, :], x_sb[0:R, :], AF.Identity, [xr, "st1"], ["xn"], scale=st1[0:R, 1:2])
            yield
            for k in range(8):
                B.tr(T0[:, k * 128:k * 128 + R], xn[0:R, k * 128:(k + 1) * 128], ident[0:R, 0:R], ["xn", "k_ident"], ["T0"])
            B.cp("vector", xnT[:, :, 0:R], T0[:, :].rearrange("p (k t) -> p k t", k=8)[:, :, 0:R], ["T0"], ["xnT"])
            yield
            def zchunk(n, bank):
                c0 = n * 512
                w = min(512, D_IN - c0)
                for k in range(8):
                    B.mm(Mb[bank][0:R, 0:w], xnT[:, k, 0:R], w_in_bf[:, k, c0:c0 + w], k == 0, k == 7,
                         ["xnT", "w_in_bf"], ["M%d" % bank])
                return w
            zchunk(0, 0)
            zchunk(1, 1)
            zq = Mb[0]
            yield
            B.act(junk[0:R, 0:512], zq[0:R, :], AF.Square, ["M0"], ["junk"])
            B.red(st1[0:R, 2:10], junk[0:R, 0:512].rearrange("p (h d) -> p h d", h=8), ["junk"], ["st1"])
            B.rsqrt(st1[0:R, 2:10], st1[0:R, 2:10], 1.0 / 64, ["st1"], ["st1"])
            qn3 = qn[0:R, :].rearrange("p (h d) -> p h d", h=8)
            B.tt("vector", qn3, zq[0:R, :].rearrange("p (h d) -> p h d", h=8),
                 st1[0:R, 2:10].unsqueeze(2).broadcast_to([R, 8, 64]), ALU.mult, ["M0", "st1"], ["qn"])
            B.tt("gpsimd", qn3, qn3, gq_sb[0:R, :].unsqueeze(1).broadcast_to([R, 8, 64]), ALU.mult, ["qn", "gq_sb"], ["qn"])
            yield
            rope16(R, qn3, 8, "qn")
            yield
            for g_ in range(2):
                B.cp("scalar", q_bf[0:R, 4 * g_:4 * g_ + 4, 64 * g_:64 * g_ + 64], qn3[:, 4 * g_:4 * g_ + 4, :], ["qn"], ["q_bf"])
            B.cp("scalar", kvo[0:R, 0:512], Mb[1][0:R, :], ["M1"], ["kvo"])
            yield
            zchunk(2, 0)
            B.cp("scalar", kvo[0:R, 512:768], Mb[0][0:R, 0:256], ["M0"], ["kvo"])
            ropeR64(R, Mb[0][0:R, 256:512], rqk_bf[0:R, 0:256], "M0")
            yield
            zchunk(3, 1)
            if "c3r" not in SKIP:
                ropeR64(R, Mb[1][0:R, 0:256], rqk_bf[0:R, 256:512], "M1")
            if "c3a" not in SKIP:
                B.cp("scalar", rv_bf[0:R, 0:256], Mb[1][0:R, 256:512], ["M1"], ["rv_bf"])
            yield
            zchunk(4, 0)
            B.cp("scalar", rv_bf[0:R, 256:512], Mb[0][0:R, 0:256], ["M0"], ["rv_bf"])
            B.cp("vector", rg_f[0:R, 0:256], Mb[0][0:R, 256:512], ["M0"], ["rg_f"])
            B.act(rg_e[0:R, 0:256], Mb[0][0:R, 256:512], AF.Exp, ["M0"], ["oret"], scale=-1.0)
            yield
            zchunk(5, 1)
            B.cp("vector", rg_f[0:R, 256:512], Mb[1][0:R, 0:256], ["M1"], ["rg_f"])
            B.act(rg_e[0:R, 256:512], Mb[1][0:R, 0:256], AF.Exp, ["M1"], ["oret"], scale=-1.0)
            B.act(gates[0:R, :], Mb[1][0:R, 256:280], AF.Exp, ["M1"], [gres], scale=-1.0)
            B.ts1("vector", gates[0:R, :], gates[0:R, :], 1.0, ALU.add, [gres], [gres])
            B.recip(gates[0:R, :], gates[0:R, :], [gres], [gres])
            yield
            B.ts1("gpsimd", rg_e[0:R, :], rg_e[0:R, :], 1.0, ALU.add, ["oret"], ["oret"])
            B.recip(rg_e[0:R, :], rg_e[0:R, :], ["oret"], ["oret"])
            B.tt("gpsimd", rg_f[0:R, :], rg_f[0:R, :], rg_e[0:R, :], ALU.mult, ["rg_f", "oret"], ["rg_f"])
            yield
            k4 = kvo[0:R, :].rearrange("p (b v g d) -> p b v g d", b=3, v=2, g=2)[:, :, 0, :, :]
            B.act(junk[0:R, 0:384].rearrange("p (b g d) -> p b g d", b=3, g=2), k4, AF.Square, ["kvo"], ["junk"])
            B.red(st1[0:R, 10:16], junk[0:R, 0:384].rearrange("p (h d) -> p h d", h=6), ["junk"], ["st1"])
            B.rsqrt(st1[0:R, 10:16], st1[0:R, 10:16], 1.0 / 64, ["st1"], ["st1"])
            B.tt("vector", k4, k4, st1[0:R, 10:16].rearrange("p (b g) -> p b g", b=3).unsqueeze(3).broadcast_to([R, 3, 2, 64]),
                 ALU.mult, ["kvo", "st1"], ["kvo"])
            B.tt("gpsimd", k4, k4, gk_sb[0:R, :].rearrange("p (b d) -> p b d", b=3).unsqueeze(2).broadcast_to([R, 3, 2, 64]),
                 ALU.mult, ["kvo", "gk_sb"], ["kvo"])
            for b_ in range(3):
                rope16(R, k4[:, b_, :, :], 2, "kvo")
            yield
            for i_, d_ in enumerate(kv_douts):
                if d_ is not None:
                    B.dma(d_, kvo[0:R, i_ * 256:(i_ + 1) * 256], ["kvo"], [], "kvo_out")
            B.cp("vector", k_bf[0:R, :, :].rearrange("p b (g d) -> p b g d", g=2), k4, ["kvo"], ["k_bf"])
            B.cp("gpsimd", cv_bf[0:R, :], kvo[0:R, 128:256], ["kvo"], ["cv_bf"])

        def rope16(R, x3, H, res):
            C = ropeN[0:R, 0:16].unsqueeze(1).broadcast_to([R, H, 16])
            nsn = ropeN[0:R, 16:24].unsqueeze(1).broadcast_to([R, H, 8])
            psn = ropeN[0:R, 24:32].unsqueeze(1).broadcast_to([R, H, 8])
            t1 = rtmp[0:R, 0:H * 16].rearrange("p (h d) -> p h d", h=H)
            t2 = rtmp2[0:R, 0:H * 16].rearrange("p (h d) -> p h d", h=H)
            B.tt("vector", t1, x3[:, :, 0:16], C, ALU.mult, [res, "ropeN"], ["rtmp"])
            B.tt("gpsimd", t2[:, :, 0:8], x3[:, :, 8:16], nsn, ALU.mult, [res, "ropeN"], ["rtmp2"])
            B.tt("gpsimd", t2[:, :, 8:16], x3[:, :, 0:8], psn, ALU.mult, [res, "ropeN"], ["rtmp2"])
            B.tt("vector", x3[:, :, 0:16], t1, t2, ALU.add, ["rtmp", "rtmp2"], [res])

        def ropeR64(R, zsrc, dst_bf, res):
            H = 4
            z3 = zsrc.rearrange("p (h d) -> p h d", h=H)
            C = ropeR[0:R, 0:64].unsqueeze(1).broadcast_to([R, H, 64])
            nsn = ropeR[0:R, 64:96].unsqueeze(1).broadcast_to([R, H, 32])
            psn = ropeR[0:R, 96:128].unsqueeze(1).broadcast_to([R, H, 32])
            t1 = rtmp[0:R, 0:256].rearrange("p (h d) -> p h d", h=H)
            t2 = rtmp2[0:R, 0:256].rearrange("p (h d) -> p h d", h=H)
            B.tt("vector", t1, z3, C, ALU.mult, [res, "ropeR"], ["rtmp"])
            B.tt("vector", t2[:, :, 0:32], z3[:, :, 32:64], nsn, ALU.mult, [res, "ropeR"], ["rtmp2"])
            B.tt("vector", t2[:, :, 32:64], z3[:, :, 0:32], psn, ALU.mult, [res, "ropeR"], ["rtmp2"])
            B.tt("gpsimd", dst_bf.rearrange("p (h d) -> p h d", h=H), t1, t2, ALU.add, ["rtmp", "rtmp2"], ["rqk_bf"])

        def compress(ncols, first_c, skip_first):
            n0 = 1 if skip_first else 0
            nb = ncols - n0
            c0 = first_c + n0
            for kv in range(2):
                i = 0
                for a in range(2):
                    for s in range(16):
                        l = 16 * a + s
                        off = 16 * (n0 + a) + s
                        for g in range(2):
                            rhs = cmpT[64 * g:64 * g + 64, kv, off:off + 16 * (nb - 1) + 1:16]
                            B.mm(Ab[2][64 * g:64 * g + 64, kv * 128:kv * 128 + nb], W1[64 * g:64 * g + 64, kv, l, :], rhs,
                                 i == 0, i == 31, ["W1", "cmpT"], ["A2"], nocheck=True)
                        i += 1
            if 'cg' in SKIP:
                return
            hw = hwork
            for kv in range(2):
                B.act(hw[:, 0, kv * 128:kv * 128 + nb], Ab[2][:, kv * 128:kv * 128 + nb], AF.Identity, ["A2", "peterm"], ["hwork"],
                      bias=peterm[:, kv:kv + 1])
            xx = hw[:, 0, :].rearrange("p (k c) -> p k c", k=2)[:, :, 0:nb]
            t_ = hw[:, 1, :].rearrange("p (k c) -> p k c", k=2)[:, :, 0:nb]
            u_ = hw[:, 2, :].rearrange("p (k c) -> p k c", k=2)[:, :, 0:nb]
            B.tt("vector", t_, xx, xx, ALU.mult, ["hwork"], ["hwork"])
            B.ts("vector", t_, t_, 0.044715, 1.0, ALU.mult, ALU.add, ["hwork"], ["hwork"])
            B.tt("vector", t_, t_, xx, ALU.mult, ["hwork"], ["hwork"])
            B.act(u_, t_, AF.Exp, ["hwork"], ["hwork"], scale=-2.0 * math.sqrt(2.0 / math.pi))
            B.ts1("vector", u_, u_, 1.0, ALU.add, ["hwork"], ["hwork"])
            B.recip(u_, u_, ["hwork"], ["hwork"])
            B.tt("vector", hgT[:, :, c0:c0 + nb], xx, u_, ALU.mult, ["hwork"], ["hgT"])
            if 'ck' in SKIP:
                return
            B.mm(Ab[2][:, 256:256 + nb], w2[:, 0:128], hgT[:, 0, c0:c0 + nb], True, True, ["w2", "hgT"], ["A2"])
            B.cp("vector", kcT[:, c0:c0 + nb], Ab[2][:, 256:256 + nb], ["A2"], ["kcT"])
            if 'cv' in SKIP:
                return
            c_last = c0 + nb - 1
            for ctile in range(c0 // 128, c_last // 128 + 1):
                M = min(128, c_last + 1 - 128 * ctile)
                B.mm(Ab[2][0:M, 0:128], hgT[:, 1, 128 * ctile:128 * ctile + M], w2[:, 128:256], True, True, ["w2", "hgT"], ["A2"])
                B.cp("vector", vc[0:M, ctile, :, 0:64], Ab[2][0:M, 0:128].rearrange("p (g d) -> p g d", g=2), ["A2"], ["vc"])

        def emask(kt, QN):
            return (ct["E"][:, (kt % 32) * 128:(kt % 32 + 1) * 128], negT4[:, kt // 32, 0:4 * QN], ["k_E", "negT4"])

        def branch(QN, qrhs, qres, tiles, acc, accres, W, tagU=None):
            n = len(tiles)
            NQ = 4 * QN
            CH = max(1, 512 // NQ)
            first = True
            for c0 in range(0, n, CH):
                chunk = tiles[c0:c0 + CH]
                sbk = scnt[0] % 2
                scnt[0] += 1
                Sx = Sb[sbk]
                sres = "S%d" % sbk
                Mmax = max(tl["M"] for tl in chunk)
                for j, tl in enumerate(chunk):
                    M = tl["M"]
                    masks = tl.get("masks", [])
                    B.mm(Sx[0:M, j * NQ:(j + 1) * NQ], tl["kT"], qrhs, True, len(masks) == 0, [tl["kres"], qres], [sres])
                    for mi, (ml, mr, mres) in enumerate(masks):
                        B.mm(Sx[0:M, j * NQ:(j + 1) * NQ], ml, mr, False, mi == len(masks) - 1, mres, [sres])
                eb = ecnt[0] % 2
                ecnt[0] += 1
                eres = "e%d" % eb
                B.act(ebuf[eb][0:Mmax, 0:len(chunk) * NQ], Sx[0:Mmax, 0:len(chunk) * NQ], AF.Exp, [sres], [eres])
                for j, tl in enumerate(chunk):
                    M = tl["M"]
                    last = (c0 + j == n - 1)
                    for r in range(4):
                        B.mm(acc[:, r, :], ebuf[eb][0:M, j * NQ + r * QN:j * NQ + (r + 1) * QN], tl["v"], first and r == 0, last,
                             [eres, tl["vres"]], [accres], nocheck=True)
                    if tagU is not None:
                        for r in range(4):
                            B.mm(tagU[0][:, r, :], ebuf[eb][0:M, j * NQ + r * QN:j * NQ + (r + 1) * QN], tl["u"], first and r == 0, last,
                                 [eres, "k_ov"], [tagU[1]], nocheck=True)
                    first = False
                yield

        def finish_branch(QN, acc, accres, odst, ores, col):
            B.ts1("vector", rs_c[0:QN, col:col + 4], acc[:, :, 64], 1e-30, ALU.max, [accres], ["rs_c"])
            B.recip(rs_c[0:QN, col:col + 4], rs_c[0:QN, col:col + 4], ["rs_c"], ["rs_c"])
            B.tt("vector", odst[0:QN, :, :], acc[:, :, 0:64], rs_c[0:QN, col:col + 4].unsqueeze(2).broadcast_to([QN, 4, 64]),
                 ALU.mult, [accres, "rs_c"], [ores])

        def combine(QN, g, gates_ap, gres="gates0"):
            g3 = gates_ap.rearrange("p (h b) -> p h b", b=3)[:, 4 * g:4 * g + 4, :]
            for bi, (o_, ores) in enumerate(((ocmp, "ocmp"), (osel, "osel"), (owin, "owin"))):
                B.tt("vector" if bi != 1 else "gpsimd", o_[0:QN, :, :], o_[0:QN, :, :],
                     g3[:, :, bi:bi + 1].broadcast_to([QN, 4, 64]), ALU.mult, [ores, gres], [ores])
            B.tt("vector", ocmp[0:QN, :, :], ocmp[0:QN, :, :], osel[0:QN, :, :], ALU.add, ["ocmp", "osel"], ["ocmp"])
            B.tt("vector", mixed[0:QN, g * 256:(g + 1) * 256].rearrange("p (r d) -> p r d", r=4), ocmp[0:QN, :, :], owin[0:QN, :, :],
                 ALU.add, ["ocmp", "owin"], ["mixed"])

        def select_blocks(QN, accU, ures, rs_col, bonus_fn, kth):
            B.ts1("vector", imp[0:QN, :], accU[:, 0, :], rs_c[0:QN, rs_col:rs_col + 1], ALU.mult, [ures, "rs_c"], ["imp"])
            for r in range(1, 4):
                B.stt("vector", imp[0:QN, :], accU[:, r, :], rs_c[0:QN, rs_col + r:rs_col + r + 1], imp[0:QN, :],
                      ALU.mult, ALU.add, [ures, "rs_c", "imp"], ["imp"])
            bonus_fn()
            B.P.add("vector", lambda e: e.max(out=mx[0:QN, 0:8], in_=imp[0:QN, :]), ["imp"], ["mx"])
            B.P.add("vector", lambda e: e.match_replace(out=imp2[0:QN, :], in_to_replace=mx[0:QN, 0:8], in_values=imp[0:QN, :],
                                                        imm_value=-1e30), ["imp", "mx"], ["imp2"])
            B.P.add("vector", lambda e: e.max(out=mx[0:QN, 8:16], in_=imp2[0:QN, :]), ["imp2"], ["mx"])
            B.ts1("vector", imp2[0:QN, :], imp[0:QN, :], mx[0:QN, 8 + kth - 9:8 + kth - 8], ALU.is_ge, ["imp", "mx"], ["imp2"])
            B.ts("vector", negq[0:QN, :], imp2[0:QN, :], -1.0, -NEGM, ALU.add, ALU.mult, ["imp2"], ["negq"])
            for hf in range(2):
                B.tr(T0[0:64, hf * 128:hf * 128 + QN], negq[0:QN, hf * 64:(hf + 1) * 64], ident[0:QN, 0:QN], ["negq", "k_ident"], ["T0"])
            for hf in range(2):
                B.cp("vector", negT4[:, hf, 0:4 * QN].rearrange("p (r q) -> p r q", r=4),
                     T0[0:64, hf * 128:hf * 128 + QN].unsqueeze(1).broadcast_to([64, 4, QN]), ["T0"], ["negT4"])

        def retention(R, t):
            gam = [1.0 - 2.0 ** (-5.0 - h) for h in range(4)]
            for h in range(4):
                B.mm(Mb[0][0:R, h * 128:h * 128 + R], rkT[:, h, 0:R], rqT[:, h, 0:R], True, True, ["rkT", "rqT"], ["M0"])
            B.tt("vector", sTm[0:R, :].rearrange("p (h i) -> p h i", h=4)[:, :, 0:R],
                 Mb[0][0:R, :].rearrange("p (h i) -> p h i", h=4)[:, :, 0:R],
                 ct["decT"][0:R, :].rearrange("p (h i) -> p h i", h=4)[:, :, 0:R], ALU.mult, ["M0", "k_decT"], ["sTm"])
            B.tt("gpsimd", qcr[:, :, 0:R], rqT[:, :, 0:R],
                 ct["crossT"][:, :].rearrange("p (h i) -> p h i", h=4)[:, :, 0:R], ALU.mult, ["rqT", "k_crossT"], ["qcr"])
            for h in range(4):
                B.mm(Mb[1][0:R, h * 128:(h + 1) * 128], sTm[0:R, h * 128:h * 128 + R], rv_bf[0:R, h * 128:(h + 1) * 128],
                     True, False, ["sTm", "rv_bf"], ["M1"])
                B.mm(Mb[1][0:R, h * 128:(h + 1) * 128], qcr[:, h, 0:R], Sbf[:, h, :], False, True, ["qcr", "Sbf"], ["M1"])
            B.tt("vector", kwk[0:R, :].rearrange("p (h d) -> p h d", h=4), rqk_bf[0:R, 256:512].rearrange("p (h d) -> p h d", h=4),
                 ct["wtab"][0:R, :].unsqueeze(2).broadcast_to([R, 4, 64]), ALU.mult, ["rqk_bf", "k_wtab"], ["kwk"])
            for h in range(4):
                B.mm(Mb[0][0:64, h * 128:(h + 1) * 128], kwk[0:R, h * 64:(h + 1) * 64],
                     rv_bf[0:R, h * 128:(h + 1) * 128], True, True, ["kwk", "rv_bf", "sTm"], ["M0"])
            for h in range(4):
                B.stt("vector", Sf[:, h, :], Sf[:, h, :], float(gam[h] ** 128), Mb[0][0:64, h * 128:(h + 1) * 128],
                      ALU.mult, ALU.add, ["Sf", "M0"], ["Sf"])
            B.cp("gpsimd", Sbf[:, :, :], Sf[:, :, :], ["Sf"], ["Sbf"])
            B.act(junk[0:R, 0:512], Mb[1][0:R, :], AF.Square, ["M1"], ["junk"])
            B.red(st1[0:R, 2:6], junk[0:R, 0:512].rearrange("p (h e) -> p h e", h=4), ["junk"], ["st1"])
            B.rsqrt(st1[0:R, 2:6], st1[0:R, 2:6], 1.0 / 128, ["st1"], ["st1"])
            B.tt("vector", oret[0:R, :].rearrange("p (h e) -> p h e", h=4), Mb[1][0:R, :].rearrange("p (h e) -> p h e", h=4),
                 st1[0:R, 2:6].unsqueeze(2).broadcast_to([R, 4, 128]), ALU.mult, ["M1", "st1"], ["oret"])
            B.tt("gpsimd", mixed[0:R, 512:1024], oret[0:R, :], rg_f[0:R, :], ALU.mult, ["oret", "rg_f"], ["mixed"])

        def out_proj(R, x_sb, xr, y_dst):
            for k in range(8):
                B.tr(T0[:, k * 128:k * 128 + R], mixed[0:R, k * 128:(k + 1) * 128], ident[0:R, 0:R], ["mixed", "k_ident"], ["T0"])
            B.cp("vector", mixT[:, :, 0:R], T0[:, :].rearrange("p (k t) -> p k t", k=8)[:, :, 0:R], ["T0"], ["xnT"])
            for nh in range(2):
                for k in range(8):
                    B.mm(Mb[nh][0:R, :], mixT[:, k, 0:R], w_out_bf[:, k, nh * 512:(nh + 1) * 512], k == 0, k == 7,
                         ["xnT", "w_out_bf"], ["M%d" % nh])
                B.tt("vector", x_sb[0:R, nh * 512:(nh + 1) * 512], Mb[nh][0:R, :], x_sb[0:R, nh * 512:(nh + 1) * 512], ALU.add,
                     ["M%d" % nh, xr], [xr])
            B.dma(y_dst, x_sb[0:R, :], [xr], [], xr)

        def run(gen):
            for _ in gen:
                pass

        def make_proj(t):
            last4 = t >= NT - 4
            kvd = [o_kvc_p[t * 128:(t + 1) * 128, :], o_kvs_p[t * 128:(t + 1) * 128, :],
                   o_kvw_p[(t - (NT - 4)) * 128:(t - (NT - 4) + 1) * 128, :] if (last4 and NT >= 4) else None]
            if NT < 4:
                kvd[2] = o_kvw_p[t * 128:(t + 1) * 128, :]
            return project(128, x_p[t * 128:(t + 1) * 128, :], t % 2, cd["ropeN_p"][t * 128:(t + 1) * 128, :],
                           cd["ropeR_p"][t * 128:(t + 1) * 128, :], kvd, "p%d" % t, gslot=t % 2)

        def interleave(gA, gP, nA):
            if gP is None:
                run(gA)
                return
            k = max(1, nA // 12)
            while True:
                try:
                    next(gP)
                except StopIteration:
                    break
                for _ in range(k):
                    try:
                        next(gA)
                    except StopIteration:
                        break
            run(gA)

        def prep(t):
            for h in range(8):
                B.tr(T0[:, h * 128:(h + 1) * 128], q_bf[:, h, :], ident[:, :], ["q_bf", "k_ident"], ["T0"])
            B.cp("vector", qT[:, :, :], T0[:, :].rearrange("p (h t) -> p h t", h=8), ["T0"], ["qT"])
            B.tr(T0[:, 0:128], k_bf[:, 1, :], ident[:, :], ["k_bf", "k_ident"], ["T0"])
            B.tr(T0[:, 128:256], k_bf[:, 2, :], ident[:, :], ["k_bf", "k_ident"], ["T0"])
            B.tr(T0[:, 4 * 128:5 * 128], k_bf[:, 0, :], ident[:, :], ["k_bf", "k_ident"], ["T0"])
            B.tr(T0[:, 5 * 128:6 * 128], cv_bf[:, :], ident[:, :], ["cv_bf", "k_ident"], ["T0"])
            wsl = t % 8
            B.cp("vector", ksT[:, t * 128:(t + 1) * 128], T0[:, 0:128], ["T0"], ["ksT"])
            B.cp("vector", kwT[:, wsl, :], T0[:, 128:256], ["T0"], ["kwT"])
            if t > 0:
                B.cp("vector", cmpT[:, :, 0:16], cmpT[:, :, 128:144], ["cmpT"], ["cmpT"])
            B.cp("vector", cmpT[:, :, 16:144], T0[:, 512:768].rearrange("p (k t) -> p k t", k=2), ["T0"], ["cmpT"])
            B.cp("gpsimd", vs[:, t, :, 0:64], kvo[:, 256:512].rearrange("p (v g d) -> p v g d", v=2, g=2)[:, 1, :, :], ["kvo"], ["vs"])
            B.cp("gpsimd", vw[:, wsl, :, 0:64], kvo[:, 512:768].rearrange("p (v g d) -> p v g d", v=2, g=2)[:, 1, :, :], ["kvo"], ["vw"])
            for j in range(8):
                B.tr(T0[0:64, j * 128:(j + 1) * 128], rqk_bf[:, j * 64:(j + 1) * 64], ident[:, :], ["rqk_bf", "k_ident"], ["T0"])
            B.cp("vector", rqT[:, :, :], T0[0:64, 0:512].rearrange("p (h t) -> p h t", h=4), ["T0"], ["rqT"])
            B.cp("vector", rkT[:, :, :], T0[0:64, 512:1024].rearrange("p (h t) -> p h t", h=4), ["T0"], ["rkT"])
            if 'cmp' not in SKIP:
                compress(8, 8 * t - 1, t == 0)

        def attn(t):
            ncb = 8 * t + 7
            for g in range(2 if DBG >= 4 else 0):
                qrhs = qT[:, 4 * g:4 * g + 4, :].rearrange("p r t -> p (r t)")
                tiles = []
                nct = (ncb + 127) // 128
                for ctile in range(nct):
                    M = min(128, ncb - 128 * ctile)
                    delta = t - 16 * ctile
                    masks = []
                    if delta <= 16:
                        a = 129 - 8 * delta
                        masks.append((ct["shid"][0:8, a:a + M], ct["cbase4"][0:8, :], ["k_shid", "k_cbase4"]))
                    tiles.append(dict(M=M, kT=kcT[:, 128 * ctile:128 * ctile + M], kres="kcT", masks=masks,
                                      v=vc[0:M, ctile, g, :], vres="vc", u=ct["ov"][0:M, ctile * 128:(ctile + 1) * 128]))
                accC = Ab[0][:, 0:260].rearrange("p (r w) -> p r w", r=4)
                accU = Ab[1][:, :].rearrange("p (r w) -> p r w", r=4)
                yield from branch(128, qrhs, "qT", tiles, accC, "A0", 65, tagU=(accU, "A1"))
                finish_branch(128, accC, "A0", ocmp, "ocmp", 0)

                def bonus(t=t):
                    lo = max(0, 2 * t - 1)
                    po = lo - (2 * t - 1)
                    B.tt("vector", imp[:, lo:2 * t + 2], imp[:, lo:2 * t + 2], ct["pat"][:, po:3], ALU.add, ["imp", "k_pat"], ["imp"])
                    B.ts1("vector", imp[:, 0:1], imp[:, 0:1], 1e4, ALU.add, ["imp"], ["imp"])
                select_blocks(128, accU, "A1", 0, bonus, 16)
                tiles = []
                for wt in range(max(0, t - 4), t + 1):
                    masks = []
                    if wt == t - 4:
                        masks.append((ident[:, :], ct["winfirst4"][:, :], ["k_ident", "k_winfirst4"]))
                    if wt == t:
                        masks.append((ident[:, :], ct["causal4"][:, :], ["k_ident", "k_causal4"]))
                    tiles.append(dict(M=128, kT=kwT[:, wt % 8, :], kres="kwT", masks=masks,
                                      v=vw[:, wt % 8, g, :], vres="vw"))
                accW = Ab[2][:, 0:260].rearrange("p (r w) -> p r w", r=4)
                yield from branch(128, qrhs, "qT", tiles, accW, "A2", 65)
                finish_branch(128, accW, "A2", owin, "owin", 8)
                tiles = []
                for kt in range(t + 1):
                    masks = [emask(kt, 128)]
                    if kt == t:
                        masks.append((ident[:, :], ct["causal4"][:, :], ["k_ident", "k_causal4"]))
                    tiles.append(dict(M=128, kT=ksT[:, kt * 128:(kt + 1) * 128], kres="ksT", masks=masks,
                                      v=vs[:, kt, g, :], vres="vs"))
                accS = Ab[0][:, 0:260].rearrange("p (r w) -> p r w", r=4)
                yield from branch(128, qrhs, "qT", tiles, accS, "A0", 65)
                finish_branch(128, accS, "A0", osel, "osel", 4)
                combine(128, g, gates_l[t % 2][:, :], "gates%d" % (t % 2))

        if NT > 0 and DBG >= 2:
            run(make_proj(0))
            for t in range(NT):
                prep(t)
                retention(128, t)
                interleave(attn(t), make_proj(t + 1) if t + 1 < NT else None, 2 * (t + 8))
                out_proj(128, xs_[t % 2], xres[t % 2], y_p[t * 128:(t + 1) * 128, :])
            if True:
                B.dma(o_ret_p, Sf[:, :, :].rearrange("p h e -> p (h e)"), ["Sf"], [], "o_ret_p")


        if NB > 0 and DBG >= 7:
            NS = PAST // 64
            GP = min(8, NPG)
            xslot = 0
            xr = xres[xslot]
            qTs = sb("qTs", [128, NB, 8, 4], BF16)
            ksTn = sb("ksTn", [128, RS], BF16)
            kwTn = sb("kwTn", [128, RS], BF16)
            vsrc = xn
            vrow = sb("vrow", [4, 1024], BF16)
            pt_i = sb("pt_i", [128, NPG], I32)
            pt_f = sb("pt_f", [128, NPG], F32)
            idx = sb("idx", [128, NPG], I32)
            idx64 = sb("idx64", [128, NPG], I32)
            kwTs = sb("kwTs", [128, 516], BF16)
            vws = sb("vws", [128, 5, 2, 65], BF16)
            wstV = qn
            oall = sb("oall", [128, 3, 2, 256], BF16)
            ob16 = sb("ob16", [4, 3, 256], BF16)
            Sfb = Sf
            Sbb = Sbf
            sTm4 = sb("sTm4", [4, 16], BF16)
            kwk4 = sb("kwk4", [4, 256], BF16)
            oret4 = sb("oret4", [4, 512], F32)
            oretS = oret
            B.memset("vector", vws[:, :, :, 64:65], 1.0, ["vws"])
            pgb = [stage[0][:, 0:512], stage[0][:, 512:1024], stage[1][:, 512:1024]]
            wstK = stage[1][:, 0:512]

            run(project(RS, x_s, xslot, cd["ropeN_s"], cd["ropeR_s"], [o_kvc_s, o_kvs_s, None], "s", gslot=0))
            B.dma(o_kvw_s[:, 0:508, :], st_win[:, 4:512, :], [], [], "kvw_d2d")
            for b in range(NB):
                B.dma(o_kvw_s[b, 508:512, :], kvo[4 * b:4 * b + 4, 512:768], ["kvo"], [], "kvo_out")
            idr = ident[0:RS, 0:RS]
            for h in range(8):
                B.tr(T0[:, h * 128:h * 128 + RS], q_bf[0:RS, h, :], idr, ["q_bf", "k_ident"], ["T0"])
            B.cp("vector", qTs[:, :, :, :].rearrange("p b h s -> p h b s"),
                 T0[:, :].rearrange("p (h t) -> p h t", h=8)[:, :, 0:RS].rearrange("p h (b s) -> p h b s", s=4), ["T0"], ["qTs"])
            B.tr(T0[:, 0:RS], k_bf[0:RS, 1, :], idr, ["k_bf", "k_ident"], ["T0"])
            B.tr(T0[:, 128:128 + RS], k_bf[0:RS, 2, :], idr, ["k_bf", "k_ident"], ["T0"])
            B.cp("vector", ksTn[:, :], T0[:, 0:RS], ["T0"], ["ksTn"])
            B.cp("vector", kwTn[:, :], T0[:, 128:128 + RS], ["T0"], ["kwTn"])
            for j in range(8):
                B.tr(T0[0:64, j * 128:j * 128 + RS], rqk_bf[0:RS, j * 64:(j + 1) * 64], idr, ["rqk_bf", "k_ident"], ["T0"])
            B.cp("vector", rqT[:, :, 0:RS], T0[0:64, 0:512].rearrange("p (h t) -> p h t", h=4)[:, :, 0:RS], ["T0"], ["rqT"])
            B.cp("vector", rkT[:, :, 0:RS], T0[0:64, 512:1024].rearrange("p (h t) -> p h t", h=4)[:, :, 0:RS], ["T0"], ["rkT"])
            B.tt("vector", qcr[:, :, 0:RS].rearrange("p h (b s) -> p h b s", s=4), rqT[:, :, 0:RS].rearrange("p h (b s) -> p h b s", s=4),
                 ct["cross4T"][:, :].rearrange("p (h s) -> p h s", h=4).unsqueeze(2).broadcast_to([64, 4, NB, 4]), ALU.mult,
                 ["rqT", "k_cross4T"], ["qcr"])
            B.cp("vector", vsrc[0:RS, 0:128], kvo[0:RS, 384:512], ["kvo"], ["xn"])
            B.cp("vector", vsrc[0:RS, 128:256], kvo[0:RS, 640:768], ["kvo"], ["xn"])
            B.cp("vector", vsrc[0:RS, 256:768], rv_bf[0:RS, :], ["rv_bf"], ["xn"])
            B.cp("vector", vsrc[0:RS, 768:1024], rqk_bf[0:RS, 256:512], ["rqk_bf"], ["xn"])
            gam = [1.0 - 2.0 ** (-5.0 - h) for h in range(4)]

            for b in range(NB):
                B.dma(vrow[0:4, :], vsrc[4 * b:4 * b + 4, :], ["xn"], ["vrow"], "vrow")
                B.dma(pt_i[:, :], ptab[0:1, b * NPG:(b + 1) * NPG].partition_broadcast(128), [], ["pt_i"], "pt_i")
                B.cp("vector", pt_f[:, :], pt_i[:, :], ["pt_i"], ["pt_f"])
                B.ts("vector", pt_f[:, :], pt_f[:, :], 128.0, ct["iota"][:, 0:1], ALU.mult, ALU.add, ["pt_f", "k_iota"], ["pt_f"])
                B.cp("vector", idx[:, :], pt_f[:, :], ["pt_f"], ["idx"])
                B.cp("vector", pt_f[:, :], pt_i[:, :], ["pt_i", "idx"], ["pt_f"])
                B.ts("vector", pt_f[:, :], pt_f[:, :], 64.0, ct["iota"][:, 0:1], ALU.mult, ALU.add, ["pt_f", "k_iota"], ["pt_f"])
                B.cp("vector", idx64[:, :], pt_f[:, :], ["pt_f"], ["idx64"])
                B.dma(wstK, st_winK[b], [], ["wstK"], "wstK")
                B.cp("vector", kwTs[:, 0:512], wstK, ["wstK"], ["kwTs"])
                B.cp("vector", kwTs[:, 512:516], kwTn[:, 4 * b:4 * b + 4], ["kwTn"], ["kwTs"])
                B.dma(wstV[:, :].rearrange("p (t c) -> p t c", t=4), st_winV[b].rearrange("(t p) c -> p t c", p=128), [], ["qn"], "qn")
                B.cp("vector", vws[:, 0:4, :, 0:64], wstV[:, :].rearrange("p (t g d) -> p t g d", t=4, g=2), ["qn"], ["vws"])
                B.cp("vector", vws[0:4, 4, :, 0:64], vrow[0:4, 128:256].rearrange("p (g d) -> p g d", g=2), ["vrow"], ["vws"])
                B.cp("vector", ksT[:, PAST:PAST + 4], ksTn[:, 4 * b:4 * b + 4], ["ksTn"], ["ksT"])
                B.cp("vector", vs[0:4, NPG, :, 0:64], vrow[0:4, 0:128].rearrange("p (g d) -> p g d", g=2), ["vrow"], ["vs"])
                for pg in range(NPG):
                    sl = pg % 3
                    B.P.add("gpsimd", (lambda o_, i_: (lambda e: e.indirect_dma_start(out=o_, out_offset=None, in_=c_pg,
                            in_offset=bass.IndirectOffsetOnAxis(ap=i_, axis=0))))(pgb[sl], idx[:, pg:pg + 1]),
                            ["idx"], ["pgb%d" % sl] + (["stage0"] if (b == 0 and pg < 2) else []), dma="pgb%d" % sl)
                    lc = 16 + (pg % GP) * 128
                    B.cp("vector", cmpT[:, :, lc:lc + 128], pgb[sl][:, 0:256].rearrange("p (k r) -> p k r", k=2), ["pgb%d" % sl], ["cmpT"])
                    B.cp("scalar", ksT[:, pg * 128:(pg + 1) * 128], pgb[sl][:, 256:384], ["pgb%d" % sl], ["ksT"])
                    B.cp("gpsimd", vs[:, pg, :, 0:64], pgb[sl][:, 384:512].rearrange("p (g d) -> p g d", g=2), ["pgb%d" % sl], ["vs"])
                    if (pg + 1) % GP == 0:
                        Gi = pg // GP
                        compress(8 * GP, 8 * GP * Gi - 1, Gi == 0)
                        B.cp("vector", cmpT[:, :, 0:16], cmpT[:, :, 128 * GP:128 * GP + 16], ["cmpT"], ["cmpT"])
                for g in range(2):
                    qrhs = qTs[:, b, 4 * g:4 * g + 4, :].rearrange("p r s -> p (r s)")
                    accC = Ab[0][0:4, 0:260].rearrange("p (r w) -> p r w", r=4)
                    accU = Ab[1][0:4, :].rearrange("p (r w) -> p r w", r=4)
                    tiles = []
                    for ctile in range((NCS + 127) // 128):
                        M = min(128, NCS - 128 * ctile)
                        tiles.append(dict(M=M, kT=kcT[:, 128 * ctile:128 * ctile + M], kres="kcT", masks=[],
                                          v=vc[0:M, ctile, g, :], vres="vc", u=ct["ov"][0:M, ctile * 128:(ctile + 1) * 128]))
                    run(branch(4, qrhs, "qTs", tiles, accC, "A0", 65, tagU=(accU, "A1")))
                    finish_branch(4, accC, "A0", ocmp, "ocmp", 0)

                    def bonus_s():
                        B.ts1("vector", imp[0:4, 0:1], imp[0:4, 0:1], 1e4, ALU.add, ["imp"], ["imp"])
                        B.ts1("vector", imp[0:4, NS - 1:NS], imp[0:4, NS - 1:NS], 1e4, ALU.add, ["imp"], ["imp"])
                    select_blocks(4, accU, "A1", 0, bonus_s, 15)
                    tiles = []
                    for wt in range(4):
                        masks = [(ident[:, :], ct["swinfirst4"][:, :], ["k_ident", "k_swinfirst4"])] if wt == 0 else []
                        tiles.append(dict(M=128, kT=kwTs[:, wt * 128:(wt + 1) * 128], kres="kwTs", masks=masks,
                                          v=vws[:, wt, g, :], vres="vws"))
                    tiles.append(dict(M=4, kT=kwTs[:, 512:516], kres="kwTs",
                                      masks=[(ident[0:4, 0:4], ct["scausal4"][0:4, :], ["k_ident", "k_scausal4"])],
                                      v=vws[0:4, 4, g, :], vres="vws"))
                    accW = Ab[2][0:4, 0:260].rearrange("p (r w) -> p r w", r=4)
                    run(branch(4, qrhs, "qTs", tiles, accW, "A2", 65))
                    finish_branch(4, accW, "A2", owin, "owin", 8)
                    tiles = []
                    for kt in range(NPG):
                        tiles.append(dict(M=128, kT=ksT[:, kt * 128:(kt + 1) * 128], kres="ksT", masks=[emask(kt, 4)],
                                          v=vs[:, kt, g, :], vres="vs"))
                    tiles.append(dict(M=4, kT=ksT[:, PAST:PAST + 4], kres="ksT",
                                      masks=[(ident[0:4, 0:4], ct["scausal4"][0:4, :], ["k_ident", "k_scausal4"])],
                                      v=vs[0:4, NPG, g, :], vres="vs"))
                    accS = Ab[0][0:4, 0:260].rearrange("p (r w) -> p r w", r=4)
                    run(branch(4, qrhs, "qTs", tiles, accS, "A0", 65))
                    finish_branch(4, accS, "A0", osel, "osel", 4)
                    for bi, (o_, ores) in enumerate(((ocmp, "ocmp"), (osel, "osel"), (owin, "owin"))):
                        B.cp("vector", ob16[0:4, bi, :], o_[0:4, :, :].rearrange("p r d -> p (r d)"), [ores], ["ob16_%d" % bi])
                        B.dma(oall[4 * b:4 * b + 4, bi, g, :], ob16[0:4, bi, :], ["ob16_%d" % bi], ["oall"], "ob16_%d" % bi)
                B.dma(Sfb[:, :, :].rearrange("p h e -> p (h e)"), st_ret[:, b * 512:(b + 1) * 512], [], ["Sf"], "Sf")
                B.cp("vector", Sbb[:, :, :], Sfb[:, :, :], ["Sf"], ["Sbf"])
                for h in range(4):
                    B.mm(Mb[0][0:4, h * 4:h * 4 + 4], rkT[:, h, 4 * b:4 * b + 4], rqT[:, h, 4 * b:4 * b + 4], True, True, ["rkT", "rqT"], ["M0"])
                B.tt("vector", sTm4[0:4, :], Mb[0][0:4, 0:16], ct["dec4T"][0:4, :], ALU.mult, ["M0", "k_dec4T"], ["sTm4"])
                for h in range(4):
                    B.mm(Mb[1][0:4, h * 128:(h + 1) * 128], sTm4[0:4, h * 4:h * 4 + 4], vrow[0:4, 256 + h * 128:256 + (h + 1) * 128],
                         True, False, ["sTm4", "vrow"], ["M1"])
                    B.mm(Mb[1][0:4, h * 128:(h + 1) * 128], qcr[:, h, 4 * b:4 * b + 4], Sbb[:, h, :], False, True, ["qcr", "Sbf"], ["M1"])
                B.tt("vector", kwk4[0:4, :].rearrange("p (h d) -> p h d", h=4), vrow[0:4, 768:1024].rearrange("p (h d) -> p h d", h=4),
                     ct["w4tab"][0:4, :].unsqueeze(2).broadcast_to([4, 4, 64]), ALU.mult, ["vrow", "k_w4tab"], ["kwk4"])
                for h in range(4):
                    B.mm(Mb[0][0:64, h * 128:(h + 1) * 128], kwk4[0:4, h * 64:(h + 1) * 64], vrow[0:4, 256 + h * 128:256 + (h + 1) * 128],
                         True, True, ["kwk4", "vrow"], ["M0"])
                for h in range(4):
                    B.stt("vector", Sfb[:, h, :], Sfb[:, h, :], float(gam[h] ** 4), Mb[0][0:64, h * 128:(h + 1) * 128],
                          ALU.mult, ALU.add, ["Sf", "M0"], ["Sf"])
                B.dma(o_ret_s[:, b * 512:(b + 1) * 512], Sfb[:, :, :].rearrange("p h e -> p (h e)"), ["Sf"], [], "Sf")
                B.cp("vector", oret4[0:4, :], Mb[1][0:4, :], ["M1"], ["oret4"])
                B.dma(oretS[4 * b:4 * b + 4, :], oret4[0:4, :], ["oret4"], ["oret"], "oret4")
            B.act(junk[0:RS, 0:512], oretS[0:RS, :], AF.Square, ["oret"], ["junk"])
            B.red(st1[0:RS, 2:6], junk[0:RS, 0:512].rearrange("p (h e) -> p h e", h=4), ["junk"], ["st1"])
            B.rsqrt(st1[0:RS, 2:6], st1[0:RS, 2:6], 1.0 / 128, ["st1"], ["st1"])
            B.tt("vector", oret[0:RS, :].rearrange("p (h e) -> p h e", h=4), oretS[0:RS, :].rearrange("p (h e) -> p h e", h=4),
                 st1[0:RS, 2:6].unsqueeze(2).broadcast_to([RS, 4, 128]), ALU.mult, ["oret", "st1"], ["oret"])
            B.tt("vector", mixed[0:RS, 512:1024], oret[0:RS, :], rg_f[0:RS, :], ALU.mult, ["oret", "rg_f"], ["mixed"])
            for g in range(2):
                for bi, (o_, ores) in enumerate(((ocmp, "ocmp"), (osel, "osel"), (owin, "owin"))):
                    B.cp("vector", o_[0:RS, :, :], oall[0:RS, bi, g, :].rearrange("p (r d) -> p r d", r=4), ["oall"], [ores])
                combine(RS, g, gates_l[0][0:RS, :], "gates0")
            out_proj(RS, xs_[xslot], xr, y_s)
        print("sbuf bytes remaining pass1", nc.sbuf_bytes_remaining)
        print("pass1 ops", P.emit(nc, "a"))

    P2 = Prog()
    with ExitStack() as es:
      if not os.environ.get('K_NOP2'):
          B = Bld(nc, P2, es)
          sb, ps = B.sb, B.ps
          G = 256
          Ub = [ps("U0", [128, 512], F32), ps("U1", [128, 512], F32)]
          Vb = [ps("V0", [128, 512], F32), ps("V1", [128, 512], F32)]
          T0 = ps("T0b", [128, 1024], BF16)
          Yb = [ps("Y0", [128, 512], F32), ps("Y1", [128, 512], F32)]
          ident = sb("ident2", [128, 128], BF16)
          B.dma(ident[:], cd["ident"], [], ["ident2"], "ident2")
          epsc = sb("epsc2", [128, 1], F32)
          B.memset("vector", epsc[:], EPS, ["epsc2"])
          B.eps_ap = epsc
          wfi = sb("wfi", [128, 8, 2 * D_FF], BF16)
          wfo = sb("wfo", [128, NFC, 1024], BF16)
          nmf = sb("nmf", [128, 8], F32)
          wcv = sb("wcv", [128, NFC, 4], F32)
          stage = [sb("stg0", [128, D_FF], F32), sb("stg1", [128, D_FF], F32)]
          B.dma(nmf[:], nm_ffn, [], ["nmf"], "nmf")
          B.dma(wcv[:, :, :].rearrange("p c j -> p (c j)"), wconv, [], ["wcv"], "wcv")
          si = 0
          for k in range(8):
              for hf in range(2):
                  s_ = si % 2
                  B.dma(stage[s_][:, :], w_fi[k * 128:(k + 1) * 128, hf * D_FF:(hf + 1) * D_FF], [], ["stg%d" % s_], "stg%d" % s_)
                  B.ts1("vector" if si % 2 == 0 else "gpsimd", wfi[:, k, hf * D_FF:(hf + 1) * D_FF], stage[s_][:, :], nmf[:, k:k + 1], ALU.mult,
                        ["stg%d" % s_, "nmf"], ["wfi"])
                  si += 1
          for fc in range(0, NFC, 2):
              s_ = si % 2
              B.dma(stage[s_][:, 0:2048].rearrange("p (c n) -> p c n", c=2),
                    w_fo[fc * 128:(fc + 2) * 128, :].rearrange("(c p) n -> p c n", p=128), [], ["stg%d" % s_], "stg%d" % s_)
              B.cp("vector" if si % 2 == 0 else "gpsimd", wfo[:, fc:fc + 2, :], stage[s_][:, 0:2048].rearrange("p (c n) -> p c n", c=2),
                   ["stg%d" % s_], ["wfo"])
              si += 1
          hs = [sb("hs%d" % i, [128, 1024], F32) for i in range(2)]
          hn = sb("hn", [128, 1024], BF16)
          hnT = sb("hnT", [128, 8, G], BF16)
          junk = sb("junk2", [128, 1024], F32)
          st1 = sb("st2", [128, 4], F32)
          ubufs = [sb("ubuf0", [128, G + 2], F32), sb("ubuf1", [128, G + 2], F32)]
          carry = sb("carry", [128, NFC, 2], F32)
          ucs = [sb("uc0", [128, G], F32), sb("uc1", [128, G], F32)]
          ues = [sb("ue0", [128, G], F32), sb("ue1", [128, G], F32)]
          actb = sb("actb", [128, NFC, G], BF16)
          y_sb = sb("y_sb", [128, 1024], F32)
          B.memset("vector", carry[:], 0.0, ["carry"])

          def ffn_group(tiles_rows, loads, stores, conv_w, carry_mode):
              ncol = sum(tiles_rows)
              col = 0
              cols = []
              for i, R in enumerate(tiles_rows):
                  hr = "hs%d" % i
                  B.dma(hs[i][0:R, :], loads[i], [], [hr], hr)
                  B.act(junk[0:R, :], hs[i][0:R, :], AF.Square, [hr], ["junk2", "st2"], accum=st1[0:R, 0:1])
                  B.rsqrt(st1[0:R, 1:2], st1[0:R, 0:1], 1.0 / 1024, ["st2"], ["st2"])
                  B.act(hn[0:R, :], hs[i][0:R, :], AF.Identity, [hr, "st2"], ["hn"], scale=st1[0:R, 1:2])
                  for k in range(8):
                      B.tr(T0[:, k * 128:k * 128 + R], hn[0:R, k * 128:(k + 1) * 128], ident[0:R, 0:R], ["hn", "ident2"], ["T0b"])
                  B.cp("vector", hnT[:, :, col:col + R], T0[:, :].rearrange("p (k t) -> p k t", k=8)[:, :, 0:R], ["T0b"], ["hnT"])
                  cols.append(col)
                  col += R
              for fc in range(NFC):
                  bk = fc % 2
                  for k in range(8):
                      B.mm(Ub[bk][:, 0:ncol], wfi[:, k, fc * 128:(fc + 1) * 128], hnT[:, k, 0:ncol], k == 0, k == 7, ["wfi", "hnT"], ["U%d" % bk])
                  for k in range(8):
                      B.mm(Vb[bk][:, 0:ncol], wfi[:, k, D_FF + fc * 128:D_FF + (fc + 1) * 128], hnT[:, k, 0:ncol], k == 0, k == 7,
                           ["wfi", "hnT"], ["V%d" % bk])
                  uc, ue, ubuf = ucs[bk], ues[bk], ubufs[bk]
                  ucr, uer, ubr = "uc%d" % bk, "ue%d" % bk, "ubuf%d" % bk
                  conv_w(fc, Ub[bk], "U%d" % bk, ncol, uc, ucr, ubuf, ubr)
                  B.act(ue[:, 0:ncol], uc[:, 0:ncol], AF.Silu, [ucr], [uer])
                  B.tt("vector", actb[:, fc, 0:ncol], ue[:, 0:ncol], Vb[bk][:, 0:ncol], ALU.mult, [uer, "V%d" % bk], ["actb%d" % fc])
              for i, R in enumerate(tiles_rows):
                  hr = "hs%d" % i
                  for nh in range(2):
                      for fc in range(NFC):
                          B.mm(Yb[nh][0:R, :], actb[:, fc, cols[i]:cols[i] + R], wfo[:, fc, nh * 512:(nh + 1) * 512], fc == 0, fc == NFC - 1,
                               ["actb%d" % fc, "wfo"], ["Y%d" % nh])
                      B.tt("vector", y_sb[0:R, nh * 512:(nh + 1) * 512], Yb[nh][0:R, :], hs[i][0:R, nh * 512:(nh + 1) * 512], ALU.add,
                           ["Y%d" % nh, hr], ["y_sb"])
                  B.dma(stores[i], y_sb[0:R, :], ["y_sb"], [], "y_out")

          def conv_seq(fc, Ux, ures, ncol, uc, ucr, ubuf, ubr):
              B.cp("scalar", ubuf[:, 0:2], carry[:, fc, :], ["carry"], [ubr])
              B.cp("scalar", ubuf[:, 2:2 + ncol], Ux[:, 0:ncol], [ures], [ubr])
              B.ts("vector", uc[:, 0:ncol], ubuf[:, 2:2 + ncol], wcv[:, fc, 2:3], wcv[:, fc, 3:4], ALU.mult, ALU.add, [ubr, "wcv"], [ucr])
              B.stt("vector", uc[:, 0:ncol], ubuf[:, 1:1 + ncol], wcv[:, fc, 1:2], uc[:, 0:ncol], ALU.mult, ALU.add, [ubr, "wcv", ucr], [ucr])
              B.stt("vector", uc[:, 0:ncol], ubuf[:, 0:ncol], wcv[:, fc, 0:1], uc[:, 0:ncol], ALU.mult, ALU.add, [ubr, "wcv", ucr], [ucr])
              B.cp("scalar", carry[:, fc, :], ubuf[:, ncol:ncol + 2], [ubr], ["carry"])

          if NT > 0 and DBG >= 6:
              tpg = G // 128
              for gi in range(NT // tpg):
                  rows = [128] * tpg
                  loads = [y_p[(gi * tpg + i) * 128:(gi * tpg + i + 1) * 128, :] for i in range(tpg)]
                  ffn_group(rows, loads, loads, conv_seq, "seq")
              B.dma(o_conv_p, carry[:, :, :].rearrange("p c j -> p (c j)"), ["carry"], [], "o_conv_p")

          if NB > 0 and DBG >= 7:
              stc = sb("stc", [128, NFC, NB, 2], F32)
              convS = sb("convS", [128, NFC, NB, 2], F32)
              B.dma(stc[:, :, :, :].rearrange("p c b j -> p (c b j)"), st_conv, [], ["stc"], "stc")

              def conv_smp(fc, Ux, ures, ncol, uc, ucr, ubuf, ubr):
                  ub = ubuf[:, 0:NB * 6].rearrange("p (b j) -> p b j", j=6)
                  B.cp("scalar", ub[:, :, 0:2], stc[:, fc, :, :], ["stc"], [ubr])
                  B.cp("scalar", ub[:, :, 2:6], Ux[:, 0:RS].rearrange("p (b s) -> p b s", s=4), [ures], [ubr])
                  uc3 = uc[:, 0:RS].rearrange("p (b s) -> p b s", s=4)
                  B.ts("vector", uc3, ub[:, :, 2:6], wcv[:, fc, 2:3], wcv[:, fc, 3:4], ALU.mult, ALU.add, [ubr, "wcv"], [ucr])
                  B.stt("vector", uc3, ub[:, :, 1:5], wcv[:, fc, 1:2], uc3, ALU.mult, ALU.add, [ubr, "wcv", ucr], [ucr])
                  B.stt("vector", uc3, ub[:, :, 0:4], wcv[:, fc, 0:1], uc3, ALU.mult, ALU.add, [ubr, "wcv", ucr], [ucr])
                  B.cp("scalar", convS[:, fc, :, :], ub[:, :, 4:6], [ubr], ["convS"])
              ffn_group([RS], [y_s], [y_s], conv_smp, "smp")
              B.dma(o_conv_s, convS[:, :, :, :].rearrange("p c b j -> p (c b j)"), ["convS"], [], "o_conv_s")

          print("pass2 ops", P2.emit(nc, "b"))
    return nc, consts


_CACHE = {}


def _w2h(w2):
    o = np.zeros((128, 256), np.float32)
    for g in range(2):
        o[64 * g:64 * g + 64, 64 * g:64 * g + 64] = w2[0]
        o[64 * g:64 * g + 64, 128 + 64 * g:128 + 64 * g + 64] = w2[1]
    return o


def _run(inputs, S, NB, PAST, NPOOL, n_cores=8):
    key = (S, NB, PAST, NPOOL)
    if key not in _CACHE:
        _CACHE[key] = build(S, NB, PAST, NPOOL)
    nc, consts = _CACHE[key]
    f32 = np.float32
    NPG = PAST // 128
    RS = NB * 4
    A = lambda v: np.ascontiguousarray(np.asarray(v))
    x_prompt, x_sample = A(inputs["x_prompt"]), A(inputs["x_sample"])
    cache_cmp, cache_sel = A(inputs["cache_kv_cmp"])[0], A(inputs["cache_kv_sel"])[0]
    c_pg = np.empty((NPOOL, 128, 512), np.float32)
    c_pg[:, :, 0:256] = cache_cmp.reshape(NPOOL, 128, 2, 128).transpose(0, 3, 2, 1).reshape(NPOOL, 128, 256)
    c_pg[:, :, 256:384] = cache_sel[:, :, 0].reshape(NPOOL, 128, 128).transpose(0, 2, 1)
    c_pg[:, :, 384:512] = cache_sel[:, :, 1].reshape(NPOOL, 128, 128)
    c_pg = c_pg.reshape(NPOOL * 128, 512)
    st_win_all = A(inputs["state_kv_win"])[0]
    st_ret_all = A(inputs["state_ret"])[0]
    st_conv_all = A(inputs["state_conv"])[0]
    pt_all = A(inputs["page_table"]).astype(np.int32)
    shared = {("c_" + k): v for k, v in consts.items()}
    shared.update({
        "c_pg": c_pg,
        "nm_mix": A(inputs["norm_mix"])[0].reshape(8, 128).T.copy(),
        "w_in": A(inputs["w_in"])[0],
        "gq": np.tile(A(inputs["norm_q"])[0][None, :], (128, 1)).astype(f32),
        "gk": np.tile(A(inputs["norm_k"])[0].reshape(1, 192), (128, 1)).astype(f32),
        "w1h": np.tile(A(inputs["w_cmp1"])[0].reshape(2, 32, 64, 64).transpose(2, 0, 1, 3).reshape(64, 4096), (2, 1)).astype(f32),
        "w2h": _w2h(A(inputs["w_cmp2"])[0]),
        "peT": np.tile(A(inputs["pe_cmp"])[0].transpose(2, 0, 1).reshape(64, 64), (2, 1)).astype(f32),
        "w_out": A(inputs["w_out"])[0],
        "nm_ffn": A(inputs["norm_ffn"])[0].reshape(8, 128).T.copy(),
        "w_fi": A(inputs["w_ffn_in"])[0],
        "wconv": np.concatenate([A(inputs["w_conv"])[0], A(inputs["b_conv"])], axis=0).reshape(4, NFC, 128).transpose(2, 1, 0).reshape(128, NFC * 4).copy(),
        "w_fo": A(inputs["w_ffn_out"])[0],
    })
    in_maps = []
    for c in range(n_cores):
        bs = slice(c * NB, (c + 1) * NB)
        sw = st_win_all[bs]
        m = dict(shared)
        m["x_p"] = x_prompt[c]
        m["x_s"] = x_sample[bs].reshape(RS, 1024)
        m["st_winK"] = np.ascontiguousarray(sw[:, :, 0].reshape(NB, 512, 128).transpose(0, 2, 1))
        m["st_winV"] = np.ascontiguousarray(sw[:, :, 1].reshape(NB, 512, 128))
        m["st_win"] = np.ascontiguousarray(sw.reshape(NB, 512, 256))
        m["st_ret"] = np.ascontiguousarray(st_ret_all[bs].transpose(2, 0, 1, 3)).reshape(64, NB * 512)
        m["st_conv"] = np.ascontiguousarray(st_conv_all[bs].reshape(NB, 2, NFC, 128).transpose(3, 2, 0, 1)).reshape(128, NFC * NB * 2)
        m["ptab"] = np.ascontiguousarray(pt_all[bs].reshape(1, NB * NPG))
        in_maps.append(m)
    res = run_bass_kernel_spmd(nc, in_maps, core_ids=list(range(n_cores)))
    R = res.results
    cat = lambda k: np.stack([r[k] for r in R], 0)
    nB = n_cores
    DB = n_cores * NB
    y_prompt = cat("y_p")
    y_sample = cat("y_s").reshape(DB, 4, 1024)
    kvc_p = cat("o_kvc_p").reshape(1, nB, S, 2, 2, 64)
    kvs_p = cat("o_kvs_p").reshape(1, nB, S, 2, 2, 64)
    kvw_p = cat("o_kvw_p").reshape(1, nB, 512, 2, 2, 64)
    kvc_s = cat("o_kvc_s").reshape(1, DB, 4, 2, 2, 64)
    kvs_s = cat("o_kvs_s").reshape(1, DB, 4, 2, 2, 64)
    kvw_s = cat("o_kvw_s").reshape(1, DB, 512, 2, 2, 64)
    ret_p = cat("o_ret_p").reshape(nB, 64, 4, 128).transpose(0, 2, 1, 3).reshape(1, nB, 4, 64, 128)
    ret_s = cat("o_ret_s").reshape(nB, 64, NB, 4, 128).transpose(0, 2, 3, 1, 4).reshape(1, DB, 4, 64, 128)
    conv_p = cat("o_conv_p").reshape(nB, 128, NFC, 2).transpose(0, 3, 2, 1).reshape(1, nB, 2, D_FF)
    conv_s = cat("o_conv_s").reshape(nB, 128, NFC, NB, 2).transpose(0, 3, 4, 2, 1).reshape(1, DB, 2, D_FF)
    outs = (y_prompt, y_sample, kvc_p, kvc_s, kvs_p, kvs_s, kvw_p, kvw_s, ret_p, ret_s, conv_p, conv_s)
    return tuple(np.ascontiguousarray(o, dtype=np.float32) for o in outs)


def kernel(**inputs):
    S = int(np.asarray(inputs["x_prompt"]).shape[1])
    DB = int(np.asarray(inputs["x_sample"]).shape[0])
    NPG = int(np.asarray(inputs["page_table"]).shape[1])
    NPOOL = int(np.asarray(inputs["cache_kv_cmp"]).shape[1])
    return _run(inputs, S, DB // 8, NPG * 128, NPOOL)
```
